# Optimizing a Trainium2 kernel written in Bass

```python
import math
import jax, jax.numpy as jnp
from jax import lax
import numpy as np

D_MODEL = 1024
BATCH = 4
SEQ = 8192
DEPTH = 2
DEC_BATCH = 2
DEC_SEQ = 16384
PAST_LEN = 128

HEAD_DIM = 64
MIX_W = D_MODEL
A_HEADS = 4
A_W = A_HEADS * HEAD_DIM
LORA = 64
B_HEADS = 4
B_W = B_HEADS * HEAD_DIM
DILATED_PAIRS = ((128, 1), (512, 4), (2048, 16))
C_HEADS = 8
C_KV_HEADS = 2
C_QW = C_HEADS * HEAD_DIM
C_KVW = C_KV_HEADS * HEAD_DIM
C_RADIUS = 128
ROPE_THETA = 10000.0
RMS_EPS = 1e-5
HEADNORM_EPS = 64e-5
NEG_INF = -1e30

SHIFT_SPLITS = (A_W, A_W, A_W, LORA, LORA, LORA, LORA)
TSHIFT_W = sum(SHIFT_SPLITS)
REST_SPLITS = (A_W, B_W, B_W, B_W, B_W, C_QW, C_KVW, C_KVW, C_QW)
IN_W = TSHIFT_W + sum(REST_SPLITS)

kernel_name = "hybrid_bidir_rwkv7_dilated_swa_encoder"


def _split(t, sizes):
    idx = [int(i) for i in np.cumsum(sizes)[:-1]]
    return jnp.split(t, idx, axis=-1)


def rms_norm(x, g):
    xf = x.astype(jnp.float32)
    y = xf * lax.rsqrt(jnp.mean(xf * xf, axis=-1, keepdims=True) + RMS_EPS)
    return (y * g.astype(jnp.float32)).astype(x.dtype)


def rope(t):
    T, hd = t.shape[1], t.shape[-1]
    inv = ROPE_THETA ** (-jnp.arange(0, hd, 2, dtype=jnp.float32) / hd)
    ang = jnp.arange(T, dtype=jnp.float32)[:, None] * inv[None, :]
    cos = jnp.cos(ang)[None, :, None, :]
    sin = jnp.sin(ang)[None, :, None, :]
    tf = t.astype(jnp.float32)
    t1, t2 = tf[..., : hd // 2], tf[..., hd // 2:]
    return jnp.concatenate([t1 * cos - t2 * sin, t2 * cos + t1 * sin], axis=-1).astype(t.dtype)


def centred_token_shift(f, mu):
    zero = jnp.zeros_like(f[:, :1])
    prev = jnp.concatenate([zero, f[:, :-1]], axis=1)
    nxt = jnp.concatenate([f[:, 1:], zero], axis=1)
    return f + mu * (0.5 * (prev + nxt) - f)


def banded_attention(q, k, v, radius, sink=None):
    b, L, hq, hd = q.shape
    hkv = k.shape[2]
    grp = hq // hkv
    blk = radius
    nb = -(-L // blk)
    lp = nb * blk
    q = jnp.pad(q, ((0, 0), (0, lp - L), (0, 0), (0, 0)))
    kv_pad = ((0, 0), (blk, lp - L + blk), (0, 0), (0, 0))
    k = jnp.pad(k, kv_pad).reshape(b, nb + 2, blk, hkv, hd)
    v = jnp.pad(v, kv_pad).reshape(b, nb + 2, blk, hkv, hd)
    kb = jnp.concatenate([k[:, :-2], k[:, 1:-1], k[:, 2:]], axis=2)
    vb = jnp.concatenate([v[:, :-2], v[:, 1:-1], v[:, 2:]], axis=2)
    qb = q.reshape(b, nb, blk, hkv, grp, hd)
    s = jnp.einsum('bnqhgd,bnkhd->bnhgqk', qb, kb,
                   preferred_element_type=jnp.float32) * (hd ** -0.5)
    qpos = jnp.arange(lp).reshape(nb, blk)
    kpos = (jnp.arange(nb)[:, None] - 1) * blk + jnp.arange(3 * blk)[None, :]
    kp = kpos[:, None, :]
    mask = (jnp.abs(kp - qpos[:, :, None]) <= radius) & (kp >= 0) & (kp < L)
    s = jnp.where(mask[None, :, None, None], s, NEG_INF)
    m = jnp.max(s, axis=-1, keepdims=True)
    if sink is not None:
        sk = sink.astype(jnp.float32).reshape(hkv, grp)[None, None, :, :, None, None]
        m = jnp.maximum(m, sk)
    p = jnp.exp(s - m)
    denom = jnp.sum(p, axis=-1, keepdims=True)
    if sink is not None:
        denom = denom + jnp.exp(sk - m)
    o = jnp.einsum('bnhgqk,bnkhd->bnhgqd', p, vb.astype(jnp.float32)) / denom
    o = o.transpose(0, 1, 4, 2, 3, 5).reshape(b, lp, hq, hd)[:, :L]
    lse = (m + jnp.log(denom))[..., 0].transpose(0, 1, 4, 2, 3).reshape(b, lp, hq)[:, :L]
    return o, lse


def dilated_mixture(q, k, v):
    b, T, h, hd = q.shape
    outs, lses = [], []
    for window, dil in DILATED_PAIRS:
        radius = window // (2 * dil)
        L = T // dil

        def to_strided(t):
            return t.reshape(b, L, dil, h, hd).transpose(0, 2, 1, 3, 4).reshape(b * dil, L, h, hd)

        o, lse = banded_attention(to_strided(q), to_strided(k), to_strided(v), radius)
        outs.append(o.reshape(b, dil, L, h, hd).transpose(0, 2, 1, 3, 4).reshape(b, T, h, hd))
        lses.append(lse.reshape(b, dil, L, h).transpose(0, 2, 1, 3).reshape(b, T, h))
    wts = jax.nn.softmax(jnp.stack(lses, axis=0), axis=0)
    return jnp.sum(wts[..., None] * jnp.stack(outs, axis=0), axis=0)


def wkv7_scan(r, w, k, v, kk, a, reverse):
    b, T, h, n = r.shape

    def step(S, inp):
        r_t, w_t, k_t, v_t, kk_t, a_t = inp
        sa = jnp.einsum('bhij,bhj->bhi', S, -kk_t)
        S = (S * w_t[:, :, None, :] + sa[..., :, None] * (kk_t * a_t)[..., None, :]
             + v_t[..., :, None] * k_t[..., None, :])
        return S, jnp.einsum('bhij,bhj->bhi', S, r_t)

    xs = tuple(jnp.moveaxis(z, 1, 0) for z in (r, w, k, v, kk, a))
    S0 = jnp.zeros((b, h, n, n), jnp.float32)
    _, y = lax.scan(step, S0, xs, reverse=reverse)
    return jnp.moveaxis(y, 0, 1)


def rwkv7_branch(r, k, v, wd, ad, w0, w2, a0, a2, k_k, k_a, r_k, ln_w, ln_b):
    f32 = jnp.float32
    b, t, _ = r.shape
    r, k, v = r.astype(f32), k.astype(f32), v.astype(f32)
    w = -jax.nn.softplus(-(w0 + jnp.einsum('btel,elc->btec', jnp.tanh(wd.astype(f32)), w2.astype(f32)))) - 0.5
    decay = jnp.exp(-jnp.exp(w))
    a = jax.nn.sigmoid(a0 + jnp.einsum('btel,elc->btec', ad.astype(f32), a2.astype(f32)))

    def heads(z):
        return z.reshape(b, t, A_HEADS, HEAD_DIM)

    kk = heads(k * k_k)
    kk = kk / jnp.maximum(jnp.sqrt(jnp.sum(kk * kk, axis=-1, keepdims=True)), 1e-12)
    k_eff = k[:, :, None, :] * (1.0 + (a - 1.0) * k_a)
    rh, vh = heads(r), heads(v)
    y_f = wkv7_scan(rh, heads(decay[:, :, 0]), heads(k_eff[:, :, 0]), vh, kk, heads(a[:, :, 0]), reverse=False)
    y_b = wkv7_scan(rh, heads(decay[:, :, 1]), heads(k_eff[:, :, 1]), vh, kk, heads(a[:, :, 1]), reverse=True)
    y = y_f + y_b
    mu = jnp.mean(y, axis=-1, keepdims=True)
    var = jnp.mean(jnp.square(y - mu), axis=-1, keepdims=True)
    yn = (y - mu) * lax.rsqrt(var + HEADNORM_EPS)
    yn = yn * ln_w.reshape(A_HEADS, HEAD_DIM) + ln_b.reshape(A_HEADS, HEAD_DIM)
    bonus = jnp.sum(rh * heads(k_eff[:, :, 0] + k_eff[:, :, 1]) * r_k, axis=-1, keepdims=True) * vh
    return (yn + bonus).reshape(b, t, A_W)


def hybrid_layer(x, norm_g, w_in, tshift_mu, rwkv_w0, rwkv_w2, rwkv_a0, rwkv_a2, rwkv_k_k, rwkv_k_a,
                 rwkv_r_k, ln_x_w, ln_x_b, attn_sink, w_out):
    b, t, _ = x.shape
    h = rms_norm(x, norm_g)
    proj = jnp.einsum('btd,de->bte', h, w_in)
    shifted = centred_token_shift(proj[..., :TSHIFT_W], tshift_mu)
    a_r, a_k, a_v, wd_f, wd_b, ad_f, ad_b = _split(shifted, SHIFT_SPLITS)
    a_g, b_q, b_k, b_v, b_g, c_q, c_k, c_v, c_g = _split(proj[..., TSHIFT_W:], REST_SPLITS)

    y_a = rwkv7_branch(a_r, a_k, a_v, jnp.stack([wd_f, wd_b], axis=2), jnp.stack([ad_f, ad_b], axis=2),
                       rwkv_w0, rwkv_w2, rwkv_a0, rwkv_a2, rwkv_k_k, rwkv_k_a, rwkv_r_k, ln_x_w, ln_x_b)
    bh = lambda z: z.reshape(b, t, B_HEADS, HEAD_DIM)
    y_b = dilated_mixture(rope(bh(b_q)), rope(bh(b_k)), bh(b_v)).reshape(b, t, B_W)
    y_c, _ = banded_attention(rope(c_q.reshape(b, t, C_HEADS, HEAD_DIM)),
                              rope(c_k.reshape(b, t, C_KV_HEADS, HEAD_DIM)),
                              c_v.reshape(b, t, C_KV_HEADS, HEAD_DIM), C_RADIUS, sink=attn_sink)
    y_c = y_c.reshape(b, t, C_QW)

    mix = jnp.concatenate([y_a.astype(x.dtype) * jax.nn.silu(a_g),
                           y_b.astype(x.dtype) * jax.nn.silu(b_g),
                           y_c.astype(x.dtype) * jax.nn.silu(c_g)], axis=-1)
    return x + jnp.einsum('btc,cd->btd', mix, w_out)


def trunk(x, norm_g, w_in, tshift_mu, rwkv_w0, rwkv_w2, rwkv_a0, rwkv_a2, rwkv_k_k, rwkv_k_a,
          rwkv_r_k, ln_x_w, ln_x_b, attn_sink, w_out, final_g):
    for l in range(DEPTH):
        x = hybrid_layer(x, norm_g[l], w_in[l], tshift_mu[l], rwkv_w0[l], rwkv_w2[l], rwkv_a0[l], rwkv_a2[l],
                         rwkv_k_k[l], rwkv_k_a[l], rwkv_r_k[l], ln_x_w[l], ln_x_b[l], attn_sink[l], w_out[l])
    return rms_norm(x, final_g)


def setup_inputs(seed: int = 0) -> dict:
    key = jax.random.key(seed)
    ks = jax.random.split(key, 20)
    nrm = jax.random.normal
    f32 = jnp.float32
    return {
        "x_prompt": nrm(ks[0], (BATCH, SEQ, D_MODEL), f32),
        "x_sample": nrm(ks[1], (DEC_BATCH, DEC_SEQ, D_MODEL), f32),
        "norm_g": 1.0 + 0.1 * nrm(ks[2], (DEPTH, D_MODEL), f32),
        "w_in": nrm(ks[3], (DEPTH, D_MODEL, IN_W), f32) * D_MODEL ** -0.5,
        "tshift_mu": jax.random.uniform(ks[4], (DEPTH, TSHIFT_W), f32),
        "rwkv_w0": -2.0 + nrm(ks[5], (DEPTH, 2, A_W), f32),
        "rwkv_w2": 0.1 * nrm(ks[6], (DEPTH, 2, LORA, A_W), f32),
        "rwkv_a0": 0.5 * nrm(ks[7], (DEPTH, 2, A_W), f32),
        "rwkv_a2": 0.1 * nrm(ks[8], (DEPTH, 2, LORA, A_W), f32),
        "rwkv_k_k": 0.85 + 0.05 * nrm(ks[9], (DEPTH, A_W), f32),
        "rwkv_k_a": 1.0 + 0.05 * nrm(ks[10], (DEPTH, A_W), f32),
        "rwkv_r_k": 0.1 * nrm(ks[11], (DEPTH, A_HEADS, HEAD_DIM), f32),
        "ln_x_w": 1.0 + 0.1 * nrm(ks[12], (DEPTH, A_W), f32),
        "ln_x_b": 0.01 * nrm(ks[13], (DEPTH, A_W), f32),
        "attn_sink": 0.5 * nrm(ks[14], (DEPTH, C_HEADS), f32),
        "w_out": nrm(ks[15], (DEPTH, MIX_W, D_MODEL), f32) * MIX_W ** -0.5,
        "final_g": 1.0 + 0.1 * nrm(ks[16], (D_MODEL,), f32),
    }


def reference(x_prompt, x_sample, norm_g, w_in, tshift_mu, rwkv_w0, rwkv_w2, rwkv_a0, rwkv_a2, rwkv_k_k,
              rwkv_k_a, rwkv_r_k, ln_x_w, ln_x_b, attn_sink, w_out, final_g):
    y_prompt = trunk(x_prompt, norm_g, w_in, tshift_mu, rwkv_w0, rwkv_w2, rwkv_a0, rwkv_a2, rwkv_k_k,
                     rwkv_k_a, rwkv_r_k, ln_x_w, ln_x_b, attn_sink, w_out, final_g)
    y_sample = trunk(x_sample, norm_g, w_in, tshift_mu, rwkv_w0, rwkv_w2, rwkv_a0, rwkv_a2, rwkv_k_k,
                     rwkv_k_a, rwkv_r_k, ln_x_w, ln_x_b, attn_sink, w_out, final_g)
    return (y_prompt, y_sample)
```

```python
import os
import numpy as np
from contextlib import ExitStack
import concourse.bass as bass
import concourse.mybir as mybir
from concourse.bass_utils import run_bass_kernel_spmd

F32 = mybir.dt.float32
BF16 = mybir.dt.bfloat16
AF = mybir.ActivationFunctionType
ALU = mybir.AluOpType

ENGS = ("pe", "act", "dve", "pool", "sp")
EPOCH = 12000
D = 1024
NCOL = 34 * 128 + 384
C0 = float(np.exp(-0.5))
PAD = 1024
NLAYER = 2
CW = 2176


class Buf:
    __slots__ = ("name", "w", "r")

    def __init__(self, name=""):
        self.name = name
        self.w = None
        self.r = []


class DSem:
    def __init__(self, k):
        self.k = k
        self.sems = []
        self.finals = {}
        self.count = 0
        self.new_epoch()
        k.dsems.append(self)

    def new_epoch(self):
        if self.sems:
            self.finals[id(self.sems[-1])] = self.count
        self.sems.append(self.k.new_sem())
        self.count = 0

    def value_for(self, h):
        if self.sems[-1] is h:
            return self.count
        return self.finals[id(h)]


class Op:
    __slots__ = ("eng", "fn", "deps", "signal", "dma", "dsem", "dsem_h", "cnt", "ep")

    def __init__(self, eng, fn):
        self.eng = eng
        self.fn = fn
        self.deps = []
        self.signal = False
        self.dma = False


class T:
    def __init__(self, k, t, name):
        self.k = k
        self.t = t
        self.b = Buf(name)
        self._ds = None

    def __getitem__(self, key):
        return self.t[key]

    @property
    def ds(self):
        if self._ds is None:
            k = self.k
            if k.sb_mark is not None:
                self._ds = k.free_ds.pop() if k.free_ds else DSem(k)
                k.phase_ds.append(self._ds)
            else:
                self._ds = DSem(k)
        return self._ds


class K:
    def __init__(self, nc, es):
        self.nc = nc
        self.es = es
        self.ops = {e: [] for e in ENGS}
        self.nsem = 0
        self.dsems = []
        self.nt = 0
        self.sb_lo = 16512
        self.sb_hi = 229344
        self.sb_cur = self.sb_lo
        self.sb_mark = None
        self.free_ds = []
        self.phase_ds = []

    def new_sem(self):
        self.nsem += 1
        return self.es.enter_context(self.nc.semaphore(f"s{self.nsem}"))

    def sb(self, name, shape, dtype):
        nb = int(np.prod(shape[1:])) * (4 if dtype == F32 else 2)
        nb = (nb + 63) // 64 * 64
        off = self.sb_cur
        self.sb_cur += nb
        assert self.sb_cur <= self.sb_hi, f"SBUF overflow at {name}: {self.sb_cur}"
        self.nt += 1
        t = self.nc.alloc_sbuf_tensor_at(f"{name}_{self.nt}", list(shape), dtype, offset=off)
        return T(self, t, name)

    def phase_begin(self):
        if self.sb_mark is None:
            self.sb_mark = self.sb_cur
        self.sb_cur = self.sb_mark
        self.free_ds.extend(self.phase_ds)
        self.phase_ds = []

    def ps(self, name, shape, dtype=F32):
        t = self.es.enter_context(self.nc.psum_tensor(name, list(shape), dtype))
        return T(self, t, name)

    @staticmethod
    def _b(x):
        return x.b if isinstance(x, T) else x

    def _dep(self, op, d):
        if d.dma:
            op.deps.append(("dma", d.dsem_h, d.dsem.value_for(d.dsem_h)))
        else:
            op.deps.append(d)

    def _track(self, op, reads, writes):
        eng = op.eng
        for x in reads:
            b = self._b(x)
            if b.w is not None and not (b.w.eng == eng and eng == "pe" and not b.w.dma):
                self._dep(op, b.w)
        for x in writes:
            b = self._b(x)
            for r in b.r:
                if r.dma or r.eng != eng:
                    self._dep(op, r)
            if b.w is not None and (b.w.dma or b.w.eng != eng):
                self._dep(op, b.w)
        for x in reads:
            self._b(x).r.append(op)
        for x in writes:
            b = self._b(x)
            b.w = op
            b.r = []

    def op(self, eng, fn, reads=(), writes=()):
        o = Op(eng, fn)
        self._track(o, reads, writes)
        self.ops[eng].append(o)
        return o

    def dma(self, out, in_, reads=(), writes=(), sem=None, eng="sp", **kw):
        if sem is None:
            for x in list(writes) + list(reads):
                if isinstance(x, T):
                    sem = x
                    break
        ds = sem.ds
        if ds.count + 16 > EPOCH:
            ds.new_epoch()
        o = Op(eng, lambda e: e.dma_start(out=out, in_=in_, **kw))
        o.dma = True
        self._track(o, reads, writes)
        ds.count += 16
        o.dsem = ds
        o.dsem_h = ds.sems[-1]
        self.ops[eng].append(o)
        return o

    def barrier(self):
        lasts = {}
        for e in ENGS:
            for o in reversed(self.ops[e]):
                if not o.dma:
                    lasts[e] = o
                    break
        dd = []
        for ds in self.dsems:
            if ds.count:
                dd.append(("dma", ds.sems[-1], ds.count))
        for e in ENGS:
            o = Op(e, lambda eng: eng.nop())
            for e2, lo in lasts.items():
                if e2 != e:
                    o.deps.append(lo)
            o.deps.extend(dd)
            self.ops[e].append(o)

    def finalize(self, final_waits=()):
        for e in ENGS:
            for o in self.ops[e]:
                for d in o.deps:
                    if not isinstance(d, tuple):
                        d.signal = True
        for e in ENGS:
            cnt = 0
            sems = [self.new_sem()]
            for o in self.ops[e]:
                if o.dma:
                    continue
                if o.signal:
                    if cnt + 1 > EPOCH:
                        sems.append(self.new_sem())
                        cnt = 0
                    cnt += 1
                    o.cnt = cnt
                    o.ep = sems[-1]
        fw = []
        for t in final_waits:
            ds = t.ds
            for h in ds.sems:
                v = ds.value_for(h)
                if v:
                    fw.append((h, v))
        k = self
        with self.nc.Block() as block:
            def emit(e_name):
                def body(e):
                    seen = {}
                    for o in k.ops[e_name]:
                        need = {}
                        for d in o.deps:
                            if isinstance(d, tuple):
                                key, h, v = id(d[1]), d[1], d[2]
                            else:
                                key, h, v = id(d.ep), d.ep, d.cnt
                            if seen.get(key, 0) >= v:
                                continue
                            if key not in need or need[key][1] < v:
                                need[key] = (h, v)
                        for key, (h, v) in need.items():
                            e.wait_ge(h, v)
                            seen[key] = v
                        ins = o.fn(e)
                        if o.dma:
                            ins.then_inc(o.dsem_h, 16)
                        elif o.signal:
                            ins.then_inc(o.ep, 1)
                    if e_name == "sp":
                        for (h, v) in fw:
                            e.wait_ge(h, v)
                return body
            block.sync(emit("sp"))
            block.scalar(emit("act"))
            block.vector(emit("dve"))
            block.gpsimd(emit("pool"))
            block.tensor(emit("pe"))


def rev(ap):
    pat = [list(p) for p in ap.ap]
    off = ap.offset + (pat[-1][1] - 1) * pat[-1][0]
    pat[-1][0] = -pat[-1][0]
    return bass.AP(ap.tensor, off, pat)


def build_nc(TC, debug=False, stop=None):
    HALF = TC // 2
    NT = TC // 512
    WIN = min(2048, HALF)
    assert HALF % WIN == 0 and WIN % 2048 == 0 or WIN == HALF
    nc = bass.Bass("TRN2", target_bir_lowering=False)
    dk = "ExternalOutput" if debug else "Internal"

    def din(name, shape, dt=F32):
        return nc.dram_tensor(name, list(shape), dt, kind="ExternalInput").ap()

    x_in = din("x", [TC, D])
    link_in = din("link", [128, 1])
    cs_in = din("cs", [128, TC])
    sn_in = din("sn", [128, TC])
    win_in = din("win", [NLAYER, D, NCOL])
    wout_in = din("wout", [NLAYER, D, D])
    prm_in = din("prm", [128, 64])
    w2_in = din("w2s", [64, NLAYER, 2, 256])
    a2_in = din("a2s", [64, NLAYER, 2, 256])
    sink_in = din("sinkrow", [1, NLAYER * 2 * 512])
    fg_in = din("fg", [128, D])
    cst_in = din("cst", [128, CW])
    y_out = nc.dram_tensor("y", [TC, D], F32, kind="ExternalOutput").ap()

    proj = nc.dram_tensor("proj", [25, 128, TC], BF16, kind=dk).ap()
    vtm = nc.dram_tensor("vtm", [TC + 2 * PAD, 384], BF16, kind=dk).ap()
    ysd = nc.dram_tensor("ysd", [2, 256, TC], F32, kind=dk).ap()
    bsd = nc.dram_tensor("bsd", [2, 256, TC], F32, kind=dk).ap()
    att = nc.dram_tensor("att", [6, 128, TC], BF16, kind=dk).ap()
    x1d = nc.dram_tensor("x1d", [TC, D], F32, kind=dk).ap()

    with ExitStack() as es:
        k = K(nc, es)
        banks = [k.ps(f"bk{i}", [128, 512]) for i in range(8)]

        def act(out, in_, func, r, w, bias=None, scale=None, accum=None, eng="act"):
            kw = {}
            if bias is not None:
                kw["bias"] = bias
            if scale is not None:
                kw["scale"] = scale
            if accum is not None:
                kw["accum_out"] = accum
            return k.op("act", lambda e: e.activation(out=out, in_=in_, func=func, **kw), r, w)

        def tt(eng, out, in0, in1, op, r, w):
            return k.op(eng, lambda e: e.tensor_tensor(out=out, in0=in0, in1=in1, op=op), r, w)

        def ts(eng, out, in0, s1, s2, op0, op1, r, w):
            if op1 is None:
                return k.op(eng, lambda e: e.tensor_scalar(out=out, in0=in0, scalar1=s1, scalar2=None, op0=op0), r, w)
            return k.op(eng, lambda e: e.tensor_scalar(out=out, in0=in0, scalar1=s1, scalar2=s2, op0=op0, op1=op1), r, w)

        def stt(out, in0, scalar, in1, op0, op1, r, w):
            return k.op("dve", lambda e: e.scalar_tensor_tensor(out=out, in0=in0, scalar=scalar, in1=in1, op0=op0, op1=op1), r, w)

        def cp(eng, out, in_, r, w):
            if eng == "act":
                return k.op("act", lambda e: e.activation(out=out, in_=in_, func=AF.Copy), r, w)
            return k.op(eng, lambda e: e.tensor_copy(out=out, in_=in_), r, w)

        def mm(out, lhsT, rhs, r, w, start=True, stop=True):
            return k.op("pe", lambda e: e.matmul(out, lhsT=lhsT, rhs=rhs, start=start, stop=stop), r, w)

        def recip(out, in_, r, w):
            return k.op("dve", lambda e: e.reciprocal(out=out, in_=in_), r, w)

        def memset(eng, ap, val, w):
            return k.op(eng, lambda e: e.memset(ap, val), (), w)

        prm = k.sb("prm", [128, 64], F32)
        prm2 = k.sb("prm2", [128, 32], F32)
        cst = k.sb("cst", [128, CW], F32)
        cb = k.sb("cstb", [128, 1088], BF16)
        w2s = k.sb("w2s", [64, NLAYER, 2, 256], BF16)
        a2s = k.sb("a2s", [64, NLAYER, 2, 256], BF16)
        linkt = k.sb("link", [128, 4], F32)
        sinkr = k.sb("sinkr", [1, NLAYER * 2 * 512], F32)
        fgt = k.sb("fg", [128, D], F32)
        zt = k.sb("zeros", [128, 384], BF16)

        k.dma(prm[:, :], prm_in, writes=[prm])
        k.dma(cst[:, :], cst_in, writes=[cst])
        k.dma(cb[:, 0:544], cst_in[:, 0:544], writes=[cb], eng="pool")
        k.dma(cb[:, 544:1088], cst_in[:, 544:1088], writes=[cb], eng="pool")
        k.dma(w2s[:, :, :, :], w2_in, writes=[w2s], eng="pool")
        k.dma(a2s[:, :, :, :], a2_in, writes=[a2s], eng="pool")
        k.dma(linkt[:, 0:1], link_in, writes=[linkt])
        k.dma(sinkr[:, :], sink_in, writes=[sinkr])
        k.dma(fgt[:, :], fg_in, writes=[fgt])
        memset("pool", zt[:, :], 0.0, [zt])
        act(sinkr[:, :], sinkr[:, :], AF.Exp, [sinkr], [sinkr])
        vtm_b = Buf("vtm")
        proj_b = Buf("proj")
        ys_b = Buf("ys")
        bs_b = Buf("bs")
        att_b = Buf("att")
        x1_b = Buf("x1")
        yout_b = Buf("yout")
        outs = []
        for side in range(2):
            base = 0 if side == 0 else PAD + TC
            for j in range(PAD // 128):
                k.dma(vtm[base + j * 128: base + (j + 1) * 128, :], zt[:, :], reads=[zt], writes=[vtm_b])
        IDN = lambda t, p0, p1, c0, c1: t[p0:p1, c0:c1]
        ts("dve", prm2[:, 0:16], prm[:, 16:32], -1.0, 1.0, ALU.mult, ALU.add, [prm], [prm2])
        ts("dve", prm2[:, 16:32], prm[:, 16:32], 0.5, None, ALU.mult, None, [prm], [prm2])
        lnb = k.sb("lnb", [128, 8], F32)
        ts("dve", lnb[:, 0:4], prm[:, 52:56], -1.0, 1.0, ALU.mult, ALU.add, [prm], [lnb])
        cp("dve", linkt[:, 1:2], linkt[:, 0:1], [linkt], [linkt])
        cp("dve", linkt[:, 2:3], linkt[:, 0:1], [linkt], [linkt])
        memset("dve", linkt[64:128, 1:2], 1.0, [linkt])
        memset("dve", linkt[0:64, 2:3], 1.0, [linkt])
        cmask = k.sb("cmask", [128, 2], F32)
        memset("pool", cmask[:, :], 1.0, [cmask])
        memset("pool", cmask[0:64, 0:1], 0.0, [cmask])
        memset("pool", cmask[64:128, 1:2], 0.0, [cmask])
        lnb_in = din("lnb", [128, 4])
        k.dma(lnb[:, 4:8], lnb_in, writes=[lnb])

        I128b = cb[:, 0:128]
        BONES = cst[:, 192:320]
        BAVG = cst[:, 320:448]

        qm = k.sb("qmask", [64, 2, 3, 4, 128], BF16)
        for e in range(2):
            b0 = 704 + e * 192
            for j in range(4):
                memset("pool", qm[:, e, 0, j, 0:64], 1.0, [qm])
                cp("pool", qm[:, e, 0, j, 64:128], cst[0:64, b0:b0 + 64], [cst], [qm])
                cp("pool", qm[:, e, 1, j, 0:64], cst[0:64, b0:b0 + 64], [cst], [qm])
                cp("pool", qm[:, e, 1, j, 64:128], cst[0:64, b0 + 64:b0 + 128], [cst], [qm])
                cp("pool", qm[:, e, 2, j, 0:64], cst[0:64, b0 + 128:b0 + 192], [cst], [qm])
                memset("pool", qm[:, e, 2, j, 64:128], 1.0, [qm])
        bm = k.sb("bmask", [128, 2, 4, 128], BF16)
        for j in range(4):
            cp("pool", bm[:, 0, j, :], cst[:, 448:576], [cst], [bm])
            cp("pool", bm[:, 1, j, :], cst[:, 576:704], [cst], [bm])
        sel65 = k.sb("sel65", [65, 64], F32)
        memset("pool", sel65[:, :], 0.0, [sel65])
        memset("pool", sel65[64:65, :], 1.0, [sel65])
        ones1 = k.sb("ones1", [1, 64], F32)
        memset("pool", ones1[:, :], 1.0, [ones1])

        k.phase_begin()

        for L in range(NLAYER):
            x_src = x_in if L == 0 else x1d
            last = (L == NLAYER - 1)
            k.barrier()
            k.phase_begin()
            wbuf = k.sb("wbuf", [128, 8, NCOL], BF16)
            for c in range(8):
                for j0 in range(0, NCOL, 1184):
                    k.dma(wbuf[:, c, j0:j0 + 1184], win_in[L, c * 128:(c + 1) * 128, j0:j0 + 1184],
                          writes=[wbuf], eng="pool")

            xt = [k.sb(f"xt{i}", [128, D], F32) for i in range(4)]
            xs = [k.sb(f"xs{i}", [128, D], F32) for i in range(4)]
            junk = k.sb("junk", [128, D], BF16)
            ss = k.sb("ss", [128, 16], F32)
            hT = [k.sb(f"hT{i}", [128, 8, 512], BF16) for i in range(2)]
            gb = [k.sb(f"gb{i}", [128, 512], BF16) for i in range(6)]
            rt1 = [k.sb(f"rt1_{i}", [128, 512], F32) for i in range(2)]
            rt2 = [k.sb(f"rt2_{i}", [128, 512], F32) for i in range(2)]
            cst_t = [k.sb(f"cos{i}", [128, 512], F32) for i in range(2)]
            snt_t = [k.sb(f"sin{i}", [128, 512], F32) for i in range(2)]
            vtb = [k.sb(f"vtb{i}", [128, 384], BF16) for i in range(2)]
            xcnt = [0]

            def p1_norm(i):
                h = hT[i % 2]
                xas = []
                for m in range(4):
                    xa = xt[m]
                    xb = xs[m]
                    r0 = i * 512 + m * 128
                    k.dma(xa[:, :], x_src[r0:r0 + 128, :], reads=[x1_b] if L else [], writes=[xa])
                    sc = ss[:, m * 4:m * 4 + 1]
                    act(junk[:, :], xa[:, :], AF.Square, [xa], [junk, ss], accum=sc)
                    s2 = ss[:, m * 4 + 1:m * 4 + 2]
                    ts("dve", s2, sc, 1.0 / D, 1e-5, ALU.mult, ALU.add, [ss], [ss])
                    act(s2, s2, AF.Sqrt, [ss], [ss])
                    s3 = ss[:, m * 4 + 2:m * 4 + 3]
                    recip(s3, s2, [ss], [ss])
                    ts("dve", xb[:, :], xa[:, :], s3, None, ALU.mult, None, [xa, ss], [xb])
                    yield
                for c in range(8):
                    bk = banks[c % 2]
                    for m in range(4):
                        xb = xs[m]
                        k.op("pe", lambda e, bk=bk, c=c, m=m, xb=xb: e.transpose(
                            bk[:, m * 128:(m + 1) * 128], xb[:, c * 128:(c + 1) * 128], cst[:, 0:128]),
                            [xb, cst], [bk])
                    ts("dve", h[:, c, :], bk[:, :], prm[:, L * 8 + c:L * 8 + c + 1], None, ALU.mult, None, [bk, prm], [h])
                    yield

            gcnt = [0]
            bcnt = [0]
            groups = [([g], g, False) for g in range(16)]
            sl = 16
            g = 16
            for _ in range(9):
                groups.append(([g, g + 1], sl, True))
                g += 2
                sl += 1

            def p1_proj(i):
                h = hT[i % 2]
                t0 = i * 512
                ct = cst_t[i % 2]
                st_ = snt_t[i % 2]
                k.dma(ct[:, :], cs_in[:, t0:t0 + 512], writes=[ct])
                k.dma(st_[:, :], sn_in[:, t0:t0 + 512], writes=[st_])
                for (wg, slot, rope) in groups:
                    bks = []
                    for wgi in wg:
                        bk = banks[2 + bcnt[0] % 4]
                        bcnt[0] += 1
                        for c in range(8):
                            mm(bk[:, :], wbuf[:, c, wgi * 128:(wgi + 1) * 128], h[:, c, :], [wbuf, h], [bk],
                               start=(c == 0), stop=(c == 7))
                        bks.append(bk)
                    o = gb[gcnt[0] % 6]
                    gcnt[0] += 1
                    if not rope:
                        cp("act", o[:, :], bks[0][:, :], [bks[0]], [o])
                    else:
                        a = rt1[gcnt[0] % 2]
                        b = rt2[gcnt[0] % 2]
                        tt("dve", a[:, :], bks[0][:, :], ct[:, :], ALU.mult, [bks[0], ct], [a])
                        tt("dve", b[:, :], bks[1][:, :], st_[:, :], ALU.mult, [bks[1], st_], [b])
                        tt("pool", o[:, :], a[:, :], b[:, :], ALU.add, [a, b], [o])
                    k.dma(proj[slot, :, t0:t0 + 512], o[:, :], reads=[o], writes=[proj_b])
                    yield
                for m in range(4):
                    bk = banks[6 + m % 2]
                    for c in range(8):
                        mm(bk[:, 0:384], h[:, c, m * 128:(m + 1) * 128], wbuf[:, c, 34 * 128:NCOL], [wbuf, h], [bk],
                           start=(c == 0), stop=(c == 7))
                    vb = vtb[m % 2]
                    cp("act", vb[:, :], bk[:, 0:384], [bk], [vb])
                    r0 = PAD + t0 + m * 128
                    k.dma(vtm[r0:r0 + 128, :], vb[:, :], reads=[vb], writes=[vtm_b])
                    yield

            for _ in p1_norm(0):
                pass
            for i in range(NT):
                side = p1_norm(i + 1) if i + 1 < NT else None
                stepn = 0
                for _ in p1_proj(i):
                    stepn += 1
                    if side is not None and stepn % 2 == 0:
                        try:
                            next(side)
                        except StopIteration:
                            side = None
                if side is not None:
                    for _ in side:
                        pass

            if stop == "p1":
                break
            k.barrier()
            k.phase_begin()
            raw = k.sb("raw", [128, 8, 514], BF16)
            sh = k.sb("shf", [128, 8, 512], BF16)
            tA = [k.sb(f"tA{i}", [128, 512], F32) for i in range(2)]
            tB = [k.sb(f"tB{i}", [128, 512], F32) for i in range(2)]
            tw = k.sb("tw", [64, 512], BF16)
            lal = k.sb("lal", [64, 512], BF16)
            sg = k.sb("sg", [128, 512], F32)
            Ssc = k.sb("Ssc", [128, 512], F32)
            Sm = k.sb("Sm", [128, 512], F32)
            eP = k.sb("eP", [128, 512], F32)
            eN = k.sb("eN", [128, 512], F32)
            ePm = k.sb("ePm", [128, 512], F32)
            av = k.sb("av", [128, 512], F32)
            kq = k.sb("kq", [128, 512], F32)
            sq = sg
            nrm = Ssc
            kk = k.sb("kk", [128, 512], F32)
            kka = k.sb("kka", [128, 512], F32)
            ka = Sm
            keff = k.sb("keff", [128, 512], F32)
            rke = kq
            bon = [k.sb(f"bon{i}", [128, 512], F32) for i in range(2)]
            T1 = [[k.sb(f"T1_{s_}{i}", [128, 8, 192], BF16) for i in range(2)] for s_ in range(2)]
            T2 = [[k.sb(f"T2_{s_}{i}", [128, 8, 192], BF16) for i in range(2)] for s_ in range(2)]
            T1o = [[k.sb(f"T1o_{s_}{i}", [64, 8, 192], BF16) for i in range(2)] for s_ in range(2)]
            T2o = [[k.sb(f"T2o_{s_}{i}", [64, 8, 192], BF16) for i in range(2)] for s_ in range(2)]
            vT = [[k.sb(f"vT{s_}{i}", [64, 8, 128], BF16) for i in range(2)] for s_ in range(2)]
            NSL = 3
            Xb = [k.sb(f"X{i}", [64, 4, 128], BF16) for i in range(2 * NSL)]
            PNb = [k.sb(f"PN{i}", [64, 4, 128], BF16) for i in range(2 * NSL)]
            RBs = [k.sb(f"RB{i}", [64, 4, 128], BF16) for i in range(NSL)]
            RKs = [k.sb(f"RK{i}", [64, 4, 128], BF16) for i in range(NSL)]
            QYs = [k.sb(f"QY{i}", [64, 4, 128], BF16) for i in range(NSL)]
            MGs = [k.sb(f"MG{i}", [64, 4, 128], BF16) for i in range(NSL)]
            Ysb = [k.sb(f"Ys{i}", [64, 4, 64], F32) for i in range(NSL)]
            Hs = [[[k.sb(f"H{e}{h}{i}", [64, 64], BF16) for i in range(2)] for h in range(4)] for e in range(2)]
            hcnt = [[0] * 4 for _ in range(2)]
            for e in range(2):
                for h in range(4):
                    memset("pool", Hs[e][h][0][:, :], 0.0, [Hs[e][h][0]])
            for s_ in range(2):
                for t_ in T1[s_]:
                    for c in range(8):
                        cp("pool", t_[:, c, 0:64], cb[:, 128:192], [cb], [t_])

            def rwkv_prep(i, e, st):
                t0 = i * 512
                rw = raw
                lo = 1 if i == 0 else 0
                hi = 513 if i == NT - 1 else 514
                k.dma(rw[:, :, lo:hi], proj[0:8, :, t0 - 1 + lo:t0 - 1 + hi].rearrange("g p n -> p g n"),
                      reads=[proj_b], writes=[rw])
                if i == 0:
                    memset("pool", rw[:, :, 0:1], 0.0, [rw])
                if i == NT - 1:
                    memset("pool", rw[:, :, 513:514], 0.0, [rw])
                if t0 == HALF:
                    ts("dve", rw[:, :, 0:1], rw[:, :, 0:1], linkt[:, 0:1], None, ALU.mult, None, [rw, linkt], [rw])
                if t0 + 512 == HALF:
                    ts("dve", rw[:, :, 513:514], rw[:, :, 513:514], linkt[:, 0:1], None, ALU.mult, None, [rw, linkt], [rw])
                yield
                for s in range(8):
                    a_ = tA[s % 2]
                    b_ = tB[s % 2]
                    tt("pool", a_[:, :], rw[:, s, 0:512], rw[:, s, 2:514], ALU.add, [rw], [a_])
                    act(b_[:, :], rw[:, s, 1:513], AF.Copy, [rw, prm2], [b_], scale=prm2[:, L * 8 + s:L * 8 + s + 1])
                    stt(sh[:, s, :], a_[:, :], prm2[:, 16 + L * 8 + s:16 + L * 8 + s + 1], b_[:, :], ALU.mult, ALU.add,
                        [a_, b_, prm2], [sh])
                    if s % 2:
                        yield
                act(tw[:, :], sh[e * 64:(e + 1) * 64, 6, :], AF.Tanh, [sh], [tw])
                cp("act", lal[:, :], sh[e * 64:(e + 1) * 64, 7, :], [sh], [lal])
                rm = cst[:, 1152:1664] if e == 0 else cst[:, 1664:2176]
                for hh in range(2):
                    r_ = sh[:, hh, :]
                    k_ = sh[:, 2 + hh, :]
                    v_ = sh[:, 4 + hh, :]
                    t1 = T1[st][hh]
                    t2 = T2[st][hh]
                    bk = banks[0]
                    mm(bk[:, :], w2s[:, L, e, hh * 128:(hh + 1) * 128], tw[:, :], [w2s, tw], [bk])
                    cidx = 32 + L * 4 + e * 2 + hh
                    act(sg[:, :], bk[:, :], AF.Sigmoid, [bk, prm], [sg], bias=prm[:, cidx:cidx + 1], scale=1.0)
                    mm(bk[:, :], a2s[:, L, e, hh * 128:(hh + 1) * 128], lal[:, :], [a2s, lal], [bk])
                    act(av[:, :], bk[:, :], AF.Sigmoid, [bk, prm], [av], bias=prm[:, cidx + 8:cidx + 9], scale=1.0)
                    yield
                    if e == 0:
                        k.op("dve", lambda en: en.tensor_tensor_scan(out=Ssc[:, :], data0=rm, data1=sg[:, :], initial=0.0,
                                                                    op0=ALU.mult, op1=ALU.add), [sg, cst], [Ssc])
                    else:
                        k.op("dve", lambda en: en.tensor_tensor_scan(out=rev(Ssc[:, :]), data0=rev(rm), data1=rev(sg[:, :]),
                                                                    initial=0.0, op0=ALU.mult, op1=ALU.add), [sg, cst], [Ssc])
                    tt("pool", Sm[:, :], Ssc[:, :], sg[:, :], ALU.subtract, [Ssc, sg], [Sm])
                    act(eP[:, :], Ssc[:, :], AF.Exp, [Ssc], [eP], scale=-C0)
                    act(eN[:, :], Ssc[:, :], AF.Exp, [Ssc], [eN], scale=C0)
                    act(ePm[:, :], Sm[:, :], AF.Exp, [Sm], [ePm], scale=-C0)
                    yield
                    ts("dve", kq[:, :], k_, prm[:, 48 + L * 2 + hh:48 + L * 2 + hh + 1], None, ALU.mult, None, [sh, prm], [kq])
                    act(sq[:, :], kq[:, :], AF.Square, [kq], [sq])
                    mm(bk[:, :], BONES, sq[:, :], [cst, sq], [bk])
                    act(nrm[:, :], bk[:, :], AF.Sqrt, [bk], [nrm])
                    yield
                    ts("dve", nrm[:, :], nrm[:, :], 1e-12, None, ALU.max, None, [nrm], [nrm])
                    recip(nrm[:, :], nrm[:, :], [nrm], [nrm])
                    tt("pool", kk[:, :], kq[:, :], nrm[:, :], ALU.mult, [kq, nrm], [kk])
                    tt("pool", kka[:, :], kk[:, :], av[:, :], ALU.mult, [kk, av], [kka])
                    yield
                    ts("dve", ka[:, :], av[:, :], prm[:, 52 + L * 2 + hh:52 + L * 2 + hh + 1], lnb[:, L * 2 + hh:L * 2 + hh + 1],
                       ALU.mult, ALU.add, [av, prm, lnb], [ka])
                    tt("pool", keff[:, :], k_, ka[:, :], ALU.mult, [sh, ka], [keff])
                    v3 = lambda ap_: ap_.rearrange("p (c n) -> p c n", n=64)
                    stt(t2[:, :, 0:64], v3(kk[:, :]), -1.0, v3(ePm[:, :]), ALU.mult, ALU.mult, [kk, ePm], [t2])
                    tt("dve", t1[:, :, 128:192], v3(kka[:, :]), v3(eN[:, :]), ALU.mult, [kka, eN], [t1])
                    yield
                    tt("dve", t1[:, :, 64:128], v3(keff[:, :]), v3(eN[:, :]), ALU.mult, [keff, eN], [t1])
                    tt("dve", t2[:, :, 64:128], v3(r_), v3(eP[:, :]), ALU.mult, [sh, eP], [t2])
                    for c in range(8):
                        col = c * 64 + (63 if e == 0 else 0)
                        ts("pool", t2[:, c, 128:192], cst[:, 128:192], eP[:, col:col + 1], None, ALU.mult, None, [cst, eP], [t2])
                    yield
                    stt(rke[:, :], r_, prm[:, 56 + L * 2 + hh:56 + L * 2 + hh + 1], keff[:, :], ALU.mult, ALU.mult,
                        [sh, prm, keff], [rke])
                    mm(bk[:, :], BONES, rke[:, :], [cst, rke], [bk])
                    bo = bon[hh]
                    tt("dve", bo[:, :], bk[:, :], v_, ALU.mult, [bk, sh], [bo])
                    k.dma(bsd[e, hh * 128:(hh + 1) * 128, t0:t0 + 512], bo[:, :], reads=[bo], writes=[bs_b])
                    yield
                    vt = vT[st][hh]
                    for q in range(2):
                        b1 = banks[1]
                        for j in range(4):
                            c = q * 4 + j
                            mm(b1[0:64, j * 128:(j + 1) * 128], sh[:, 4 + hh, c * 64:(c + 1) * 64], I128b, [sh, cb], [b1])
                        cp("act", vt[:, q * 4:(q + 1) * 4, :], b1[0:64, :].rearrange("p (j n) -> p j n", n=128), [b1], [vt])
                        yield
                    cp("act", T1o[st][hh][:, :, :], t1[64:128, :, :], [t1], [T1o[st][hh]])
                    cp("dve", T2o[st][hh][:, :, :], t2[64:128, :, :], [t2], [T2o[st][hh]])
                    yield

            def quad(i, e, hh, pp, q, st):
                def gen(slot):
                    hd = hh * 2 + pp
                    if pp == 1:
                        t1, t2 = T1o[st][hh], T2o[st][hh]
                    else:
                        t1, t2 = T1[st][hh], T2[st][hh]
                    vt = vT[st][hh]
                    cs_ = [q * 4 + j for j in range(4)]
                    at = lambda c: t2[0:64, c, 0:64]
                    rtc = lambda c: t2[0:64, c, 64:128]
                    dW = lambda c: t2[0:64, c, 128:192]
                    Ic = lambda c: t1[0:64, c, 0:64]
                    ktc = lambda c: t1[0:64, c, 64:128]
                    btc = lambda c: t1[0:64, c, 128:192]
                    bX, bP = banks[2 + 2 * slot], banks[3 + 2 * slot]
                    v4 = lambda bk_: bk_[0:64, :].rearrange("p (j n) -> p j n", n=128)
                    for j, c in enumerate(cs_):
                        mm(bX[0:64, j * 128:(j + 1) * 128], at(c), t1[0:64, c, 0:128], [t1, t2], [bX])
                        mm(bP[0:64, j * 128:j * 128 + 64], at(c), btc(c), [t1, t2], [bP])
                        mm(bP[0:64, j * 128 + 64:(j + 1) * 128], btc(c), at(c), [t1, t2], [bP])
                    X = Xb[slot * 2]
                    X2 = Xb[slot * 2 + 1]
                    PN = PNb[slot * 2]
                    PN2 = PNb[slot * 2 + 1]
                    RB = RBs[slot]
                    RK = RKs[slot]
                    tt("dve", X[:, :, :], v4(bX), qm[:, e, 0, :, :], ALU.mult, [bX, qm], [X])
                    tt("dve", PN[:, :, :], v4(bP), qm[:, e, 1, :, :], ALU.mult, [bP, qm], [PN])
                    yield
                    for lv in range(6):
                        for j in range(4):
                            mm(bX[0:64, j * 128:(j + 1) * 128], PN[:, j, 64:128], X[:, j, :], [PN, X], [bX])
                            if lv < 5:
                                mm(bP[0:64, j * 128:j * 128 + 64], PN[:, j, 64:128], PN[:, j, 0:64], [PN], [bP])
                                mm(bP[0:64, j * 128 + 64:(j + 1) * 128], PN[:, j, 0:64], PN[:, j, 64:128], [PN], [bP])
                        tt("dve", X2[:, :, :], v4(bX), X[:, :, :], ALU.add, [bX, X], [X2])
                        if lv < 5:
                            cp("act", PN2[:, :, :], v4(bP), [bP], [PN2])
                        X, X2 = X2, X
                        PN, PN2 = PN2, PN
                        yield
                    for j, c in enumerate(cs_):
                        mm(bX[0:64, j * 128:(j + 1) * 128], btc(c), t2[0:64, c, 64:192], [t1, t2], [bX])
                        mm(bP[0:64, j * 128:(j + 1) * 128], ktc(c), t2[0:64, c, 64:192], [t1, t2], [bP])
                    tt("dve", RB[:, :, :], v4(bX), qm[:, e, 2, :, :], ALU.mult, [bX, qm], [RB])
                    tt("dve", RK[:, :, :], v4(bP), qm[:, e, 2, :, :], ALU.mult, [bP, qm], [RK])
                    yield
                    QY = QYs[slot]
                    MG = MGs[slot]
                    Itok = cb[0:64, 0:64]
                    for j, c in enumerate(cs_):
                        o = j * 128
                        mm(bX[0:64, o:o + 128], Ic(c), t2[0:64, c, 64:192], [t1, t2], [bX], start=True, stop=False)
                        mm(bX[0:64, o:o + 128], X[:, j, 0:64], RB[:, j, :], [X, RB], [bX], start=False, stop=True)
                        mm(bP[0:64, o:o + 128], X[:, j, 64:128], RB[:, j, :], [X, RB], [bP], start=True, stop=False)
                        mm(bP[0:64, o:o + 128], Itok, RK[:, j, :], [cb, RK], [bP], start=False, stop=True)
                    cp("act", QY[:, :, :], v4(bX), [bX], [QY])
                    cp("act", MG[:, :, :], v4(bP), [bP], [MG])
                    yield
                    order = range(4) if e == 0 else range(3, -1, -1)
                    Ys = Ysb[slot]
                    for j in order:
                        c = cs_[j]
                        hc = hcnt[e][hd]
                        H = Hs[e][hd][hc % 2]
                        Hn = Hs[e][hd][(hc + 1) % 2]
                        hcnt[e][hd] += 1
                        tok = i * 512 + c * 64
                        if (e == 0 and tok == HALF) or (e == 1 and tok + 64 == HALF):
                            ts("dve", H[:, :], H[:, :], linkt[0:64, 0:1], None, ALU.mult, None, [H, linkt], [H])
                        vv = vt[:, c, pp * 64:(pp + 1) * 64]
                        mm(bX[0:64, j * 64:(j + 1) * 64], vv, MG[:, j, 0:64], [vt, MG], [bX], start=True, stop=False)
                        mm(bX[0:64, j * 64:(j + 1) * 64], H[:, :], QY[:, j, 0:64], [H, QY], [bX], start=False, stop=True)
                        mm(bP[0:64, 0:64], MG[:, j, 64:128], vv, [MG, vt], [bP], start=True, stop=False)
                        mm(bP[0:64, 0:64], QY[:, j, 64:128], H[:, :], [QY, H], [bP], start=False, stop=True)
                        cp("dve", Hn[:, :], bP[0:64, 0:64], [bP], [Hn])
                        yield
                    cp("act", Ys[:, :, :], bX[0:64, 0:256].rearrange("p (j n) -> p j n", n=64), [bX], [Ys])
                    t0 = i * 512 + q * 256
                    k.dma(ysd[e, hd * 64:(hd + 1) * 64, t0:t0 + 256], Ys[:, :, :].rearrange("p j n -> p (j n)"),
                          reads=[Ys], writes=[ys_b])
                return gen

            def run_side(factories, nslots, side):
                active = []
                free = list(range(nslots))
                it = iter(factories)
                done = False
                while active or not done:
                    while free and not done:
                        try:
                            f_ = next(it)
                        except StopIteration:
                            done = True
                            break
                        sl_ = free.pop(0)
                        active.append((sl_, f_(sl_)))
                    if side is not None:
                        try:
                            next(side)
                        except StopIteration:
                            side = None
                    for (sl_, g_) in list(active):
                        try:
                            next(g_)
                        except StopIteration:
                            active.remove((sl_, g_))
                            free.append(sl_)
                if side is not None:
                    for _ in side:
                        pass

            seq = []
            for i in range(NT):
                seq.append((i, 0))
                seq.append((NT - 1 - i, 1))
            if stop == "p2c":
                seq = [(i, 0) for i in range(NT)]
            for _ in rwkv_prep(seq[0][0], seq[0][1], 0):
                pass
            for n_, (i, e) in enumerate(seq):
                st = n_ % 2
                qorder = [0, 1] if e == 0 else [1, 0]
                facs = [quad(i, e, hh, pp, q, st) for q in qorder for hh in range(2) for pp in range(2)]
                side = None
                if n_ + 1 < len(seq):
                    side = rwkv_prep(seq[n_ + 1][0], seq[n_ + 1][1], (n_ + 1) % 2)
                run_side(facs, NSL, side)
            if stop in ("p2a", "p2b", "p2c", "p2d", "p2e"):
                break

            if stop == "p2":
                break
            k.barrier()
            k.phase_begin()
            def run_pipelined(factories, nslots):
                active = []
                free = list(range(nslots))
                it = iter(factories)
                done = False
                while active or not done:
                    while free and not done:
                        try:
                            f_ = next(it)
                        except StopIteration:
                            done = True
                            break
                        sl_ = free.pop(0)
                        active.append((sl_, f_(sl_)))
                    for (sl_, g_) in list(active):
                        try:
                            next(g_)
                        except StopIteration:
                            active.remove((sl_, g_))
                            free.append(sl_)

            qTb = k.sb("qTb", [64, 4, WIN], BF16)
            kTb = k.sb("kTb", [64, 4, WIN + 2048], BF16)
            acc = k.sb("acc", [65, 4, WIN], F32)
            Vb = [k.sb(f"Vb{i}", [128, 4, 65], BF16) for i in range(8)]
            Pt = [k.sb(f"Pt{i}", [128, 512], BF16) for i in range(6)]
            rden = [k.sb(f"rden{i}", [64, 512], F32) for i in range(2)]
            ocb = [k.sb(f"ocb{i}", [64, 512], BF16) for i in range(2)]
            for v_ in Vb:
                memset("pool", v_[:, :, :], 1.0, [v_])
            ncnt = [0]
            NW = TC // WIN
            for w in range(NW):
                w0 = w * WIN
                k.dma(qTb[:, :, :], proj[16:18, :, w0:w0 + WIN].rearrange("g (h p) n -> p (g h) n", p=64), reads=[proj_b], writes=[qTb])
                lo = max(0, w0 - 1024)
                hi = min(TC, w0 + WIN + 1024)
                k.dma(kTb[:, :, lo - (w0 - 1024):hi - (w0 - 1024)], proj[18:20, :, lo:hi].rearrange("g (h p) n -> p (g h) n", p=64),
                      reads=[proj_b], writes=[kTb])
                if w0 - 1024 < 0:
                    memset("pool", kTb[:, :, 0:1024], 0.0, [kTb])
                if w0 + WIN + 1024 > TC:
                    memset("pool", kTb[:, :, WIN + 1024:WIN + 2048], 0.0, [kTb])
                combos = []
                for di, d in enumerate((1, 4, 16)):
                    nb = WIN // (128 * d)
                    for r in range(d):
                        for b in range(nb):
                            combos.append((di, d, nb, r, b))

                def v_issue(n, w0=w0, combos=combos):
                    if n >= len(combos):
                        return
                    di, d, nb, r, b = combos[n]
                    for tile in range(2):
                        kpos0 = 128 * b - 64 + 128 * tile
                        rs = PAD + w0 + r + d * kpos0
                        vb = Vb[(n % 4) * 2 + tile]
                        k.dma(vb[:, :, 0:64], vtm[rs:rs + 127 * d + 1:d, 0:256].rearrange("p (h c) -> p h c", c=64),
                              reads=[vtm_b], writes=[vb])

                def b_unit(n, w0=w0, combos=combos):
                    def gen(slot):
                        di, d, nb, r, b = combos[n]
                        qc0 = r + d * 128 * b
                        v_issue(n + 2)
                        pts = []
                        vbs = []
                        sbks = []
                        for tile in range(2):
                            kpos0 = 128 * b - 64 + 128 * tile
                            kc0 = 1024 + r + d * kpos0
                            sbk = banks[3 * slot + tile]
                            for h in range(4):
                                mm(sbk[:, h * 128:(h + 1) * 128], kTb[:, h, kc0:kc0 + 127 * d + 1:d],
                                   qTb[:, h, qc0:qc0 + 127 * d + 1:d], [kTb, qTb], [sbk])
                            sbks.append(sbk)
                            pts.append(Pt[3 * slot + tile])
                            vbs.append(Vb[(n % 4) * 2 + tile])
                        yield
                        for tile in range(2):
                            pt = pts[tile]
                            act(pt[:, :], sbks[tile][:, :], AF.Exp, [sbks[tile]], [pt], scale=0.125)
                            colm = None
                            if tile == 0 and b == 0:
                                if w0 == 0:
                                    colm = (cmask, cmask[:, 0:1])
                                elif w0 == HALF:
                                    colm = (linkt, linkt[:, 1:2])
                            if tile == 1 and b == nb - 1:
                                if w0 + WIN == TC:
                                    colm = (cmask, cmask[:, 1:2])
                                elif w0 + WIN == HALF:
                                    colm = (linkt, linkt[:, 2:3])
                            band = bm[:, tile, :, :].rearrange("p a n -> p (a n)")
                            if colm is None:
                                tt("dve", pt[:, :], pt[:, :], band, ALU.mult, [pt, bm], [pt])
                            else:
                                stt(pt[:, :], pt[:, :], colm[1], band, ALU.mult, ALU.mult, [pt, bm, colm[0]], [pt])
                        yield
                        ob = banks[6 + slot]
                        for h in range(4):
                            for tile in range(2):
                                mm(ob[0:65, h * 128:(h + 1) * 128], vbs[tile][:, h, :], pts[tile][:, h * 128:(h + 1) * 128],
                                   [vbs[tile], pts[tile]], [ob], start=(tile == 0), stop=(tile == 1))
                        yield
                        accv = acc[0:65, :, qc0:qc0 + 127 * d + 1:d]
                        obv = ob[0:65, :].rearrange("p (h n) -> p h n", n=128)
                        if di == 0:
                            cp("act", accv, obv, [ob], [acc])
                        else:
                            tt("dve", accv, accv, obv, ALU.add, [ob, acc], [acc])
                    return gen

                v_issue(0)
                v_issue(1)
                run_pipelined([b_unit(n) for n in range(len(combos))], 2)
                for h in range(4):
                    for ch in range(WIN // 512):
                        n = ncnt[0]
                        ncnt[0] += 1
                        db = banks[2 + 3 * (n % 2)]
                        mm(db[0:64, :], sel65[:, :], acc[0:65, h, ch * 512:(ch + 1) * 512], [sel65, acc], [db])
                        rd = rden[n % 2]
                        recip(rd[:, :], db[0:64, :], [db], [rd])
                        oc_ = ocb[n % 2]
                        tt("pool", oc_[:, :], acc[0:64, h, ch * 512:(ch + 1) * 512], rd[:, :], ALU.mult, [acc, rd], [oc_])
                        k.dma(att[h // 2, (h % 2) * 64:(h % 2) * 64 + 64, w0 + ch * 512:w0 + (ch + 1) * 512], oc_[:, :],
                              reads=[oc_], writes=[att_b])
            if stop == "p3b":
                break
            qTc = [k.sb(f"qTc{i}", [64, 2, 4, 512], BF16) for i in range(2)]
            kTc = [k.sb(f"kTc{i}", [64, 2, 768], BF16) for i in range(2)]
            Vc = [k.sb(f"Vc{i}", [128, 6, 2, 65], BF16) for i in range(2)]
            Osb = [k.sb(f"Osb{i}", [65, 512], F32) for i in range(2)]
            occ = [[k.sb(f"occ{i}{g_}", [64, 4, 512], BF16) for g_ in range(2)] for i in range(2)]
            for v_ in Vc:
                memset("pool", v_[:, :, :, :], 1.0, [v_])

            def c_load(sbi):
                t0 = sbi * 512
                qt = qTc[sbi % 2]
                kt_ = kTc[sbi % 2]
                vc = Vc[sbi % 2]
                for g_ in range(2):
                    k.dma(qt[:, g_, :, :], proj[20:24, g_ * 64:(g_ + 1) * 64, t0:t0 + 512].rearrange("a p n -> p a n"),
                          reads=[proj_b], writes=[qt])
                lo = max(0, t0 - 128)
                hi = min(TC, t0 + 640)
                k.dma(kt_[:, :, lo - (t0 - 128):hi - (t0 - 128)], proj[24, :, lo:hi].rearrange("(g p) n -> p g n", p=64),
                      reads=[proj_b], writes=[kt_])
                if t0 - 128 < 0:
                    memset("pool", kt_[:, :, 0:128], 0.0, [kt_])
                if t0 + 640 > TC:
                    memset("pool", kt_[:, :, 640:768], 0.0, [kt_])
                for kt6 in range(6):
                    rs = PAD + t0 - 128 + kt6 * 128
                    k.dma(vc[:, kt6, :, 0:64], vtm[rs:rs + 128, 256:384].rearrange("p (g c) -> p g c", c=64),
                          reads=[vtm_b], writes=[vc])

            sinkb = [k.sb(f"sinkb{g_}", [64, 512], F32) for g_ in range(2)]
            for g_ in range(2):
                so = (L * 2 + g_) * 512
                mm(banks[0][0:64, :], ones1[:, :], sinkr[0:1, so:so + 512], [ones1, sinkr], [banks[0]])
                cp("act", sinkb[g_][:, :], banks[0][0:64, :], [banks[0]], [sinkb[g_]])
            onesb = k.sb("onesb", [128, 64], BF16)
            memset("pool", onesb[:, :], 1.0, [onesb])
            dtmp = [k.sb(f"dtmp{i}", [64, 512], F32) for i in range(2)]

            def c_unit(sbi, jq, g_):
                def gen(slot):
                    t0 = sbi * 512
                    qt = qTc[sbi % 2]
                    kt_ = kTc[sbi % 2]
                    vc = Vc[sbi % 2]
                    if jq == 1 and g_ == 0 and sbi + 1 < NT:
                        c_load(sbi + 1)
                    qtok = t0 + 128 * jq
                    tiles = []
                    for rel in range(3):
                        ktile = jq + rel
                        ktok = t0 - 128 + 128 * ktile
                        if ktok < 0 or ktok >= TC:
                            continue
                        tiles.append((rel, ktile, ktok))
                    sb_ = [banks[4 * slot], banks[4 * slot + 1]]
                    ob = banks[4 * slot + 2]
                    db = banks[4 * slot + 3]

                    def s_mm(ti):
                        rel, ktile, ktok = tiles[ti]
                        sbk = sb_[ti % 2]
                        mm(sbk[:, :], kt_[:, g_, ktile * 128:(ktile + 1) * 128],
                           qt[:, g_, :, jq * 128:(jq + 1) * 128], [kt_, qt], [sbk])

                    def p_ev(ti):
                        rel, ktile, ktok = tiles[ti]
                        sbk = sb_[ti % 2]
                        pt = Pt[3 * slot + ti]
                        act(pt[:, :], sbk[:, :], AF.Exp, [sbk], [pt], scale=0.125)
                        cross = (ktok < HALF) != (qtok < HALF)
                        if rel != 1:
                            band = bm[:, 0 if rel == 0 else 1, :, :].rearrange("p a n -> p (a n)")
                            if cross:
                                stt(pt[:, :], pt[:, :], linkt[:, 0:1], band, ALU.mult, ALU.mult, [pt, bm, linkt], [pt])
                            else:
                                tt("dve", pt[:, :], pt[:, :], band, ALU.mult, [pt, bm], [pt])
                    for ti in range(min(2, len(tiles))):
                        s_mm(ti)
                    yield
                    for ti in range(min(2, len(tiles))):
                        p_ev(ti)
                    yield
                    if len(tiles) > 2:
                        s_mm(2)
                        yield
                        p_ev(2)
                        yield
                    for ti, (rel, ktile, ktok) in enumerate(tiles):
                        pt = Pt[3 * slot + ti]
                        mm(ob[0:64, :], vc[:, ktile, g_, 0:64], pt[:, :], [vc, pt], [ob], start=(ti == 0), stop=(ti == len(tiles) - 1))
                    for ti, (rel, ktile, ktok) in enumerate(tiles):
                        pt = Pt[3 * slot + ti]
                        mm(db[0:64, :], onesb[:, :], pt[:, :], [onesb, pt], [db], start=(ti == 0), stop=(ti == len(tiles) - 1))
                    yield
                    osb = Osb[slot]
                    cp("act", osb[0:64, :], ob[0:64, :], [ob], [osb])
                    dt_ = dtmp[slot]
                    tt("dve", dt_[:, :], db[0:64, :], sinkb[g_][:, :], ALU.add, [db, sinkb[g_]], [dt_])
                    yield
                    rd = rden[slot]
                    recip(rd[:, :], dt_[:, :], [dt_], [rd])
                    yield
                    oc_ = occ[sbi % 2][g_]
                    tt("pool", oc_[:, :, jq * 128:(jq + 1) * 128], osb[0:64, :].rearrange("p (a n) -> p a n", n=128),
                       rd[:, :].rearrange("p (a n) -> p a n", n=128), ALU.mult, [osb, rd], [oc_])
                    if jq == 3:
                        k.dma(att[2:6, g_ * 64:(g_ + 1) * 64, t0:t0 + 512].rearrange("a p n -> p a n"), oc_[:, :, :],
                              reads=[oc_], writes=[att_b])
                return gen

            c_load(0)
            run_pipelined([c_unit(sbi, jq, g_) for sbi in range(NT) for jq in range(4) for g_ in range(2)], 2)

            k.barrier()
            k.phase_begin()
            wob = k.sb("wob", [128, 8, D], BF16)
            for c in range(8):
                k.dma(wob[:, c, :], wout_in[L, c * 128:(c + 1) * 128, :], writes=[wob], eng="pool")
            gt = [k.sb(f"gt{i}", [128, 8, 512], BF16) for i in range(2)]
            att_t = [k.sb(f"att{i}", [128, 6, 512], BF16) for i in range(2)]
            yv = [k.sb(f"yv{i}", [128, 2, 2, 512], F32) for i in range(2)]
            bv = [k.sb(f"bv{i}", [128, 2, 2, 512], F32) for i in range(2)]
            xt4 = [k.sb(f"xt4_{i}", [128, 4, D], F32) for i in range(2)]
            mix = [k.sb(f"mix{i}", [128, 8, 512], BF16) for i in range(2)]
            tYs = [[k.sb(f"tY{sl}{hh}", [128, 512], F32) for hh in range(2)] for sl in range(2)]
            tCs = [[k.sb(f"tC{sl}{hh}", [128, 512], F32) for hh in range(2)] for sl in range(2)]
            tRs = [[k.sb(f"tR{sl}{hh}", [128, 512], F32) for hh in range(2)] for sl in range(2)]
            tG = [k.sb(f"tG{i}", [128, 512], BF16) for i in range(4)]
            ss4 = k.sb("ss4", [128, 8], F32)
            junk4 = k.sb("junk4", [128, D], BF16)
            xo = [k.sb(f"xo{i}", [128, D], F32) for i in range(2)]

            def p4_load(i):
                t0 = i * 512
                k.dma(gt[i % 2][:, :, :], proj[8:16, :, t0:t0 + 512].rearrange("g p n -> p g n"), reads=[proj_b], writes=[gt[i % 2]])
                k.dma(att_t[i % 2][:, :, :], att[:, :, t0:t0 + 512].rearrange("g p n -> p g n"), reads=[att_b], writes=[att_t[i % 2]])
                for e in range(2):
                    k.dma(yv[i % 2][:, e, :, :], ysd[e, :, t0:t0 + 512].rearrange("(h p) n -> p h n", p=128),
                          reads=[ys_b], writes=[yv[i % 2]])
                    k.dma(bv[i % 2][:, e, :, :], bsd[e, :, t0:t0 + 512].rearrange("(h p) n -> p h n", p=128),
                          reads=[bs_b], writes=[bv[i % 2]])
                for m in range(4):
                    k.dma(xt4[i % 2][:, m, :], x_src[t0 + m * 128:t0 + (m + 1) * 128, :], reads=[x1_b] if L else [],
                          writes=[xt4[i % 2]])

            opc = [0]

            def p4_unit(i):
                def gen(slot):
                    t0 = i * 512
                    g_ = gt[i % 2]
                    a_ = att_t[i % 2]
                    y_ = yv[i % 2]
                    b_ = bv[i % 2]
                    x_ = xt4[i % 2]
                    mx = mix[i % 2]
                    tY = tYs[slot]
                    tC = tCs[slot]
                    tR = tRs[slot]
                    bks = [banks[slot * 2], banks[slot * 2 + 1]]
                    for hh in range(2):
                        tt("pool", tY[hh][:, :], y_[:, 0, hh, :], y_[:, 1, hh, :], ALU.add, [y_], [tY[hh]])
                        mm(bks[hh][:, :], BAVG, tY[hh][:, :], [cst, tY[hh]], [bks[hh]])
                    yield
                    for hh in range(2):
                        tt("dve", tC[hh][:, :], tY[hh][:, :], bks[hh][:, :], ALU.subtract, [tY[hh], bks[hh]], [tC[hh]])
                        act(tY[hh][:, :], tC[hh][:, :], AF.Square, [tC[hh]], [tY[hh]])
                        mm(bks[hh][:, :], BAVG, tY[hh][:, :], [cst, tY[hh]], [bks[hh]])
                    yield
                    for hh in range(2):
                        ts("dve", tR[hh][:, :], bks[hh][:, :], 64e-5, None, ALU.add, None, [bks[hh]], [tR[hh]])
                        act(tR[hh][:, :], tR[hh][:, :], AF.Sqrt, [tR[hh]], [tR[hh]])
                    yield
                    for hh in range(2):
                        recip(tR[hh][:, :], tR[hh][:, :], [tR[hh]], [tR[hh]])
                        tt("pool", tC[hh][:, :], tC[hh][:, :], tR[hh][:, :], ALU.mult, [tC[hh], tR[hh]], [tC[hh]])
                        tt("pool", tR[hh][:, :], b_[:, 0, hh, :], b_[:, 1, hh, :], ALU.add, [b_], [tR[hh]])
                    yield
                    for hh in range(2):
                        ts("dve", tC[hh][:, :], tC[hh][:, :], prm[:, 60 + L * 2 + hh:60 + L * 2 + hh + 1],
                           lnb[:, 4 + L * 2 + hh:4 + L * 2 + hh + 1], ALU.mult, ALU.add, [tC[hh], prm, lnb], [tC[hh]])
                        tt("pool", tC[hh][:, :], tC[hh][:, :], tR[hh][:, :], ALU.add, [tC[hh], tR[hh]], [tC[hh]])
                        tg = tG[hh]
                        act(tg[:, :], g_[:, hh, :], AF.Silu, [g_], [tg])
                        tt("dve", mx[:, hh, :], tC[hh][:, :], tg[:, :], ALU.mult, [tC[hh], tg], [mx])
                    yield
                    for j in range(6):
                        tg = tG[j % 4]
                        act(tg[:, :], g_[:, 2 + j, :], AF.Silu, [g_], [tg])
                        tt("pool" if j % 2 else "dve", mx[:, 2 + j, :], a_[:, j, :], tg[:, :], ALU.mult, [a_, tg], [mx])
                        if j % 2:
                            yield
                    for m in range(4):
                        for hf in range(2):
                            bk = banks[4 + opc[0] % 4]
                            opc[0] += 1
                            for c in range(8):
                                mm(bk[:, :], mx[:, c, m * 128:(m + 1) * 128], wob[:, c, hf * 512:(hf + 1) * 512], [mx, wob], [bk],
                                   start=(c == 0), stop=(c == 7))
                            tt("dve", x_[:, m, hf * 512:(hf + 1) * 512], bk[:, :], x_[:, m, hf * 512:(hf + 1) * 512], ALU.add,
                               [bk, x_], [x_])
                        r0 = t0 + m * 128
                        if not last:
                            k.dma(x1d[r0:r0 + 128, :], x_[:, m, :], reads=[x_], writes=[x1_b])
                        else:
                            n = i * 4 + m
                            sc = ss4[:, (n % 2) * 4:(n % 2) * 4 + 1]
                            s2 = ss4[:, (n % 2) * 4 + 1:(n % 2) * 4 + 2]
                            s3 = ss4[:, (n % 2) * 4 + 2:(n % 2) * 4 + 3]
                            act(junk4[:, :], x_[:, m, :], AF.Square, [x_], [junk4, ss4], accum=sc)
                            ts("dve", s2, sc, 1.0 / D, 1e-5, ALU.mult, ALU.add, [ss4], [ss4])
                            act(s2, s2, AF.Sqrt, [ss4], [ss4])
                            recip(s3, s2, [ss4], [ss4])
                            o_ = xo[n % 2]
                            stt(o_[:, :], x_[:, m, :], s3, fgt[:, :], ALU.mult, ALU.mult, [x_, ss4, fgt], [o_])
                            k.dma(y_out[r0:r0 + 128, :], o_[:, :], reads=[o_], writes=[yout_b])
                            outs.append(o_)
                        if m < 3:
                            yield
                    if i + 2 < NT:
                        p4_load(i + 2)
                return gen

            p4_load(0)
            if NT > 1:
                p4_load(1)
            run_pipelined([p4_unit(i) for i in range(NT)], 2)
            if stop == "p4":
                break

        k.barrier()
        k.finalize(final_waits=list({id(o): o for o in outs}.values()))
    return nc


def _col_layout():
    rot = lambda base: [base + (j + 32) % 64 for j in range(64)]
    pl = lambda base: [base + j for j in range(64)]
    cols = list(range(1024))
    R = 1024
    cols += [R + j for j in range(256)]
    cols += [R + 1024 + j for j in range(256)]
    permc = []
    for a in range(4):
        permc += pl(a * 64) + pl((4 + a) * 64)
    cols += [R + 2048 + j for j in permc]
    for base in (R + 256, R + 512):
        for gi in range(2):
            cols += pl(base + gi * 128) + pl(base + gi * 128 + 64)
            cols += rot(base + gi * 128) + rot(base + gi * 128 + 64)
    for a in range(4):
        base = R + 1280
        cols += pl(base + a * 64) + pl(base + (4 + a) * 64)
        cols += rot(base + a * 64) + rot(base + (4 + a) * 64)
    base = R + 1792
    cols += pl(base) + pl(base + 64)
    cols += rot(base) + rot(base + 64)
    cols += [R + 768 + j for j in range(256)]
    cols += [R + 1920 + j for j in range(128)]
    assert len(cols) == NCOL
    return np.array(cols), np.array(permc)


def _constants():
    c = np.zeros((128, CW), np.float32)
    p = np.arange(128)[:, None]
    j = np.arange(128)[None, :]
    c[:, 0:128] = (p == j)
    c[:, 128:192] = (np.arange(64)[None, :] == (p % 64))
    c[:, 192:320] = (p // 64 == j // 64)
    c[:, 320:448] = (p // 64 == j // 64) / 64.0
    c[:, 448:576] = (p >= j)
    c[:, 576:704] = (p <= j)
    r = np.arange(64)[:, None]
    q = np.arange(64)[None, :]
    c[0:64, 704:768] = (q < r)
    c[0:64, 768:832] = (r < q)
    c[0:64, 832:896] = (r <= q)
    c[0:64, 896:960] = (q > r)
    c[0:64, 960:1024] = (r > q)
    c[0:64, 1024:1088] = (r >= q)
    col = np.arange(512)
    c[:, 1152:1664] = (col % 64 != 0)[None, :]
    c[:, 1664:2176] = (col % 64 != 63)[None, :]
    return c


def _prep_shared(inp):
    cols, permc = _col_layout()
    f = lambda a: np.ascontiguousarray(np.asarray(a, dtype=np.float32))
    win = f(np.asarray(inp["w_in"])[:, :, cols])
    rows = np.concatenate([np.arange(512), 512 + permc])
    wout = f(np.asarray(inp["w_out"])[:, rows, :])
    prm = np.zeros((128, 64), np.float32)
    lnb = np.zeros((128, 4), np.float32)
    for l in range(NLAYER):
        prm[:, l * 8:(l + 1) * 8] = np.asarray(inp["norm_g"])[l].reshape(8, 128).T
        prm[:, 16 + l * 8:16 + (l + 1) * 8] = np.asarray(inp["tshift_mu"])[l].reshape(8, 128).T
        for e in range(2):
            prm[:, 32 + l * 4 + e * 2:32 + l * 4 + e * 2 + 2] = np.asarray(inp["rwkv_w0"])[l, e].reshape(2, 128).T
            prm[:, 40 + l * 4 + e * 2:40 + l * 4 + e * 2 + 2] = np.asarray(inp["rwkv_a0"])[l, e].reshape(2, 128).T
        prm[:, 48 + l * 2:48 + l * 2 + 2] = np.asarray(inp["rwkv_k_k"])[l].reshape(2, 128).T
        prm[:, 52 + l * 2:52 + l * 2 + 2] = np.asarray(inp["rwkv_k_a"])[l].reshape(2, 128).T
        prm[:, 56 + l * 2:56 + l * 2 + 2] = np.asarray(inp["rwkv_r_k"])[l].reshape(2, 128).T
        prm[:, 60 + l * 2:60 + l * 2 + 2] = np.asarray(inp["ln_x_w"])[l].reshape(2, 128).T
        lnb[:, l * 2:l * 2 + 2] = np.asarray(inp["ln_x_b"])[l].reshape(2, 128).T
    w2s = f(np.asarray(inp["rwkv_w2"]).transpose(2, 0, 1, 3))
    a2s = f(np.asarray(inp["rwkv_a2"]).transpose(2, 0, 1, 3))
    sink = np.asarray(inp["attn_sink"], dtype=np.float32)
    sinkrow = f(np.repeat(sink.reshape(NLAYER, 2, 4, 1), 128, axis=3).reshape(1, -1))
    fg = f(np.tile(np.asarray(inp["final_g"], dtype=np.float32)[None, :], (128, 1)))
    return dict(win=win, wout=wout, prm=prm, lnb=lnb, w2s=w2s, a2s=a2s, sinkrow=sinkrow, fg=fg, cst=_constants())


def _rope_tables(pos):
    inv = (10000.0 ** (-np.arange(0, 64, 2, dtype=np.float32) / np.float32(64))).astype(np.float32)
    ang = (pos.astype(np.float32)[:, None] * inv[None, :]).astype(np.float32)
    cos = np.cos(ang).astype(np.float32).T
    sin = np.sin(ang).astype(np.float32).T
    p = np.arange(128)
    cs = cos[p % 32, :]
    sn = sin[p % 32, :] * np.where((p % 64) < 32, -1.0, 1.0)[:, None].astype(np.float32)
    return np.ascontiguousarray(cs, dtype=np.float32), np.ascontiguousarray(sn, dtype=np.float32)


_NC_CACHE = {}


def _run(core_x, core_link, inp, TC, debug=False, stop=None):
    sh = _prep_shared(inp)
    key = (TC, debug, stop)
    if key not in _NC_CACHE:
        _NC_CACHE[key] = build_nc(TC, debug, stop)
    nc = _NC_CACHE[key]
    HALF = TC // 2
    maps = []
    tabs = {}
    for x, lk in zip(core_x, core_link):
        if lk not in tabs:
            t = np.arange(TC)
            pos = t if lk else (t % HALF)
            tabs[lk] = _rope_tables(pos)
        cs, sn = tabs[lk]
        m = dict(sh)
        m["x"] = np.ascontiguousarray(x, dtype=np.float32)
        m["link"] = np.full((128, 1), float(lk), np.float32)
        m["cs"] = cs
        m["sn"] = sn
        maps.append(m)
    res = run_bass_kernel_spmd(nc, maps, core_ids=list(range(len(maps))))
    return res.results


def kernel(**inputs):
    xp = np.asarray(inputs["x_prompt"], dtype=np.float32)
    xs = np.asarray(inputs["x_sample"], dtype=np.float32)
    TC = 16384
    core_x = [xp[0:2].reshape(TC, D), xp[2:4].reshape(TC, D), xs[0], xs[1]]
    core_link = [0, 0, 1, 1]
    for _ in range(4):
        core_x.append(np.zeros((TC, D), np.float32))
        core_link.append(0)
    res = _run(core_x, core_link, inputs, TC)
    yp = np.stack([res[0]["y"], res[1]["y"]]).reshape(4, 8192, D).astype(np.float32)
    ys = np.stack([res[2]["y"], res[3]["y"]]).astype(np.float32)
    return (yp, ys)
```

```python
import os
import numpy as np
from contextlib import ExitStack
import concourse.bass as bass
import concourse.mybir as mybir
from concourse.bass_utils import run_bass_kernel_spmd

F32 = mybir.dt.float32
BF16 = mybir.dt.bfloat16
AF = mybir.ActivationFunctionType
ALU = mybir.AluOpType

ENGS = ("pe", "act", "dve", "pool", "sp")
EPOCH = 12000
D = 1024
NCOL = 34 * 128 + 384
C0 = float(np.exp(-0.5))
PAD = 1024
NLAYER = 2
CW = 2176


class Buf:
    __slots__ = ("name", "w", "r")

    def __init__(self, name=""):
        self.name = name
        self.w = None
        self.r = []


class DSem:
    def __init__(self, k):
        self.k = k
        self.sems = []
        self.finals = {}
        self.count = 0
        self.new_epoch()
        k.dsems.append(self)

    def new_epoch(self):
        if self.sems:
            self.finals[id(self.sems[-1])] = self.count
        self.sems.append(self.k.new_sem())
        self.count = 0

    def value_for(self, h):
        if self.sems[-1] is h:
            return self.count
        return self.finals[id(h)]


class Op:
    __slots__ = ("eng", "fn", "deps", "signal", "dma", "dsem", "dsem_h", "cnt", "ep")

    def __init__(self, eng, fn):
        self.eng = eng
        self.fn = fn
        self.deps = []
        self.signal = False
        self.dma = False


class T:
    def __init__(self, k, t, name):
        self.k = k
        self.t = t
        self.b = Buf(name)
        self._ds = None

    def __getitem__(self, key):
        return self.t[key]

    @property
    def ds(self):
        if self._ds is None:
            k = self.k
            if k.sb_mark is not None:
                self._ds = k.free_ds.pop() if k.free_ds else DSem(k)
                k.phase_ds.append(self._ds)
            else:
                self._ds = DSem(k)
        return self._ds


class K:
    def __init__(self, nc, es):
        self.nc = nc
        self.es = es
        self.ops = {e: [] for e in ENGS}
        self.nsem = 0
        self.dsems = []
        self.nt = 0
        self.sb_lo = 16512
        self.sb_hi = 229344
        self.sb_cur = self.sb_lo
        self.sb_mark = None
        self.free_ds = []
        self.phase_ds = []

    def new_sem(self):
        self.nsem += 1
        return self.es.enter_context(self.nc.semaphore(f"s{self.nsem}"))

    def sb(self, name, shape, dtype):
        nb = int(np.prod(shape[1:])) * (4 if dtype == F32 else 2)
        nb = (nb + 63) // 64 * 64
        off = self.sb_cur
        self.sb_cur += nb
        assert self.sb_cur <= self.sb_hi, f"SBUF overflow at {name}: {self.sb_cur}"
        self.nt += 1
        t = self.nc.alloc_sbuf_tensor_at(f"{name}_{self.nt}", list(shape), dtype, offset=off)
        return T(self, t, name)

    def phase_begin(self):
        if self.sb_mark is None:
            self.sb_mark = self.sb_cur
        self.sb_cur = self.sb_mark
        self.free_ds.extend(self.phase_ds)
        self.phase_ds = []

    def ps(self, name, shape, dtype=F32):
        t = self.es.enter_context(self.nc.psum_tensor(name, list(shape), dtype))
        return T(self, t, name)

    @staticmethod
    def _b(x):
        return x.b if isinstance(x, T) else x

    def _dep(self, op, d):
        if d.dma:
            op.deps.append(("dma", d.dsem_h, d.dsem.value_for(d.dsem_h)))
        else:
            op.deps.append(d)

    def _track(self, op, reads, writes):
        eng = op.eng
        for x in reads:
            b = self._b(x)
            if b.w is not None and not (b.w.eng == eng and eng == "pe" and not b.w.dma):
                self._dep(op, b.w)
        for x in writes:
            b = self._b(x)
            for r in b.r:
                if r.dma or r.eng != eng:
                    self._dep(op, r)
            if b.w is not None and (b.w.dma or b.w.eng != eng):
                self._dep(op, b.w)
        for x in reads:
            self._b(x).r.append(op)
        for x in writes:
            b = self._b(x)
            b.w = op
            b.r = []

    def op(self, eng, fn, reads=(), writes=()):
        o = Op(eng, fn)
        self._track(o, reads, writes)
        self.ops[eng].append(o)
        return o

    def dma(self, out, in_, reads=(), writes=(), sem=None, eng="sp", **kw):
        if sem is None:
            for x in list(writes) + list(reads):
                if isinstance(x, T):
                    sem = x
                    break
        ds = sem.ds
        if ds.count + 16 > EPOCH:
            ds.new_epoch()
        o = Op(eng, lambda e: e.dma_start(out=out, in_=in_, **kw))
        o.dma = True
        self._track(o, reads, writes)
        ds.count += 16
        o.dsem = ds
        o.dsem_h = ds.sems[-1]
        self.ops[eng].append(o)
        return o

    def barrier(self):
        lasts = {}
        for e in ENGS:
            for o in reversed(self.ops[e]):
                if not o.dma:
                    lasts[e] = o
                    break
        dd = []
        for ds in self.dsems:
            if ds.count:
                dd.append(("dma", ds.sems[-1], ds.count))
        for e in ENGS:
            o = Op(e, lambda eng: eng.nop())
            for e2, lo in lasts.items():
                if e2 != e:
                    o.deps.append(lo)
            o.deps.extend(dd)
            self.ops[e].append(o)

    def finalize(self, final_waits=()):
        for e in ENGS:
            for o in self.ops[e]:
                for d in o.deps:
                    if not isinstance(d, tuple):
                        d.signal = True
        for e in ENGS:
            cnt = 0
            sems = [self.new_sem()]
            for o in self.ops[e]:
                if o.dma:
                    continue
                if o.signal:
                    if cnt + 1 > EPOCH:
                        sems.append(self.new_sem())
                        cnt = 0
                    cnt += 1
                    o.cnt = cnt
                    o.ep = sems[-1]
        fw = []
        for t in final_waits:
            ds = t.ds
            for h in ds.sems:
                v = ds.value_for(h)
                if v:
                    fw.append((h, v))
        k = self
        with self.nc.Block() as block:
            def emit(e_name):
                def body(e):
                    seen = {}
                    for o in k.ops[e_name]:
                        need = {}
                        for d in o.deps:
                            if isinstance(d, tuple):
                                key, h, v = id(d[1]), d[1], d[2]
                            else:
                                key, h, v = id(d.ep), d.ep, d.cnt
                            if seen.get(key, 0) >= v:
                                continue
                            if key not in need or need[key][1] < v:
                                need[key] = (h, v)
                        for key, (h, v) in need.items():
                            e.wait_ge(h, v)
                            seen[key] = v
                        ins = o.fn(e)
                        if o.dma:
                            ins.then_inc(o.dsem_h, 16)
                        elif o.signal:
                            ins.then_inc(o.ep, 1)
                    if e_name == "sp":
                        for (h, v) in fw:
                            e.wait_ge(h, v)
                return body
            block.sync(emit("sp"))
            block.scalar(emit("act"))
            block.vector(emit("dve"))
            block.gpsimd(emit("pool"))
            block.tensor(emit("pe"))


def rev(ap):
    pat = [list(p) for p in ap.ap]
    off = ap.offset + (pat[-1][1] - 1) * pat[-1][0]
    pat[-1][0] = -pat[-1][0]
    return bass.AP(ap.tensor, off, pat)


def build_nc(TC, debug=False, stop=None):
    HALF = TC // 2
    NT = TC // 512
    WIN = min(2048, HALF)
    assert HALF % WIN == 0 and WIN % 2048 == 0 or WIN == HALF
    nc = bass.Bass("TRN2", target_bir_lowering=False)
    dk = "ExternalOutput" if debug else "Internal"

    def din(name, shape, dt=F32):
        return nc.dram_tensor(name, list(shape), dt, kind="ExternalInput").ap()

    x_in = din("x", [TC, D])
    link_in = din("link", [128, 1])
    cs_in = din("cs", [128, TC])
    sn_in = din("sn", [128, TC])
    win_in = din("win", [NLAYER, D, NCOL])
    wout_in = din("wout", [NLAYER, D, D])
    prm_in = din("prm", [128, 64])
    w2_in = din("w2s", [64, NLAYER, 2, 256])
    a2_in = din("a2s", [64, NLAYER, 2, 256])
    sink_in = din("sinkrow", [1, NLAYER * 2 * 512])
    fg_in = din("fg", [128, D])
    cst_in = din("cst", [128, CW])
    y_out = nc.dram_tensor("y", [TC, D], F32, kind="ExternalOutput").ap()

    proj = nc.dram_tensor("proj", [25, 128, TC], BF16, kind=dk).ap()
    vtm = nc.dram_tensor("vtm", [TC + 2 * PAD, 384], BF16, kind=dk).ap()
    ysd = nc.dram_tensor("ysd", [2, 256, TC], F32, kind=dk).ap()
    bsd = nc.dram_tensor("bsd", [2, 256, TC], F32, kind=dk).ap()
    att = nc.dram_tensor("att", [6, 128, TC], BF16, kind=dk).ap()
    x1d = nc.dram_tensor("x1d", [TC, D], F32, kind=dk).ap()

    with ExitStack() as es:
        k = K(nc, es)
        banks = [k.ps(f"bk{i}", [128, 512]) for i in range(8)]

        def act(out, in_, func, r, w, bias=None, scale=None, accum=None, eng="act"):
            kw = {}
            if bias is not None:
                kw["bias"] = bias
            if scale is not None:
                kw["scale"] = scale
            if accum is not None:
                kw["accum_out"] = accum
            return k.op("act", lambda e: e.activation(out=out, in_=in_, func=func, **kw), r, w)

        def tt(eng, out, in0, in1, op, r, w):
            return k.op(eng, lambda e: e.tensor_tensor(out=out, in0=in0, in1=in1, op=op), r, w)

        def ts(eng, out, in0, s1, s2, op0, op1, r, w):
            if op1 is None:
                return k.op(eng, lambda e: e.tensor_scalar(out=out, in0=in0, scalar1=s1, scalar2=None, op0=op0), r, w)
            return k.op(eng, lambda e: e.tensor_scalar(out=out, in0=in0, scalar1=s1, scalar2=s2, op0=op0, op1=op1), r, w)

        def stt(out, in0, scalar, in1, op0, op1, r, w):
            return k.op("dve", lambda e: e.scalar_tensor_tensor(out=out, in0=in0, scalar=scalar, in1=in1, op0=op0, op1=op1), r, w)

        def cp(eng, out, in_, r, w):
            if eng == "act":
                return k.op("act", lambda e: e.activation(out=out, in_=in_, func=AF.Copy), r, w)
            return k.op(eng, lambda e: e.tensor_copy(out=out, in_=in_), r, w)

        def mm(out, lhsT, rhs, r, w, start=True, stop=True):
            return k.op("pe", lambda e: e.matmul(out, lhsT=lhsT, rhs=rhs, start=start, stop=stop), r, w)

        def recip(out, in_, r, w):
            return k.op("dve", lambda e: e.reciprocal(out=out, in_=in_), r, w)

        def memset(eng, ap, val, w):
            return k.op(eng, lambda e: e.memset(ap, val), (), w)

        prm = k.sb("prm", [128, 64], F32)
        prm2 = k.sb("prm2", [128, 32], F32)
        cst = k.sb("cst", [128, CW], F32)
        cb = k.sb("cstb", [128, 1088], BF16)
        w2s = k.sb("w2s", [64, NLAYER, 2, 256], BF16)
        a2s = k.sb("a2s", [64, NLAYER, 2, 256], BF16)
        linkt = k.sb("link", [128, 4], F32)
        sinkr = k.sb("sinkr", [1, NLAYER * 2 * 512], F32)
        fgt = k.sb("fg", [128, D], F32)
        zt = k.sb("zeros", [128, 384], BF16)

        k.dma(prm[:, :], prm_in, writes=[prm])
        k.dma(cst[:, :], cst_in, writes=[cst])
        k.dma(cb[:, 0:544], cst_in[:, 0:544], writes=[cb], eng="pool")
        k.dma(cb[:, 544:1088], cst_in[:, 544:1088], writes=[cb], eng="pool")
        k.dma(w2s[:, :, :, :], w2_in, writes=[w2s], eng="pool")
        k.dma(a2s[:, :, :, :], a2_in, writes=[a2s], eng="pool")
        k.dma(linkt[:, 0:1], link_in, writes=[linkt])
        k.dma(sinkr[:, :], sink_in, writes=[sinkr])
        k.dma(fgt[:, :], fg_in, writes=[fgt])
        memset("pool", zt[:, :], 0.0, [zt])
        act(sinkr[:, :], sinkr[:, :], AF.Exp, [sinkr], [sinkr])
        vtm_b = Buf("vtm")
        proj_b = Buf("proj")
        ys_b = Buf("ys")
        bs_b = Buf("bs")
        att_b = Buf("att")
        x1_b = Buf("x1")
        yout_b = Buf("yout")
        outs = []
        for side in range(2):
            base = 0 if side == 0 else PAD + TC
            for j in range(PAD // 128):
                k.dma(vtm[base + j * 128: base + (j + 1) * 128, :], zt[:, :], reads=[zt], writes=[vtm_b])
        IDN = lambda t, p0, p1, c0, c1: t[p0:p1, c0:c1]
        ts("dve", prm2[:, 0:16], prm[:, 16:32], -1.0, 1.0, ALU.mult, ALU.add, [prm], [prm2])
        ts("dve", prm2[:, 16:32], prm[:, 16:32], 0.5, None, ALU.mult, None, [prm], [prm2])
        lnb = k.sb("lnb", [128, 8], F32)
        ts("dve", lnb[:, 0:4], prm[:, 52:56], -1.0, 1.0, ALU.mult, ALU.add, [prm], [lnb])
        cp("dve", linkt[:, 1:2], linkt[:, 0:1], [linkt], [linkt])
        cp("dve", linkt[:, 2:3], linkt[:, 0:1], [linkt], [linkt])
        memset("dve", linkt[64:128, 1:2], 1.0, [linkt])
        memset("dve", linkt[0:64, 2:3], 1.0, [linkt])
        cmask = k.sb("cmask", [128, 2], F32)
        memset("pool", cmask[:, :], 1.0, [cmask])
        memset("pool", cmask[0:64, 0:1], 0.0, [cmask])
        memset("pool", cmask[64:128, 1:2], 0.0, [cmask])
        lnb_in = din("lnb", [128, 4])
        k.dma(lnb[:, 4:8], lnb_in, writes=[lnb])

        I128b = cb[:, 0:128]
        BONES = cst[:, 192:320]
        BAVG = cst[:, 320:448]

        qm = k.sb("qmask", [64, 2, 3, 4, 128], BF16)
        for e in range(2):
            b0 = 704 + e * 192
            for j in range(4):
                memset("pool", qm[:, e, 0, j, 0:64], 1.0, [qm])
                cp("pool", qm[:, e, 0, j, 64:128], cst[0:64, b0:b0 + 64], [cst], [qm])
                cp("pool", qm[:, e, 1, j, 0:64], cst[0:64, b0:b0 + 64], [cst], [qm])
                cp("pool", qm[:, e, 1, j, 64:128], cst[0:64, b0 + 64:b0 + 128], [cst], [qm])
                cp("pool", qm[:, e, 2, j, 0:64], cst[0:64, b0 + 128:b0 + 192], [cst], [qm])
                memset("pool", qm[:, e, 2, j, 64:128], 1.0, [qm])
        bm = k.sb("bmask", [128, 2, 4, 128], BF16)
        for j in range(4):
            cp("pool", bm[:, 0, j, :], cst[:, 448:576], [cst], [bm])
            cp("pool", bm[:, 1, j, :], cst[:, 576:704], [cst], [bm])
        sel65 = k.sb("sel65", [65, 64], F32)
        memset("pool", sel65[:, :], 0.0, [sel65])
        memset("pool", sel65[64:65, :], 1.0, [sel65])
        ones1 = k.sb("ones1", [1, 64], F32)
        memset("pool", ones1[:, :], 1.0, [ones1])

        k.phase_begin()

        for L in range(NLAYER):
            x_src = x_in if L == 0 else x1d
            last = (L == NLAYER - 1)
            k.barrier()
            k.phase_begin()
            wbuf = k.sb("wbuf", [128, 8, NCOL], BF16)
            for c in range(8):
                for j0 in range(0, NCOL, 1184):
                    k.dma(wbuf[:, c, j0:j0 + 1184], win_in[L, c * 128:(c + 1) * 128, j0:j0 + 1184],
                          writes=[wbuf], eng="pool")

            xt = [k.sb(f"xt{i}", [128, D], F32) for i in range(4)]
            xs = [k.sb(f"xs{i}", [128, D], F32) for i in range(4)]
            junk = k.sb("junk", [128, D], BF16)
            ss = k.sb("ss", [128, 16], F32)
            hT = [k.sb(f"hT{i}", [128, 8, 512], BF16) for i in range(2)]
            gb = [k.sb(f"gb{i}", [128, 512], BF16) for i in range(6)]
            rt1 = [k.sb(f"rt1_{i}", [128, 512], F32) for i in range(2)]
            rt2 = [k.sb(f"rt2_{i}", [128, 512], F32) for i in range(2)]
            cst_t = [k.sb(f"cos{i}", [128, 512], F32) for i in range(2)]
            snt_t = [k.sb(f"sin{i}", [128, 512], F32) for i in range(2)]
            vtb = [k.sb(f"vtb{i}", [128, 384], BF16) for i in range(2)]
            xcnt = [0]

            def p1_norm(i):
                h = hT[i % 2]
                xas = []
                for m in range(4):
                    xa = xt[m]
                    xb = xs[m]
                    r0 = i * 512 + m * 128
                    k.dma(xa[:, :], x_src[r0:r0 + 128, :], reads=[x1_b] if L else [], writes=[xa])
                    sc = ss[:, m * 4:m * 4 + 1]
                    act(junk[:, :], xa[:, :], AF.Square, [xa], [junk, ss], accum=sc)
                    s2 = ss[:, m * 4 + 1:m * 4 + 2]
                    ts("dve", s2, sc, 1.0 / D, 1e-5, ALU.mult, ALU.add, [ss], [ss])
                    act(s2, s2, AF.Sqrt, [ss], [ss])
                    s3 = ss[:, m * 4 + 2:m * 4 + 3]
                    recip(s3, s2, [ss], [ss])
                    ts("dve", xb[:, :], xa[:, :], s3, None, ALU.mult, None, [xa, ss], [xb])
                    yield
                for c in range(8):
                    bk = banks[c % 2]
                    for m in range(4):
                        xb = xs[m]
                        k.op("pe", lambda e, bk=bk, c=c, m=m, xb=xb: e.transpose(
                            bk[:, m * 128:(m + 1) * 128], xb[:, c * 128:(c + 1) * 128], cst[:, 0:128]),
                            [xb, cst], [bk])
                    ts("dve", h[:, c, :], bk[:, :], prm[:, L * 8 + c:L * 8 + c + 1], None, ALU.mult, None, [bk, prm], [h])
                    yield

            gcnt = [0]
            bcnt = [0]
            groups = [([g], g, False) for g in range(16)]
            sl = 16
            g = 16
            for _ in range(9):
                groups.append(([g, g + 1], sl, True))
                g += 2
                sl += 1

            def p1_proj(i):
                h = hT[i % 2]
                t0 = i * 512
                ct = cst_t[i % 2]
                st_ = snt_t[i % 2]
                k.dma(ct[:, :], cs_in[:, t0:t0 + 512], writes=[ct])
                k.dma(st_[:, :], sn_in[:, t0:t0 + 512], writes=[st_])
                for (wg, slot, rope) in groups:
                    bks = []
                    for wgi in wg:
                        bk = banks[2 + bcnt[0] % 4]
                        bcnt[0] += 1
                        for c in range(8):
                            mm(bk[:, :], wbuf[:, c, wgi * 128:(wgi + 1) * 128], h[:, c, :], [wbuf, h], [bk],
                               start=(c == 0), stop=(c == 7))
                        bks.append(bk)
                    o = gb[gcnt[0] % 6]
                    gcnt[0] += 1
                    if not rope:
                        cp("act", o[:, :], bks[0][:, :], [bks[0]], [o])
                    else:
                        a = rt1[gcnt[0] % 2]
                        b = rt2[gcnt[0] % 2]
                        tt("dve", a[:, :], bks[0][:, :], ct[:, :], ALU.mult, [bks[0], ct], [a])
                        tt("dve", b[:, :], bks[1][:, :], st_[:, :], ALU.mult, [bks[1], st_], [b])
                        tt("pool", o[:, :], a[:, :], b[:, :], ALU.add, [a, b], [o])
                    k.dma(proj[slot, :, t0:t0 + 512], o[:, :], reads=[o], writes=[proj_b])
                    yield
                for m in range(4):
                    bk = banks[6 + m % 2]
                    for c in range(8):
                        mm(bk[:, 0:384], h[:, c, m * 128:(m + 1) * 128], wbuf[:, c, 34 * 128:NCOL], [wbuf, h], [bk],
                           start=(c == 0), stop=(c == 7))
                    vb = vtb[m % 2]
                    cp("act", vb[:, :], bk[:, 0:384], [bk], [vb])
                    r0 = PAD + t0 + m * 128
                    k.dma(vtm[r0:r0 + 128, :], vb[:, :], reads=[vb], writes=[vtm_b])
                    yield

            for _ in p1_norm(0):
                pass
            for i in range(NT):
                side = p1_norm(i + 1) if i + 1 < NT else None
                stepn = 0
                for _ in p1_proj(i):
                    stepn += 1
                    if side is not None and stepn % 2 == 0:
                        try:
                            next(side)
                        except StopIteration:
                            side = None
                if side is not None:
                    for _ in side:
                        pass

            if stop == "p1":
                break
            k.barrier()
            k.phase_begin()
            raw = k.sb("raw", [128, 8, 514], BF16)
            sh = k.sb("shf", [128, 8, 512], BF16)
            tA = [k.sb(f"tA{i}", [128, 512], F32) for i in range(2)]
            tB = [k.sb(f"tB{i}", [128, 512], F32) for i in range(2)]
            tw = k.sb("tw", [64, 512], BF16)
            lal = k.sb("lal", [64, 512], BF16)
            sg = k.sb("sg", [128, 512], F32)
            Ssc = k.sb("Ssc", [128, 512], F32)
            Sm = k.sb("Sm", [128, 512], F32)
            eP = k.sb("eP", [128, 512], F32)
            eN = k.sb("eN", [128, 512], F32)
            ePm = k.sb("ePm", [128, 512], F32)
            av = k.sb("av", [128, 512], F32)
            kq = k.sb("kq", [128, 512], F32)
            sq = sg
            nrm = Ssc
            kk = k.sb("kk", [128, 512], F32)
            kka = k.sb("kka", [128, 512], F32)
            ka = Sm
            keff = k.sb("keff", [128, 512], F32)
            rke = kq
            bon = [k.sb(f"bon{i}", [128, 512], F32) for i in range(2)]
            T1 = [[k.sb(f"T1_{s_}{i}", [128, 8, 192], BF16) for i in range(2)] for s_ in range(2)]
            T2 = [[k.sb(f"T2_{s_}{i}", [128, 8, 192], BF16) for i in range(2)] for s_ in range(2)]
            T1o = [[k.sb(f"T1o_{s_}{i}", [64, 8, 192], BF16) for i in range(2)] for s_ in range(2)]
            T2o = [[k.sb(f"T2o_{s_}{i}", [64, 8, 192], BF16) for i in range(2)] for s_ in range(2)]
            vT = [[k.sb(f"vT{s_}{i}", [64, 8, 128], BF16) for i in range(2)] for s_ in range(2)]
            NSL = 4
            bank_lo = [Buf(f"lo{i}") for i in range(8)]
            bank_hi = [Buf(f"hi{i}") for i in range(8)]
            Xb = [k.sb(f"X{i}", [64, 4, 128], BF16) for i in range(2 * NSL)]
            PNb = [k.sb(f"PN{i}", [64, 4, 128], BF16) for i in range(2 * NSL)]
            RBs = [k.sb(f"RB{i}", [64, 4, 128], BF16) for i in range(NSL)]
            RKs = [k.sb(f"RK{i}", [64, 4, 128], BF16) for i in range(NSL)]
            QYs = [k.sb(f"QY{i}", [64, 4, 128], BF16) for i in range(NSL)]
            MGs = [k.sb(f"MG{i}", [64, 4, 128], BF16) for i in range(NSL)]
            Ysb = [k.sb(f"Ys{i}", [64, 4, 64], F32) for i in range(NSL)]
            Hs = [[[k.sb(f"H{e}{h}{i}", [64, 64], BF16) for i in range(2)] for h in range(4)] for e in range(2)]
            hcnt = [[0] * 4 for _ in range(2)]
            for e in range(2):
                for h in range(4):
                    memset("pool", Hs[e][h][0][:, :], 0.0, [Hs[e][h][0]])
            for s_ in range(2):
                for t_ in T1[s_]:
                    for c in range(8):
                        cp("pool", t_[:, c, 0:64], cb[:, 128:192], [cb], [t_])

            def rwkv_prep(i, e, st):
                t0 = i * 512
                rw = raw
                lo = 1 if i == 0 else 0
                hi = 513 if i == NT - 1 else 514
                k.dma(rw[:, :, lo:hi], proj[0:8, :, t0 - 1 + lo:t0 - 1 + hi].rearrange("g p n -> p g n"),
                      reads=[proj_b], writes=[rw])
                if i == 0:
                    memset("pool", rw[:, :, 0:1], 0.0, [rw])
                if i == NT - 1:
                    memset("pool", rw[:, :, 513:514], 0.0, [rw])
                if t0 == HALF:
                    ts("dve", rw[:, :, 0:1], rw[:, :, 0:1], linkt[:, 0:1], None, ALU.mult, None, [rw, linkt], [rw])
                if t0 + 512 == HALF:
                    ts("dve", rw[:, :, 513:514], rw[:, :, 513:514], linkt[:, 0:1], None, ALU.mult, None, [rw, linkt], [rw])
                yield
                for s in range(8):
                    a_ = tA[s % 2]
                    b_ = tB[s % 2]
                    tt("pool", a_[:, :], rw[:, s, 0:512], rw[:, s, 2:514], ALU.add, [rw], [a_])
                    act(b_[:, :], rw[:, s, 1:513], AF.Copy, [rw, prm2], [b_], scale=prm2[:, L * 8 + s:L * 8 + s + 1])
                    stt(sh[:, s, :], a_[:, :], prm2[:, 16 + L * 8 + s:16 + L * 8 + s + 1], b_[:, :], ALU.mult, ALU.add,
                        [a_, b_, prm2], [sh])
                    if s % 2:
                        yield
                act(tw[:, :], sh[e * 64:(e + 1) * 64, 6, :], AF.Tanh, [sh], [tw])
                cp("act", lal[:, :], sh[e * 64:(e + 1) * 64, 7, :], [sh], [lal])
                rm = cst[:, 1152:1664] if e == 0 else cst[:, 1664:2176]
                for hh in range(2):
                    r_ = sh[:, hh, :]
                    k_ = sh[:, 2 + hh, :]
                    v_ = sh[:, 4 + hh, :]
                    t1 = T1[st][hh]
                    t2 = T2[st][hh]
                    bk = banks[0]
                    mm(bk[:, :], w2s[:, L, e, hh * 128:(hh + 1) * 128], tw[:, :], [w2s, tw], [bk])
                    cidx = 32 + L * 4 + e * 2 + hh
                    act(sg[:, :], bk[:, :], AF.Sigmoid, [bk, prm], [sg], bias=prm[:, cidx:cidx + 1], scale=1.0)
                    mm(bk[:, :], a2s[:, L, e, hh * 128:(hh + 1) * 128], lal[:, :], [a2s, lal], [bk])
                    act(av[:, :], bk[:, :], AF.Sigmoid, [bk, prm], [av], bias=prm[:, cidx + 8:cidx + 9], scale=1.0)
                    yield
                    if e == 0:
                        k.op("dve", lambda en: en.tensor_tensor_scan(out=Ssc[:, :], data0=rm, data1=sg[:, :], initial=0.0,
                                                                    op0=ALU.mult, op1=ALU.add), [sg, cst], [Ssc])
                    else:
                        k.op("dve", lambda en: en.tensor_tensor_scan(out=rev(Ssc[:, :]), data0=rev(rm), data1=rev(sg[:, :]),
                                                                    initial=0.0, op0=ALU.mult, op1=ALU.add), [sg, cst], [Ssc])
                    tt("pool", Sm[:, :], Ssc[:, :], sg[:, :], ALU.subtract, [Ssc, sg], [Sm])
                    act(eP[:, :], Ssc[:, :], AF.Exp, [Ssc], [eP], scale=-C0)
                    act(eN[:, :], Ssc[:, :], AF.Exp, [Ssc], [eN], scale=C0)
                    act(ePm[:, :], Sm[:, :], AF.Exp, [Sm], [ePm], scale=-C0)
                    yield
                    ts("dve", kq[:, :], k_, prm[:, 48 + L * 2 + hh:48 + L * 2 + hh + 1], None, ALU.mult, None, [sh, prm], [kq])
                    act(sq[:, :], kq[:, :], AF.Square, [kq], [sq])
                    mm(bk[:, :], BONES, sq[:, :], [cst, sq], [bk])
                    act(nrm[:, :], bk[:, :], AF.Sqrt, [bk], [nrm])
                    yield
                    ts("dve", nrm[:, :], nrm[:, :], 1e-12, None, ALU.max, None, [nrm], [nrm])
                    recip(nrm[:, :], nrm[:, :], [nrm], [nrm])
                    tt("pool", kk[:, :], kq[:, :], nrm[:, :], ALU.mult, [kq, nrm], [kk])
                    tt("pool", kka[:, :], kk[:, :], av[:, :], ALU.mult, [kk, av], [kka])
                    yield
                    ts("dve", ka[:, :], av[:, :], prm[:, 52 + L * 2 + hh:52 + L * 2 + hh + 1], lnb[:, L * 2 + hh:L * 2 + hh + 1],
                       ALU.mult, ALU.add, [av, prm, lnb], [ka])
                    tt("pool", keff[:, :], k_, ka[:, :], ALU.mult, [sh, ka], [keff])
                    v3 = lambda ap_: ap_.rearrange("p (c n) -> p c n", n=64)
                    stt(t2[:, :, 0:64], v3(kk[:, :]), -1.0, v3(ePm[:, :]), ALU.mult, ALU.mult, [kk, ePm], [t2])
                    tt("dve", t1[:, :, 128:192], v3(kka[:, :]), v3(eN[:, :]), ALU.mult, [kka, eN], [t1])
                    yield
                    tt("dve", t1[:, :, 64:128], v3(keff[:, :]), v3(eN[:, :]), ALU.mult, [keff, eN], [t1])
                    tt("dve", t2[:, :, 64:128], v3(r_), v3(eP[:, :]), ALU.mult, [sh, eP], [t2])
                    for c in range(8):
                        col = c * 64 + (63 if e == 0 else 0)
                        ts("pool", t2[:, c, 128:192], cst[:, 128:192], eP[:, col:col + 1], None, ALU.mult, None, [cst, eP], [t2])
                    yield
                    stt(rke[:, :], r_, prm[:, 56 + L * 2 + hh:56 + L * 2 + hh + 1], keff[:, :], ALU.mult, ALU.mult,
                        [sh, prm, keff], [rke])
                    mm(bk[:, :], BONES, rke[:, :], [cst, rke], [bk])
                    bo = bon[hh]
                    tt("dve", bo[:, :], bk[:, :], v_, ALU.mult, [bk, sh], [bo])
                    k.dma(bsd[e, hh * 128:(hh + 1) * 128, t0:t0 + 512], bo[:, :], reads=[bo], writes=[bs_b])
                    yield
                    vt = vT[st][hh]
                    for q in range(2):
                        b1 = banks[1]
                        for j in range(4):
                            c = q * 4 + j
                            mm(b1[0:64, j * 128:(j + 1) * 128], sh[:, 4 + hh, c * 64:(c + 1) * 64], I128b, [sh, cb], [b1])
                        cp("act", vt[:, q * 4:(q + 1) * 4, :], b1[0:64, :].rearrange("p (j n) -> p j n", n=128), [b1], [vt])
                        yield
                    cp("act", T1o[st][hh][:, :, :], t1[64:128, :, :], [t1], [T1o[st][hh]])
                    cp("dve", T2o[st][hh][:, :, :], t2[64:128, :, :], [t2], [T2o[st][hh]])
                    yield

            def quad(i, e, hh, pp, q, st):
                def gen(slot):
                    hd = hh * 2 + pp
                    if pp == 1:
                        t1, t2 = T1o[st][hh], T2o[st][hh]
                    else:
                        t1, t2 = T1[st][hh], T2[st][hh]
                    vt = vT[st][hh]
                    cs_ = [q * 4 + j for j in range(4)]
                    at = lambda c: t2[0:64, c, 0:64]
                    rtc = lambda c: t2[0:64, c, 64:128]
                    dW = lambda c: t2[0:64, c, 128:192]
                    Ic = lambda c: t1[0:64, c, 0:64]
                    ktc = lambda c: t1[0:64, c, 64:128]
                    btc = lambda c: t1[0:64, c, 128:192]
                    bk = banks[2 + slot]
                    LO = bank_lo[2 + slot]
                    HI = bank_hi[2 + slot]
                    lo4 = bk[0:64, :].rearrange("p (j n) -> p j n", n=128)
                    hi4 = bk[64:128, :].rearrange("p (j n) -> p j n", n=128)
                    for j, c in enumerate(cs_):
                        mm(bk[0:64, j * 128:(j + 1) * 128], at(c), t1[0:64, c, 0:128], [t1, t2], [LO])
                        mm(bk[64:128, j * 128:j * 128 + 64], at(c), btc(c), [t1, t2], [HI])
                        mm(bk[64:128, j * 128 + 64:(j + 1) * 128], btc(c), at(c), [t1, t2], [HI])
                    X = Xb[slot * 2]
                    X2 = Xb[slot * 2 + 1]
                    PN = PNb[slot * 2]
                    PN2 = PNb[slot * 2 + 1]
                    RB = RBs[slot]
                    RK = RKs[slot]
                    tt("dve", X[:, :, :], lo4, qm[:, e, 0, :, :], ALU.mult, [LO, qm], [X])
                    cp("act", PN2[:, :, :], hi4, [HI], [PN2])
                    tt("pool", PN[:, :, :], PN2[:, :, :], qm[:, e, 1, :, :], ALU.mult, [PN2, qm], [PN])
                    yield
                    for lv in range(6):
                        for j in range(4):
                            mm(bk[0:64, j * 128:(j + 1) * 128], PN[:, j, 64:128], X[:, j, :], [PN, X], [LO])
                            if lv < 5:
                                mm(bk[64:128, j * 128:j * 128 + 64], PN[:, j, 64:128], PN[:, j, 0:64], [PN], [HI])
                                mm(bk[64:128, j * 128 + 64:(j + 1) * 128], PN[:, j, 0:64], PN[:, j, 64:128], [PN], [HI])
                        tt("dve", X2[:, :, :], lo4, X[:, :, :], ALU.add, [LO, X], [X2])
                        if lv < 5:
                            cp("act", PN2[:, :, :], hi4, [HI], [PN2])
                        X, X2 = X2, X
                        PN, PN2 = PN2, PN
                        yield
                    for j, c in enumerate(cs_):
                        mm(bk[0:64, j * 128:(j + 1) * 128], btc(c), t2[0:64, c, 64:192], [t1, t2], [LO])
                        mm(bk[64:128, j * 128:(j + 1) * 128], ktc(c), t2[0:64, c, 64:192], [t1, t2], [HI])
                    tt("dve", RB[:, :, :], lo4, qm[:, e, 2, :, :], ALU.mult, [LO, qm], [RB])
                    cp("act", PN2[:, :, :], hi4, [HI], [PN2])
                    tt("pool", RK[:, :, :], PN2[:, :, :], qm[:, e, 2, :, :], ALU.mult, [PN2, qm], [RK])
                    yield
                    QY = QYs[slot]
                    MG = MGs[slot]
                    Itok = cb[0:64, 0:64]
                    for j, c in enumerate(cs_):
                        o = j * 128
                        mm(bk[0:64, o:o + 128], Ic(c), t2[0:64, c, 64:192], [t1, t2], [LO], start=True, stop=False)
                        mm(bk[0:64, o:o + 128], X[:, j, 0:64], RB[:, j, :], [X, RB], [LO], start=False, stop=True)
                        mm(bk[64:128, o:o + 128], X[:, j, 64:128], RB[:, j, :], [X, RB], [HI], start=True, stop=False)
                        mm(bk[64:128, o:o + 128], Itok, RK[:, j, :], [cb, RK], [HI], start=False, stop=True)
                    cp("act", QY[:, :, :], lo4, [LO], [QY])
                    cp("act", MG[:, :, :], hi4, [HI], [MG])
                    yield
                    order = range(4) if e == 0 else range(3, -1, -1)
                    Ys = Ysb[slot]
                    for j in order:
                        c = cs_[j]
                        hc = hcnt[e][hd]
                        H = Hs[e][hd][hc % 2]
                        Hn = Hs[e][hd][(hc + 1) % 2]
                        hcnt[e][hd] += 1
                        tok = i * 512 + c * 64
                        if (e == 0 and tok == HALF) or (e == 1 and tok + 64 == HALF):
                            ts("dve", H[:, :], H[:, :], linkt[0:64, 0:1], None, ALU.mult, None, [H, linkt], [H])
                        vv = vt[:, c, pp * 64:(pp + 1) * 64]
                        mm(bk[0:64, j * 64:(j + 1) * 64], vv, MG[:, j, 0:64], [vt, MG], [LO], start=True, stop=False)
                        mm(bk[0:64, j * 64:(j + 1) * 64], H[:, :], QY[:, j, 0:64], [H, QY], [LO], start=False, stop=True)
                        mm(bk[64:128, 0:64], MG[:, j, 64:128], vv, [MG, vt], [HI], start=True, stop=False)
                        mm(bk[64:128, 0:64], QY[:, j, 64:128], H[:, :], [QY, H], [HI], start=False, stop=True)
                        cp("dve", Hn[:, :], bk[64:128, 0:64], [HI], [Hn])
                        yield
                    cp("act", Ys[:, :, :], bk[0:64, 0:256].rearrange("p (j n) -> p j n", n=64), [LO], [Ys])
                    t0 = i * 512 + q * 256
                    k.dma(ysd[e, hd * 64:(hd + 1) * 64, t0:t0 + 256], Ys[:, :, :].rearrange("p j n -> p (j n)"),
                          reads=[Ys], writes=[ys_b])
                return gen

            def run_side(factories, nslots, side):
                active = []
                free = list(range(nslots))
                it = iter(factories)
                done = False
                while active or not done:
                    while free and not done:
                        try:
                            f_ = next(it)
                        except StopIteration:
                            done = True
                            break
                        sl_ = free.pop(0)
                        active.append((sl_, f_(sl_)))
                    if side is not None:
                        try:
                            next(side)
                        except StopIteration:
                            side = None
                    for (sl_, g_) in list(active):
                        try:
                            next(g_)
                        except StopIteration:
                            active.remove((sl_, g_))
                            free.append(sl_)
                if side is not None:
                    for _ in side:
                        pass

            seq = []
            for i in range(NT):
                seq.append((i, 0))
                seq.append((NT - 1 - i, 1))
            if stop == "p2c":
                seq = [(i, 0) for i in range(NT)]
            for _ in rwkv_prep(seq[0][0], seq[0][1], 0):
                pass
            for n_, (i, e) in enumerate(seq):
                st = n_ % 2
                qorder = [0, 1] if e == 0 else [1, 0]
                facs = [quad(i, e, hh, pp, q, st) for q in qorder for hh in range(2) for pp in range(2)]
                side = None
                if n_ + 1 < len(seq):
                    side = rwkv_prep(seq[n_ + 1][0], seq[n_ + 1][1], (n_ + 1) % 2)
                run_side(facs, NSL, side)
            if stop in ("p2a", "p2b", "p2c", "p2d", "p2e"):
                break

            if stop == "p2":
                break
            k.barrier()
            k.phase_begin()
            def run_pipelined(factories, nslots):
                active = []
                free = list(range(nslots))
                it = iter(factories)
                done = False
                while active or not done:
                    while free and not done:
                        try:
                            f_ = next(it)
                        except StopIteration:
                            done = True
                            break
                        sl_ = free.pop(0)
                        active.append((sl_, f_(sl_)))
                    for (sl_, g_) in list(active):
                        try:
                            next(g_)
                        except StopIteration:
                            active.remove((sl_, g_))
                            free.append(sl_)

            qTb = k.sb("qTb", [64, 4, WIN], BF16)
            kTb = k.sb("kTb", [64, 4, WIN + 2048], BF16)
            acc = k.sb("acc", [65, 4, WIN], F32)
            Vb = [k.sb(f"Vb{i}", [128, 4, 65], BF16) for i in range(8)]
            Pt = [k.sb(f"Pt{i}", [128, 512], BF16) for i in range(6)]
            rden = [k.sb(f"rden{i}", [64, 512], F32) for i in range(2)]
            ocb = [k.sb(f"ocb{i}", [64, 512], BF16) for i in range(2)]
            for v_ in Vb:
                memset("pool", v_[:, :, :], 1.0, [v_])
            ncnt = [0]
            NW = TC // WIN
            for w in range(NW):
                w0 = w * WIN
                k.dma(qTb[:, :, :], proj[16:18, :, w0:w0 + WIN].rearrange("g (h p) n -> p (g h) n", p=64), reads=[proj_b], writes=[qTb])
                lo = max(0, w0 - 1024)
                hi = min(TC, w0 + WIN + 1024)
                k.dma(kTb[:, :, lo - (w0 - 1024):hi - (w0 - 1024)], proj[18:20, :, lo:hi].rearrange("g (h p) n -> p (g h) n", p=64),
                      reads=[proj_b], writes=[kTb])
                if w0 - 1024 < 0:
                    memset("pool", kTb[:, :, 0:1024], 0.0, [kTb])
                if w0 + WIN + 1024 > TC:
                    memset("pool", kTb[:, :, WIN + 1024:WIN + 2048], 0.0, [kTb])
                combos = []
                for di, d in enumerate((1, 4, 16)):
                    nb = WIN // (128 * d)
                    for r in range(d):
                        for b in range(nb):
                            combos.append((di, d, nb, r, b))

                def v_issue(n, w0=w0, combos=combos):
                    if n >= len(combos):
                        return
                    di, d, nb, r, b = combos[n]
                    for tile in range(2):
                        kpos0 = 128 * b - 64 + 128 * tile
                        rs = PAD + w0 + r + d * kpos0
                        vb = Vb[(n % 4) * 2 + tile]
                        k.dma(vb[:, :, 0:64], vtm[rs:rs + 127 * d + 1:d, 0:256].rearrange("p (h c) -> p h c", c=64),
                              reads=[vtm_b], writes=[vb])

                def b_unit(n, w0=w0, combos=combos):
                    def gen(slot):
                        di, d, nb, r, b = combos[n]
                        qc0 = r + d * 128 * b
                        v_issue(n + 2)
                        pts = []
                        vbs = []
                        sbks = []
                        for tile in range(2):
                            kpos0 = 128 * b - 64 + 128 * tile
                            kc0 = 1024 + r + d * kpos0
                            sbk = banks[3 * slot + tile]
                            for h in range(4):
                                mm(sbk[:, h * 128:(h + 1) * 128], kTb[:, h, kc0:kc0 + 127 * d + 1:d],
                                   qTb[:, h, qc0:qc0 + 127 * d + 1:d], [kTb, qTb], [sbk])
                            sbks.append(sbk)
                            pts.append(Pt[3 * slot + tile])
                            vbs.append(Vb[(n % 4) * 2 + tile])
                        yield
                        for tile in range(2):
                            pt = pts[tile]
                            act(pt[:, :], sbks[tile][:, :], AF.Exp, [sbks[tile]], [pt], scale=0.125)
                            colm = None
                            if tile == 0 and b == 0:
                                if w0 == 0:
                                    colm = (cmask, cmask[:, 0:1])
                                elif w0 == HALF:
                                    colm = (linkt, linkt[:, 1:2])
                            if tile == 1 and b == nb - 1:
                                if w0 + WIN == TC:
                                    colm = (cmask, cmask[:, 1:2])
                                elif w0 + WIN == HALF:
                                    colm = (linkt, linkt[:, 2:3])
                            band = bm[:, tile, :, :].rearrange("p a n -> p (a n)")
                            if colm is None:
                                tt("dve", pt[:, :], pt[:, :], band, ALU.mult, [pt, bm], [pt])
                            else:
                                stt(pt[:, :], pt[:, :], colm[1], band, ALU.mult, ALU.mult, [pt, bm, colm[0]], [pt])
                        yield
                        ob = banks[6 + slot]
                        for h in range(4):
                            for tile in range(2):
                                mm(ob[0:65, h * 128:(h + 1) * 128], vbs[tile][:, h, :], pts[tile][:, h * 128:(h + 1) * 128],
                                   [vbs[tile], pts[tile]], [ob], start=(tile == 0), stop=(tile == 1))
                        yield
                        accv = acc[0:65, :, qc0:qc0 + 127 * d + 1:d]
                        obv = ob[0:65, :].rearrange("p (h n) -> p h n", n=128)
                        if di == 0:
                            cp("act", accv, obv, [ob], [acc])
                        else:
                            tt("dve", accv, accv, obv, ALU.add, [ob, acc], [acc])
                    return gen

                v_issue(0)
                v_issue(1)
                run_pipelined([b_unit(n) for n in range(len(combos))], 2)
                for h in range(4):
                    for ch in range(WIN // 512):
                        n = ncnt[0]
                        ncnt[0] += 1
                        db = banks[2 + 3 * (n % 2)]
                        mm(db[0:64, :], sel65[:, :], acc[0:65, h, ch * 512:(ch + 1) * 512], [sel65, acc], [db])
                        rd = rden[n % 2]
                        recip(rd[:, :], db[0:64, :], [db], [rd])
                        oc_ = ocb[n % 2]
                        tt("pool", oc_[:, :], acc[0:64, h, ch * 512:(ch + 1) * 512], rd[:, :], ALU.mult, [acc, rd], [oc_])
                        k.dma(att[h // 2, (h % 2) * 64:(h % 2) * 64 + 64, w0 + ch * 512:w0 + (ch + 1) * 512], oc_[:, :],
                              reads=[oc_], writes=[att_b])
            if stop == "p3b":
                break
            qTc = [k.sb(f"qTc{i}", [64, 2, 4, 512], BF16) for i in range(2)]
            kTc = [k.sb(f"kTc{i}", [64, 2, 768], BF16) for i in range(2)]
            Vc = [k.sb(f"Vc{i}", [128, 6, 2, 65], BF16) for i in range(2)]
            Osb = [k.sb(f"Osb{i}", [65, 512], F32) for i in range(2)]
            occ = [[k.sb(f"occ{i}{g_}", [64, 4, 512], BF16) for g_ in range(2)] for i in range(2)]
            for v_ in Vc:
                memset("pool", v_[:, :, :, :], 1.0, [v_])

            def c_load(sbi):
                t0 = sbi * 512
                qt = qTc[sbi % 2]
                kt_ = kTc[sbi % 2]
                vc = Vc[sbi % 2]
                for g_ in range(2):
                    k.dma(qt[:, g_, :, :], proj[20:24, g_ * 64:(g_ + 1) * 64, t0:t0 + 512].rearrange("a p n -> p a n"),
                          reads=[proj_b], writes=[qt])
                lo = max(0, t0 - 128)
                hi = min(TC, t0 + 640)
                k.dma(kt_[:, :, lo - (t0 - 128):hi - (t0 - 128)], proj[24, :, lo:hi].rearrange("(g p) n -> p g n", p=64),
                      reads=[proj_b], writes=[kt_])
                if t0 - 128 < 0:
                    memset("pool", kt_[:, :, 0:128], 0.0, [kt_])
                if t0 + 640 > TC:
                    memset("pool", kt_[:, :, 640:768], 0.0, [kt_])
                for kt6 in range(6):
                    rs = PAD + t0 - 128 + kt6 * 128
                    k.dma(vc[:, kt6, :, 0:64], vtm[rs:rs + 128, 256:384].rearrange("p (g c) -> p g c", c=64),
                          reads=[vtm_b], writes=[vc])

            sinkb = [k.sb(f"sinkb{g_}", [64, 512], F32) for g_ in range(2)]
            for g_ in range(2):
                so = (L * 2 + g_) * 512
                mm(banks[0][0:64, :], ones1[:, :], sinkr[0:1, so:so + 512], [ones1, sinkr], [banks[0]])
                cp("act", sinkb[g_][:, :], banks[0][0:64, :], [banks[0]], [sinkb[g_]])
            onesb = k.sb("onesb", [128, 64], BF16)
            memset("pool", onesb[:, :], 1.0, [onesb])
            dtmp = [k.sb(f"dtmp{i}", [64, 512], F32) for i in range(2)]

            def c_unit(sbi, jq, g_):
                def gen(slot):
                    t0 = sbi * 512
                    qt = qTc[sbi % 2]
                    kt_ = kTc[sbi % 2]
                    vc = Vc[sbi % 2]
                    if jq == 1 and g_ == 0 and sbi + 1 < NT:
                        c_load(sbi + 1)
                    qtok = t0 + 128 * jq
                    tiles = []
                    for rel in range(3):
                        ktile = jq + rel
                        ktok = t0 - 128 + 128 * ktile
                        if ktok < 0 or ktok >= TC:
                            continue
                        tiles.append((rel, ktile, ktok))
                    sb_ = [banks[4 * slot], banks[4 * slot + 1]]
                    ob = banks[4 * slot + 2]
                    db = banks[4 * slot + 3]

                    def s_mm(ti):
                        rel, ktile, ktok = tiles[ti]
                        sbk = sb_[ti % 2]
                        mm(sbk[:, :], kt_[:, g_, ktile * 128:(ktile + 1) * 128],
                           qt[:, g_, :, jq * 128:(jq + 1) * 128], [kt_, qt], [sbk])

                    def p_ev(ti):
                        rel, ktile, ktok = tiles[ti]
                        sbk = sb_[ti % 2]
                        pt = Pt[3 * slot + ti]
                        act(pt[:, :], sbk[:, :], AF.Exp, [sbk], [pt], scale=0.125)
                        cross = (ktok < HALF) != (qtok < HALF)
                        if rel != 1:
                            band = bm[:, 0 if rel == 0 else 1, :, :].rearrange("p a n -> p (a n)")
                            if cross:
                                stt(pt[:, :], pt[:, :], linkt[:, 0:1], band, ALU.mult, ALU.mult, [pt, bm, linkt], [pt])
                            else:
                                tt("dve", pt[:, :], pt[:, :], band, ALU.mult, [pt, bm], [pt])
                    for ti in range(min(2, len(tiles))):
                        s_mm(ti)
                    yield
                    for ti in range(min(2, len(tiles))):
                        p_ev(ti)
                    yield
                    if len(tiles) > 2:
                        s_mm(2)
                        yield
                        p_ev(2)
                        yield
                    for ti, (rel, ktile, ktok) in enumerate(tiles):
                        pt = Pt[3 * slot + ti]
                        mm(ob[0:64, :], vc[:, ktile, g_, 0:64], pt[:, :], [vc, pt], [ob], start=(ti == 0), stop=(ti == len(tiles) - 1))
                    for ti, (rel, ktile, ktok) in enumerate(tiles):
                        pt = Pt[3 * slot + ti]
                        mm(db[0:64, :], onesb[:, :], pt[:, :], [onesb, pt], [db], start=(ti == 0), stop=(ti == len(tiles) - 1))
                    yield
                    osb = Osb[slot]
                    cp("act", osb[0:64, :], ob[0:64, :], [ob], [osb])
                    dt_ = dtmp[slot]
                    tt("dve", dt_[:, :], db[0:64, :], sinkb[g_][:, :], ALU.add, [db, sinkb[g_]], [dt_])
                    yield
                    rd = rden[slot]
                    recip(rd[:, :], dt_[:, :], [dt_], [rd])
                    yield
                    oc_ = occ[sbi % 2][g_]
                    tt("pool", oc_[:, :, jq * 128:(jq + 1) * 128], osb[0:64, :].rearrange("p (a n) -> p a n", n=128),
                       rd[:, :].rearrange("p (a n) -> p a n", n=128), ALU.mult, [osb, rd], [oc_])
                    if jq == 3:
                        k.dma(att[2:6, g_ * 64:(g_ + 1) * 64, t0:t0 + 512].rearrange("a p n -> p a n"), oc_[:, :, :],
                              reads=[oc_], writes=[att_b])
                return gen

            c_load(0)
            run_pipelined([c_unit(sbi, jq, g_) for sbi in range(NT) for jq in range(4) for g_ in range(2)], 2)

            k.barrier()
            k.phase_begin()
            wob = k.sb("wob", [128, 8, D], BF16)
            for c in range(8):
                k.dma(wob[:, c, :], wout_in[L, c * 128:(c + 1) * 128, :], writes=[wob], eng="pool")
            gt = [k.sb(f"gt{i}", [128, 8, 512], BF16) for i in range(2)]
            att_t = [k.sb(f"att{i}", [128, 6, 512], BF16) for i in range(2)]
            yv = [k.sb(f"yv{i}", [128, 2, 2, 512], F32) for i in range(2)]
            bv = [k.sb(f"bv{i}", [128, 2, 2, 512], F32) for i in range(2)]
            xt4 = [k.sb(f"xt4_{i}", [128, 4, D], F32) for i in range(2)]
            mix = [k.sb(f"mix{i}", [128, 8, 512], BF16) for i in range(2)]
            tYs = [[k.sb(f"tY{sl}{hh}", [128, 512], F32) for hh in range(2)] for sl in range(2)]
            tCs = [[k.sb(f"tC{sl}{hh}", [128, 512], F32) for hh in range(2)] for sl in range(2)]
            tRs = [[k.sb(f"tR{sl}{hh}", [128, 512], F32) for hh in range(2)] for sl in range(2)]
            tG = [k.sb(f"tG{i}", [128, 512], BF16) for i in range(4)]
            ss4 = k.sb("ss4", [128, 8], F32)
            junk4 = k.sb("junk4", [128, D], BF16)
            xo = [k.sb(f"xo{i}", [128, D], F32) for i in range(2)]

            def p4_load(i):
                t0 = i * 512
                k.dma(gt[i % 2][:, :, :], proj[8:16, :, t0:t0 + 512].rearrange("g p n -> p g n"), reads=[proj_b], writes=[gt[i % 2]])
                k.dma(att_t[i % 2][:, :, :], att[:, :, t0:t0 + 512].rearrange("g p n -> p g n"), reads=[att_b], writes=[att_t[i % 2]])
                for e in range(2):
                    k.dma(yv[i % 2][:, e, :, :], ysd[e, :, t0:t0 + 512].rearrange("(h p) n -> p h n", p=128),
                          reads=[ys_b], writes=[yv[i % 2]])
                    k.dma(bv[i % 2][:, e, :, :], bsd[e, :, t0:t0 + 512].rearrange("(h p) n -> p h n", p=128),
                          reads=[bs_b], writes=[bv[i % 2]])
                for m in range(4):
                    k.dma(xt4[i % 2][:, m, :], x_src[t0 + m * 128:t0 + (m + 1) * 128, :], reads=[x1_b] if L else [],
                          writes=[xt4[i % 2]])

            opc = [0]

            def p4_unit(i):
                def gen(slot):
                    t0 = i * 512
                    g_ = gt[i % 2]
                    a_ = att_t[i % 2]
                    y_ = yv[i % 2]
                    b_ = bv[i % 2]
                    x_ = xt4[i % 2]
                    mx = mix[i % 2]
                    tY = tYs[slot]
                    tC = tCs[slot]
                    tR = tRs[slot]
                    bks = [banks[slot * 2], banks[slot * 2 + 1]]
                    for hh in range(2):
                        tt("pool", tY[hh][:, :], y_[:, 0, hh, :], y_[:, 1, hh, :], ALU.add, [y_], [tY[hh]])
                        mm(bks[hh][:, :], BAVG, tY[hh][:, :], [cst, tY[hh]], [bks[hh]])
                    yield
                    for hh in range(2):
                        tt("dve", tC[hh][:, :], tY[hh][:, :], bks[hh][:, :], ALU.subtract, [tY[hh], bks[hh]], [tC[hh]])
                        act(tY[hh][:, :], tC[hh][:, :], AF.Square, [tC[hh]], [tY[hh]])
                        mm(bks[hh][:, :], BAVG, tY[hh][:, :], [cst, tY[hh]], [bks[hh]])
                    yield
                    for hh in range(2):
                        ts("dve", tR[hh][:, :], bks[hh][:, :], 64e-5, None, ALU.add, None, [bks[hh]], [tR[hh]])
                        act(tR[hh][:, :], tR[hh][:, :], AF.Sqrt, [tR[hh]], [tR[hh]])
                    yield
                    for hh in range(2):
                        recip(tR[hh][:, :], tR[hh][:, :], [tR[hh]], [tR[hh]])
                        tt("pool", tC[hh][:, :], tC[hh][:, :], tR[hh][:, :], ALU.mult, [tC[hh], tR[hh]], [tC[hh]])
                        tt("pool", tR[hh][:, :], b_[:, 0, hh, :], b_[:, 1, hh, :], ALU.add, [b_], [tR[hh]])
                    yield
                    for hh in range(2):
                        ts("dve", tC[hh][:, :], tC[hh][:, :], prm[:, 60 + L * 2 + hh:60 + L * 2 + hh + 1],
                           lnb[:, 4 + L * 2 + hh:4 + L * 2 + hh + 1], ALU.mult, ALU.add, [tC[hh], prm, lnb], [tC[hh]])
                        tt("pool", tC[hh][:, :], tC[hh][:, :], tR[hh][:, :], ALU.add, [tC[hh], tR[hh]], [tC[hh]])
                        tg = tG[hh]
                        act(tg[:, :], g_[:, hh, :], AF.Silu, [g_], [tg])
                        tt("dve", mx[:, hh, :], tC[hh][:, :], tg[:, :], ALU.mult, [tC[hh], tg], [mx])
                    yield
                    for j in range(6):
                        tg = tG[j % 4]
                        act(tg[:, :], g_[:, 2 + j, :], AF.Silu, [g_], [tg])
                        tt("pool" if j % 2 else "dve", mx[:, 2 + j, :], a_[:, j, :], tg[:, :], ALU.mult, [a_, tg], [mx])
                        if j % 2:
                            yield
                    for m in range(4):
                        for hf in range(2):
                            bk = banks[4 + opc[0] % 4]
                            opc[0] += 1
                            for c in range(8):
                                mm(bk[:, :], mx[:, c, m * 128:(m + 1) * 128], wob[:, c, hf * 512:(hf + 1) * 512], [mx, wob], [bk],
                                   start=(c == 0), stop=(c == 7))
                            tt("dve", x_[:, m, hf * 512:(hf + 1) * 512], bk[:, :], x_[:, m, hf * 512:(hf + 1) * 512], ALU.add,
                               [bk, x_], [x_])
                        r0 = t0 + m * 128
                        if not last:
                            k.dma(x1d[r0:r0 + 128, :], x_[:, m, :], reads=[x_], writes=[x1_b])
                        else:
                            n = i * 4 + m
                            sc = ss4[:, (n % 2) * 4:(n % 2) * 4 + 1]
                            s2 = ss4[:, (n % 2) * 4 + 1:(n % 2) * 4 + 2]
                            s3 = ss4[:, (n % 2) * 4 + 2:(n % 2) * 4 + 3]
                            act(junk4[:, :], x_[:, m, :], AF.Square, [x_], [junk4, ss4], accum=sc)
                            ts("dve", s2, sc, 1.0 / D, 1e-5, ALU.mult, ALU.add, [ss4], [ss4])
                            act(s2, s2, AF.Sqrt, [ss4], [ss4])
                            recip(s3, s2, [ss4], [ss4])
                            o_ = xo[n % 2]
                            stt(o_[:, :], x_[:, m, :], s3, fgt[:, :], ALU.mult, ALU.mult, [x_, ss4, fgt], [o_])
                            k.dma(y_out[r0:r0 + 128, :], o_[:, :], reads=[o_], writes=[yout_b])
                            outs.append(o_)
                        if m < 3:
                            yield
                    if i + 2 < NT:
                        p4_load(i + 2)
                return gen

            p4_load(0)
            if NT > 1:
                p4_load(1)
            run_pipelined([p4_unit(i) for i in range(NT)], 2)
            if stop == "p4":
                break

        k.barrier()
        k.finalize(final_waits=list({id(o): o for o in outs}.values()))
    return nc


def _col_layout():
    rot = lambda base: [base + (j + 32) % 64 for j in range(64)]
    pl = lambda base: [base + j for j in range(64)]
    cols = list(range(1024))
    R = 1024
    cols += [R + j for j in range(256)]
    cols += [R + 1024 + j for j in range(256)]
    permc = []
    for a in range(4):
        permc += pl(a * 64) + pl((4 + a) * 64)
    cols += [R + 2048 + j for j in permc]
    for base in (R + 256, R + 512):
        for gi in range(2):
            cols += pl(base + gi * 128) + pl(base + gi * 128 + 64)
            cols += rot(base + gi * 128) + rot(base + gi * 128 + 64)
    for a in range(4):
        base = R + 1280
        cols += pl(base + a * 64) + pl(base + (4 + a) * 64)
        cols += rot(base + a * 64) + rot(base + (4 + a) * 64)
    base = R + 1792
    cols += pl(base) + pl(base + 64)
    cols += rot(base) + rot(base + 64)
    cols += [R + 768 + j for j in range(256)]
    cols += [R + 1920 + j for j in range(128)]
    assert len(cols) == NCOL
    return np.array(cols), np.array(permc)


def _constants():
    c = np.zeros((128, CW), np.float32)
    p = np.arange(128)[:, None]
    j = np.arange(128)[None, :]
    c[:, 0:128] = (p == j)
    c[:, 128:192] = (np.arange(64)[None, :] == (p % 64))
    c[:, 192:320] = (p // 64 == j // 64)
    c[:, 320:448] = (p // 64 == j // 64) / 64.0
    c[:, 448:576] = (p >= j)
    c[:, 576:704] = (p <= j)
    r = np.arange(64)[:, None]
    q = np.arange(64)[None, :]
    c[0:64, 704:768] = (q < r)
    c[0:64, 768:832] = (r < q)
    c[0:64, 832:896] = (r <= q)
    c[0:64, 896:960] = (q > r)
    c[0:64, 960:1024] = (r > q)
    c[0:64, 1024:1088] = (r >= q)
    col = np.arange(512)
    c[:, 1152:1664] = (col % 64 != 0)[None, :]
    c[:, 1664:2176] = (col % 64 != 63)[None, :]
    return c


def _prep_shared(inp):
    cols, permc = _col_layout()
    f = lambda a: np.ascontiguousarray(np.asarray(a, dtype=np.float32))
    win = f(np.asarray(inp["w_in"])[:, :, cols])
    rows = np.concatenate([np.arange(512), 512 + permc])
    wout = f(np.asarray(inp["w_out"])[:, rows, :])
    prm = np.zeros((128, 64), np.float32)
    lnb = np.zeros((128, 4), np.float32)
    for l in range(NLAYER):
        prm[:, l * 8:(l + 1) * 8] = np.asarray(inp["norm_g"])[l].reshape(8, 128).T
        prm[:, 16 + l * 8:16 + (l + 1) * 8] = np.asarray(inp["tshift_mu"])[l].reshape(8, 128).T
        for e in range(2):
            prm[:, 32 + l * 4 + e * 2:32 + l * 4 + e * 2 + 2] = np.asarray(inp["rwkv_w0"])[l, e].reshape(2, 128).T
            prm[:, 40 + l * 4 + e * 2:40 + l * 4 + e * 2 + 2] = np.asarray(inp["rwkv_a0"])[l, e].reshape(2, 128).T
        prm[:, 48 + l * 2:48 + l * 2 + 2] = np.asarray(inp["rwkv_k_k"])[l].reshape(2, 128).T
        prm[:, 52 + l * 2:52 + l * 2 + 2] = np.asarray(inp["rwkv_k_a"])[l].reshape(2, 128).T
        prm[:, 56 + l * 2:56 + l * 2 + 2] = np.asarray(inp["rwkv_r_k"])[l].reshape(2, 128).T
        prm[:, 60 + l * 2:60 + l * 2 + 2] = np.asarray(inp["ln_x_w"])[l].reshape(2, 128).T
        lnb[:, l * 2:l * 2 + 2] = np.asarray(inp["ln_x_b"])[l].reshape(2, 128).T
    w2s = f(np.asarray(inp["rwkv_w2"]).transpose(2, 0, 1, 3))
    a2s = f(np.asarray(inp["rwkv_a2"]).transpose(2, 0, 1, 3))
    sink = np.asarray(inp["attn_sink"], dtype=np.float32)
    sinkrow = f(np.repeat(sink.reshape(NLAYER, 2, 4, 1), 128, axis=3).reshape(1, -1))
    fg = f(np.tile(np.asarray(inp["final_g"], dtype=np.float32)[None, :], (128, 1)))
    return dict(win=win, wout=wout, prm=prm, lnb=lnb, w2s=w2s, a2s=a2s, sinkrow=sinkrow, fg=fg, cst=_constants())


def _rope_tables(pos):
    inv = (10000.0 ** (-np.arange(0, 64, 2, dtype=np.float32) / np.float32(64))).astype(np.float32)
    ang = (pos.astype(np.float32)[:, None] * inv[None, :]).astype(np.float32)
    cos = np.cos(ang).astype(np.float32).T
    sin = np.sin(ang).astype(np.float32).T
    p = np.arange(128)
    cs = cos[p % 32, :]
    sn = sin[p % 32, :] * np.where((p % 64) < 32, -1.0, 1.0)[:, None].astype(np.float32)
    return np.ascontiguousarray(cs, dtype=np.float32), np.ascontiguousarray(sn, dtype=np.float32)


_NC_CACHE = {}


def _run(core_x, core_link, inp, TC, debug=False, stop=None):
    sh = _prep_shared(inp)
    key = (TC, debug, stop)
    if key not in _NC_CACHE:
        _NC_CACHE[key] = build_nc(TC, debug, stop)
    nc = _NC_CACHE[key]
    HALF = TC // 2
    maps = []
    tabs = {}
    for x, lk in zip(core_x, core_link):
        if lk not in tabs:
            t = np.arange(TC)
            pos = t if lk else (t % HALF)
            tabs[lk] = _rope_tables(pos)
        cs, sn = tabs[lk]
        m = dict(sh)
        m["x"] = np.ascontiguousarray(x, dtype=np.float32)
        m["link"] = np.full((128, 1), float(lk), np.float32)
        m["cs"] = cs
        m["sn"] = sn
        maps.append(m)
    res = run_bass_kernel_spmd(nc, maps, core_ids=list(range(len(maps))))
    return res.results


def kernel(**inputs):
    xp = np.asarray(inputs["x_prompt"], dtype=np.float32)
    xs = np.asarray(inputs["x_sample"], dtype=np.float32)
    TC = 16384
    core_x = [xp[0:2].reshape(TC, D), xp[2:4].reshape(TC, D), xs[0], xs[1]]
    core_link = [0, 0, 1, 1]
    for _ in range(4):
        core_x.append(np.zeros((TC, D), np.float32))
        core_link.append(0)
    res = _run(core_x, core_link, inputs, TC)
    yp = np.stack([res[0]["y"], res[1]["y"]]).reshape(4, 8192, D).astype(np.float32)
    ys = np.stack([res[2]["y"], res[3]["y"]]).astype(np.float32)
    return (yp, ys)
```

```python
import os
import numpy as np
from contextlib import ExitStack
import concourse.bass as bass
import concourse.mybir as mybir
from concourse.bass_utils import run_bass_kernel_spmd

F32 = mybir.dt.float32
BF16 = mybir.dt.bfloat16
AF = mybir.ActivationFunctionType
ALU = mybir.AluOpType

ENGS = ("pe", "act", "dve", "pool", "sp")
EPOCH = 12000
D = 1024
NCOL = 34 * 128 + 384
C0 = float(np.exp(-0.5))
PAD = 1024
NLAYER = 2
CW = 2176


class Buf:
    __slots__ = ("name", "w", "r")

    def __init__(self, name=""):
        self.name = name
        self.w = None
        self.r = []


class DSem:
    def __init__(self, k):
        self.k = k
        self.sems = []
        self.finals = {}
        self.count = 0
        self.new_epoch()
        k.dsems.append(self)

    def new_epoch(self):
        if self.sems:
            self.finals[id(self.sems[-1])] = self.count
        self.sems.append(self.k.new_sem())
        self.count = 0

    def value_for(self, h):
        if self.sems[-1] is h:
            return self.count
        return self.finals[id(h)]


class Op:
    __slots__ = ("eng", "fn", "deps", "signal", "dma", "dsem", "dsem_h", "cnt", "ep")

    def __init__(self, eng, fn):
        self.eng = eng
        self.fn = fn
        self.deps = []
        self.signal = False
        self.dma = False


class T:
    def __init__(self, k, t, name):
        self.k = k
        self.t = t
        self.b = Buf(name)
        self._ds = None

    def __getitem__(self, key):
        return self.t[key]

    @property
    def ds(self):
        if self._ds is None:
            k = self.k
            if k.sb_mark is not None:
                self._ds = k.free_ds.pop() if k.free_ds else DSem(k)
                k.phase_ds.append(self._ds)
            else:
                self._ds = DSem(k)
        return self._ds


class K:
    def __init__(self, nc, es):
        self.nc = nc
        self.es = es
        self.ops = {e: [] for e in ENGS}
        self.nsem = 0
        self.dsems = []
        self.nt = 0
        self.sb_lo = 16512
        self.sb_hi = 229344
        self.sb_cur = self.sb_lo
        self.sb_mark = None
        self.free_ds = []
        self.phase_ds = []

    def new_sem(self):
        self.nsem += 1
        return self.es.enter_context(self.nc.semaphore(f"s{self.nsem}"))

    def sb(self, name, shape, dtype):
        nb = int(np.prod(shape[1:])) * (4 if dtype == F32 else 2)
        nb = (nb + 63) // 64 * 64
        off = self.sb_cur
        self.sb_cur += nb
        assert self.sb_cur <= self.sb_hi, f"SBUF overflow at {name}: {self.sb_cur}"
        self.nt += 1
        t = self.nc.alloc_sbuf_tensor_at(f"{name}_{self.nt}", list(shape), dtype, offset=off)
        return T(self, t, name)

    def phase_begin(self):
        if self.sb_mark is None:
            self.sb_mark = self.sb_cur
        self.sb_cur = self.sb_mark
        self.free_ds.extend(self.phase_ds)
        self.phase_ds = []

    def ps(self, name, shape, dtype=F32):
        t = self.es.enter_context(self.nc.psum_tensor(name, list(shape), dtype))
        return T(self, t, name)

    @staticmethod
    def _b(x):
        return x.b if isinstance(x, T) else x

    def _dep(self, op, d):
        if d.dma:
            op.deps.append(("dma", d.dsem_h, d.dsem.value_for(d.dsem_h)))
        else:
            op.deps.append(d)

    def _track(self, op, reads, writes):
        eng = op.eng
        for x in reads:
            b = self._b(x)
            if b.w is not None and not (b.w.eng == eng and eng == "pe" and not b.w.dma):
                self._dep(op, b.w)
        for x in writes:
            b = self._b(x)
            for r in b.r:
                if r.dma or r.eng != eng:
                    self._dep(op, r)
            if b.w is not None and (b.w.dma or b.w.eng != eng):
                self._dep(op, b.w)
        for x in reads:
            self._b(x).r.append(op)
        for x in writes:
            b = self._b(x)
            b.w = op
            b.r = []

    def op(self, eng, fn, reads=(), writes=()):
        o = Op(eng, fn)
        self._track(o, reads, writes)
        self.ops[eng].append(o)
        return o

    def dma(self, out, in_, reads=(), writes=(), sem=None, eng="sp", **kw):
        if sem is None:
            for x in list(writes) + list(reads):
                if isinstance(x, T):
                    sem = x
                    break
        ds = sem.ds
        if ds.count + 16 > EPOCH:
            ds.new_epoch()
        o = Op(eng, lambda e: e.dma_start(out=out, in_=in_, **kw))
        o.dma = True
        self._track(o, reads, writes)
        ds.count += 16
        o.dsem = ds
        o.dsem_h = ds.sems[-1]
        self.ops[eng].append(o)
        return o

    def barrier(self):
        lasts = {}
        for e in ENGS:
            for o in reversed(self.ops[e]):
                if not o.dma:
                    lasts[e] = o
                    break
        dd = []
        for ds in self.dsems:
            if ds.count:
                dd.append(("dma", ds.sems[-1], ds.count))
        for e in ENGS:
            o = Op(e, lambda eng: eng.nop())
            for e2, lo in lasts.items():
                if e2 != e:
                    o.deps.append(lo)
            o.deps.extend(dd)
            self.ops[e].append(o)

    def finalize(self, final_waits=()):
        for e in ENGS:
            for o in self.ops[e]:
                for d in o.deps:
                    if not isinstance(d, tuple):
                        d.signal = True
        for e in ENGS:
            cnt = 0
            sems = [self.new_sem()]
            for o in self.ops[e]:
                if o.dma:
                    continue
                if o.signal:
                    if cnt + 1 > EPOCH:
                        sems.append(self.new_sem())
                        cnt = 0
                    cnt += 1
                    o.cnt = cnt
                    o.ep = sems[-1]
        fw = []
        for t in final_waits:
            ds = t.ds
            for h in ds.sems:
                v = ds.value_for(h)
                if v:
                    fw.append((h, v))
        k = self
        with self.nc.Block() as block:
            def emit(e_name):
                def body(e):
                    seen = {}
                    for o in k.ops[e_name]:
                        need = {}
                        for d in o.deps:
                            if isinstance(d, tuple):
                                key, h, v = id(d[1]), d[1], d[2]
                            else:
                                key, h, v = id(d.ep), d.ep, d.cnt
                            if seen.get(key, 0) >= v:
                                continue
                            if key not in need or need[key][1] < v:
                                need[key] = (h, v)
                        for key, (h, v) in need.items():
                            e.wait_ge(h, v)
                            seen[key] = v
                        ins = o.fn(e)
                        if o.dma:
                            ins.then_inc(o.dsem_h, 16)
                        elif o.signal:
                            ins.then_inc(o.ep, 1)
                    if e_name == "sp":
                        for (h, v) in fw:
                            e.wait_ge(h, v)
                return body
            block.sync(emit("sp"))
            block.scalar(emit("act"))
            block.vector(emit("dve"))
            block.gpsimd(emit("pool"))
            block.tensor(emit("pe"))


def rev(ap):
    pat = [list(p) for p in ap.ap]
    off = ap.offset + (pat[-1][1] - 1) * pat[-1][0]
    pat[-1][0] = -pat[-1][0]
    return bass.AP(ap.tensor, off, pat)


def build_nc(TC, debug=False, stop=None):
    HALF = TC // 2
    NT = TC // 512
    WIN = min(2048, HALF)
    assert HALF % WIN == 0 and WIN % 2048 == 0 or WIN == HALF
    nc = bass.Bass("TRN2", target_bir_lowering=False)
    dk = "ExternalOutput" if debug else "Internal"

    def din(name, shape, dt=F32):
        return nc.dram_tensor(name, list(shape), dt, kind="ExternalInput").ap()

    x_in = din("x", [TC, D])
    link_in = din("link", [128, 1])
    cs_in = din("cs", [128, TC])
    sn_in = din("sn", [128, TC])
    win_in = din("win", [NLAYER, D, NCOL])
    wout_in = din("wout", [NLAYER, D, D])
    prm_in = din("prm", [128, 64])
    w2_in = din("w2s", [64, NLAYER, 2, 256])
    a2_in = din("a2s", [64, NLAYER, 2, 256])
    sink_in = din("sinkrow", [1, NLAYER * 2 * 512])
    fg_in = din("fg", [128, D])
    cst_in = din("cst", [128, CW])
    y_out = nc.dram_tensor("y", [TC, D], F32, kind="ExternalOutput").ap()

    proj = nc.dram_tensor("proj", [25, 128, TC], BF16, kind=dk).ap()
    vtm = nc.dram_tensor("vtm", [TC + 2 * PAD, 384], BF16, kind=dk).ap()
    ysd = nc.dram_tensor("ysd", [2, 256, TC], F32, kind=dk).ap()
    bsd = nc.dram_tensor("bsd", [2, 256, TC], F32, kind=dk).ap()
    att = nc.dram_tensor("att", [6, 128, TC], BF16, kind=dk).ap()
    x1d = nc.dram_tensor("x1d", [TC, D], F32, kind=dk).ap()

    with ExitStack() as es:
        k = K(nc, es)
        banks = [k.ps(f"bk{i}", [128, 512]) for i in range(8)]

        def act(out, in_, func, r, w, bias=None, scale=None, accum=None, eng="act"):
            kw = {}
            if bias is not None:
                kw["bias"] = bias
            if scale is not None:
                kw["scale"] = scale
            if accum is not None:
                kw["accum_out"] = accum
            return k.op("act", lambda e: e.activation(out=out, in_=in_, func=func, **kw), r, w)

        def tt(eng, out, in0, in1, op, r, w):
            return k.op(eng, lambda e: e.tensor_tensor(out=out, in0=in0, in1=in1, op=op), r, w)

        def ts(eng, out, in0, s1, s2, op0, op1, r, w):
            if op1 is None:
                return k.op(eng, lambda e: e.tensor_scalar(out=out, in0=in0, scalar1=s1, scalar2=None, op0=op0), r, w)
            return k.op(eng, lambda e: e.tensor_scalar(out=out, in0=in0, scalar1=s1, scalar2=s2, op0=op0, op1=op1), r, w)

        def stt(out, in0, scalar, in1, op0, op1, r, w):
            return k.op("dve", lambda e: e.scalar_tensor_tensor(out=out, in0=in0, scalar=scalar, in1=in1, op0=op0, op1=op1), r, w)

        def cp(eng, out, in_, r, w):
            if eng == "act":
                return k.op("act", lambda e: e.activation(out=out, in_=in_, func=AF.Copy), r, w)
            return k.op(eng, lambda e: e.tensor_copy(out=out, in_=in_), r, w)

        def mm(out, lhsT, rhs, r, w, start=True, stop=True):
            return k.op("pe", lambda e: e.matmul(out, lhsT=lhsT, rhs=rhs, start=start, stop=stop), r, w)

        def recip(out, in_, r, w):
            return k.op("dve", lambda e: e.reciprocal(out=out, in_=in_), r, w)

        def memset(eng, ap, val, w):
            return k.op(eng, lambda e: e.memset(ap, val), (), w)

        prm = k.sb("prm", [128, 64], F32)
        prm2 = k.sb("prm2", [128, 32], F32)
        cst = k.sb("cst", [128, CW], F32)
        cb = k.sb("cstb", [128, 1088], BF16)
        w2s = k.sb("w2s", [64, NLAYER, 2, 256], BF16)
        a2s = k.sb("a2s", [64, NLAYER, 2, 256], BF16)
        linkt = k.sb("link", [128, 4], F32)
        sinkr = k.sb("sinkr", [1, NLAYER * 2 * 512], F32)
        fgt = k.sb("fg", [128, D], F32)
        zt = k.sb("zeros", [128, 384], BF16)

        k.dma(prm[:, :], prm_in, writes=[prm])
        k.dma(cst[:, :], cst_in, writes=[cst])
        k.dma(cb[:, 0:544], cst_in[:, 0:544], writes=[cb], eng="pool")
        k.dma(cb[:, 544:1088], cst_in[:, 544:1088], writes=[cb], eng="pool")
        k.dma(w2s[:, :, :, :], w2_in, writes=[w2s], eng="pool")
        k.dma(a2s[:, :, :, :], a2_in, writes=[a2s], eng="pool")
        k.dma(linkt[:, 0:1], link_in, writes=[linkt])
        k.dma(sinkr[:, :], sink_in, writes=[sinkr])
        k.dma(fgt[:, :], fg_in, writes=[fgt])
        memset("pool", zt[:, :], 0.0, [zt])
        act(sinkr[:, :], sinkr[:, :], AF.Exp, [sinkr], [sinkr])
        vtm_b = Buf("vtm")
        proj_b = Buf("proj")
        ys_b = Buf("ys")
        bs_b = Buf("bs")
        att_b = Buf("att")
        x1_b = Buf("x1")
        yout_b = Buf("yout")
        outs = []
        for side in range(2):
            base = 0 if side == 0 else PAD + TC
            for j in range(PAD // 128):
                k.dma(vtm[base + j * 128: base + (j + 1) * 128, :], zt[:, :], reads=[zt], writes=[vtm_b])
        IDN = lambda t, p0, p1, c0, c1: t[p0:p1, c0:c1]
        ts("dve", prm2[:, 0:16], prm[:, 16:32], -1.0, 1.0, ALU.mult, ALU.add, [prm], [prm2])
        ts("dve", prm2[:, 16:32], prm[:, 16:32], 0.5, None, ALU.mult, None, [prm], [prm2])
        lnb = k.sb("lnb", [128, 8], F32)
        ts("dve", lnb[:, 0:4], prm[:, 52:56], -1.0, 1.0, ALU.mult, ALU.add, [prm], [lnb])
        cp("dve", linkt[:, 1:2], linkt[:, 0:1], [linkt], [linkt])
        cp("dve", linkt[:, 2:3], linkt[:, 0:1], [linkt], [linkt])
        memset("dve", linkt[64:128, 1:2], 1.0, [linkt])
        memset("dve", linkt[0:64, 2:3], 1.0, [linkt])
        cmask = k.sb("cmask", [128, 2], F32)
        memset("pool", cmask[:, :], 1.0, [cmask])
        memset("pool", cmask[0:64, 0:1], 0.0, [cmask])
        memset("pool", cmask[64:128, 1:2], 0.0, [cmask])
        lnb_in = din("lnb", [128, 4])
        k.dma(lnb[:, 4:8], lnb_in, writes=[lnb])

        I128b = cb[:, 0:128]
        BONES = cst[:, 192:320]
        BAVG = cst[:, 320:448]

        qm = k.sb("qmask", [64, 2, 3, 4, 128], BF16)
        for e in range(2):
            b0 = 704 + e * 192
            for j in range(4):
                memset("pool", qm[:, e, 0, j, 0:64], 1.0, [qm])
                cp("pool", qm[:, e, 0, j, 64:128], cst[0:64, b0:b0 + 64], [cst], [qm])
                cp("pool", qm[:, e, 1, j, 0:64], cst[0:64, b0:b0 + 64], [cst], [qm])
                cp("pool", qm[:, e, 1, j, 64:128], cst[0:64, b0 + 64:b0 + 128], [cst], [qm])
                cp("pool", qm[:, e, 2, j, 0:64], cst[0:64, b0 + 128:b0 + 192], [cst], [qm])
                memset("pool", qm[:, e, 2, j, 64:128], 1.0, [qm])
        bm = k.sb("bmask", [128, 2, 4, 128], BF16)
        for j in range(4):
            cp("pool", bm[:, 0, j, :], cst[:, 448:576], [cst], [bm])
            cp("pool", bm[:, 1, j, :], cst[:, 576:704], [cst], [bm])
        sel65 = k.sb("sel65", [65, 64], F32)
        memset("pool", sel65[:, :], 0.0, [sel65])
        memset("pool", sel65[64:65, :], 1.0, [sel65])
        ones1 = k.sb("ones1", [1, 64], F32)
        memset("pool", ones1[:, :], 1.0, [ones1])

        k.phase_begin()

        for L in range(NLAYER):
            x_src = x_in if L == 0 else x1d
            last = (L == NLAYER - 1)
            k.barrier()
            k.phase_begin()
            WCH = [(0, 1280), (1280, 2560), (2560, 3840), (3840, NCOL)]
            wbufs = [k.sb(f"wbuf{j}", [128, 8, b_ - a_], BF16) for j, (a_, b_) in enumerate(WCH)]
            for j, (a_, b_) in enumerate(WCH):
                for c in range(8):
                    k.dma(wbufs[j][:, c, :], win_in[L, c * 128:(c + 1) * 128, a_:b_], writes=[wbufs[j]], eng="pool")

            def wcols(c, a_, b_):
                for j, (lo_, hi_) in enumerate(WCH):
                    if lo_ <= a_ and b_ <= hi_:
                        return wbufs[j], wbufs[j][:, c, a_ - lo_:b_ - lo_]
                raise AssertionError((a_, b_))

            xt = [k.sb(f"xt{i}", [128, D], F32) for i in range(4)]
            xs = [k.sb(f"xs{i}", [128, D], F32) for i in range(4)]
            junk = k.sb("junk", [128, D], BF16)
            ss = k.sb("ss", [128, 16], F32)
            hT = [k.sb(f"hT{i}", [128, 8, 512], BF16) for i in range(2)]
            gb = [k.sb(f"gb{i}", [128, 512], BF16) for i in range(6)]
            rt1 = [k.sb(f"rt1_{i}", [128, 512], F32) for i in range(2)]
            rt2 = [k.sb(f"rt2_{i}", [128, 512], F32) for i in range(2)]
            cst_t = [k.sb(f"cos{i}", [128, 512], F32) for i in range(2)]
            snt_t = [k.sb(f"sin{i}", [128, 512], F32) for i in range(2)]
            vtb = [k.sb(f"vtb{i}", [128, 384], BF16) for i in range(2)]
            xcnt = [0]

            def p1_norm(i):
                h = hT[i % 2]
                xas = []
                for m in range(4):
                    xa = xt[m]
                    xb = xs[m]
                    r0 = i * 512 + m * 128
                    k.dma(xa[:, :], x_src[r0:r0 + 128, :], reads=[x1_b] if L else [], writes=[xa])
                    sc = ss[:, m * 4:m * 4 + 1]
                    act(junk[:, :], xa[:, :], AF.Square, [xa], [junk, ss], accum=sc)
                    s2 = ss[:, m * 4 + 1:m * 4 + 2]
                    ts("dve", s2, sc, 1.0 / D, 1e-5, ALU.mult, ALU.add, [ss], [ss])
                    act(s2, s2, AF.Sqrt, [ss], [ss])
                    s3 = ss[:, m * 4 + 2:m * 4 + 3]
                    recip(s3, s2, [ss], [ss])
                    ts("dve", xb[:, :], xa[:, :], s3, None, ALU.mult, None, [xa, ss], [xb])
                    yield
                for c in range(8):
                    bk = banks[c % 2]
                    for m in range(4):
                        xb = xs[m]
                        k.op("pe", lambda e, bk=bk, c=c, m=m, xb=xb: e.transpose(
                            bk[:, m * 128:(m + 1) * 128], xb[:, c * 128:(c + 1) * 128], cst[:, 0:128]),
                            [xb, cst], [bk])
                    ts("dve", h[:, c, :], bk[:, :], prm[:, L * 8 + c:L * 8 + c + 1], None, ALU.mult, None, [bk, prm], [h])
                    yield

            gcnt = [0]
            bcnt = [0]
            groups = [([g], g, False) for g in range(16)]
            sl = 16
            g = 16
            for _ in range(9):
                groups.append(([g, g + 1], sl, True))
                g += 2
                sl += 1

            def p1_proj(i):
                h = hT[i % 2]
                t0 = i * 512
                ct = cst_t[i % 2]
                st_ = snt_t[i % 2]
                k.dma(ct[:, :], cs_in[:, t0:t0 + 512], writes=[ct])
                k.dma(st_[:, :], sn_in[:, t0:t0 + 512], writes=[st_])
                for (wg, slot, rope) in groups:
                    bks = []
                    for wgi in wg:
                        bk = banks[2 + bcnt[0] % 4]
                        bcnt[0] += 1
                        for c in range(8):
                            wt_, wap_ = wcols(c, wgi * 128, (wgi + 1) * 128)
                            mm(bk[:, :], wap_, h[:, c, :], [wt_, h], [bk],
                               start=(c == 0), stop=(c == 7))
                        bks.append(bk)
                    o = gb[gcnt[0] % 6]
                    gcnt[0] += 1
                    if not rope:
                        cp("act", o[:, :], bks[0][:, :], [bks[0]], [o])
                    else:
                        a = rt1[gcnt[0] % 2]
                        b = rt2[gcnt[0] % 2]
                        tt("dve", a[:, :], bks[0][:, :], ct[:, :], ALU.mult, [bks[0], ct], [a])
                        tt("dve", b[:, :], bks[1][:, :], st_[:, :], ALU.mult, [bks[1], st_], [b])
                        tt("pool", o[:, :], a[:, :], b[:, :], ALU.add, [a, b], [o])
                    k.dma(proj[slot, :, t0:t0 + 512], o[:, :], reads=[o], writes=[proj_b])
                    yield
                for m in range(4):
                    bk = banks[6 + m % 2]
                    for c in range(8):
                        wt_, wap_ = wcols(c, 34 * 128, NCOL)
                        mm(bk[:, 0:384], h[:, c, m * 128:(m + 1) * 128], wap_, [wt_, h], [bk],
                           start=(c == 0), stop=(c == 7))
                    vb = vtb[m % 2]
                    cp("act", vb[:, :], bk[:, 0:384], [bk], [vb])
                    r0 = PAD + t0 + m * 128
                    k.dma(vtm[r0:r0 + 128, :], vb[:, :], reads=[vb], writes=[vtm_b])
                    yield

            for _ in (p1_norm(0) if stop not in ("x3", "x4") else []):
                pass
            for i in range(NT if stop not in ("x3", "x4") else 0):
                side = p1_norm(i + 1) if i + 1 < NT else None
                stepn = 0
                for _ in p1_proj(i):
                    stepn += 1
                    if side is not None and stepn % 2 == 0:
                        try:
                            next(side)
                        except StopIteration:
                            side = None
                if side is not None:
                    for _ in side:
                        pass

            if stop == "p1":
                break
            k.barrier()
            k.phase_begin()
            raw = k.sb("raw", [128, 8, 514], BF16)
            sh = k.sb("shf", [128, 8, 512], BF16)
            tA = [k.sb(f"tA{i}", [128, 512], F32) for i in range(2)]
            tB = [k.sb(f"tB{i}", [128, 512], F32) for i in range(2)]
            tw = k.sb("tw", [64, 512], BF16)
            lal = k.sb("lal", [64, 512], BF16)
            sg = k.sb("sg", [128, 512], F32)
            Ssc = k.sb("Ssc", [128, 512], F32)
            Sm = k.sb("Sm", [128, 512], F32)
            eP = k.sb("eP", [128, 512], F32)
            eN = k.sb("eN", [128, 512], F32)
            ePm = k.sb("ePm", [128, 512], F32)
            av = k.sb("av", [128, 512], F32)
            kq = k.sb("kq", [128, 512], F32)
            sq = sg
            nrm = Ssc
            kk = k.sb("kk", [128, 512], F32)
            kka = k.sb("kka", [128, 512], F32)
            ka = Sm
            keff = k.sb("keff", [128, 512], F32)
            rke = kq
            bon = [k.sb(f"bon{i}", [128, 512], F32) for i in range(2)]
            T1 = [[k.sb(f"T1_{s_}{i}", [128, 8, 192], BF16) for i in range(2)] for s_ in range(2)]
            T2 = [[k.sb(f"T2_{s_}{i}", [128, 8, 192], BF16) for i in range(2)] for s_ in range(2)]
            T1o = [[k.sb(f"T1o_{s_}{i}", [64, 8, 192], BF16) for i in range(2)] for s_ in range(2)]
            T2o = [[k.sb(f"T2o_{s_}{i}", [64, 8, 192], BF16) for i in range(2)] for s_ in range(2)]
            vT = [[k.sb(f"vT{s_}{i}", [64, 8, 128], BF16) for i in range(2)] for s_ in range(2)]
            NSL = 4
            bank_lo = [Buf(f"lo{i}") for i in range(8)]
            bank_hi = [Buf(f"hi{i}") for i in range(8)]
            Xb = [k.sb(f"X{i}", [64, 4, 128], BF16) for i in range(2 * NSL)]
            PNb = [k.sb(f"PN{i}", [64, 4, 128], BF16) for i in range(2 * NSL)]
            RBs = [k.sb(f"RB{i}", [64, 4, 128], BF16) for i in range(NSL)]
            RKs = [k.sb(f"RK{i}", [64, 4, 128], BF16) for i in range(NSL)]
            QYs = [k.sb(f"QY{i}", [64, 4, 128], BF16) for i in range(NSL)]
            MGs = [k.sb(f"MG{i}", [64, 4, 128], BF16) for i in range(NSL)]
            Ysb = [k.sb(f"Ys{i}", [64, 4, 64], F32) for i in range(NSL)]
            Hs = [[[k.sb(f"H{e}{h}{i}", [64, 64], BF16) for i in range(2)] for h in range(4)] for e in range(2)]
            hcnt = [[0] * 4 for _ in range(2)]
            for e in range(2):
                for h in range(4):
                    memset("pool", Hs[e][h][0][:, :], 0.0, [Hs[e][h][0]])
            for s_ in range(2):
                for t_ in T1[s_]:
                    for c in range(8):
                        cp("pool", t_[:, c, 0:64], cb[:, 128:192], [cb], [t_])

            def rwkv_prep(i, e, st):
                t0 = i * 512
                rw = raw
                lo = 1 if i == 0 else 0
                hi = 513 if i == NT - 1 else 514
                k.dma(rw[:, :, lo:hi], proj[0:8, :, t0 - 1 + lo:t0 - 1 + hi].rearrange("g p n -> p g n"),
                      reads=[proj_b], writes=[rw])
                if i == 0:
                    memset("pool", rw[:, :, 0:1], 0.0, [rw])
                if i == NT - 1:
                    memset("pool", rw[:, :, 513:514], 0.0, [rw])
                if t0 == HALF:
                    ts("dve", rw[:, :, 0:1], rw[:, :, 0:1], linkt[:, 0:1], None, ALU.mult, None, [rw, linkt], [rw])
                if t0 + 512 == HALF:
                    ts("dve", rw[:, :, 513:514], rw[:, :, 513:514], linkt[:, 0:1], None, ALU.mult, None, [rw, linkt], [rw])
                yield
                for s in range(8):
                    a_ = tA[s % 2]
                    b_ = tB[s % 2]
                    tt("pool", a_[:, :], rw[:, s, 0:512], rw[:, s, 2:514], ALU.add, [rw], [a_])
                    act(b_[:, :], rw[:, s, 1:513], AF.Copy, [rw, prm2], [b_], scale=prm2[:, L * 8 + s:L * 8 + s + 1])
                    stt(sh[:, s, :], a_[:, :], prm2[:, 16 + L * 8 + s:16 + L * 8 + s + 1], b_[:, :], ALU.mult, ALU.add,
                        [a_, b_, prm2], [sh])
                    if s % 2:
                        yield
                act(tw[:, :], sh[e * 64:(e + 1) * 64, 6, :], AF.Tanh, [sh], [tw])
                cp("act", lal[:, :], sh[e * 64:(e + 1) * 64, 7, :], [sh], [lal])
                rm = cst[:, 1152:1664] if e == 0 else cst[:, 1664:2176]
                for hh in range(2):
                    r_ = sh[:, hh, :]
                    k_ = sh[:, 2 + hh, :]
                    v_ = sh[:, 4 + hh, :]
                    t1 = T1[st][hh]
                    t2 = T2[st][hh]
                    bk = banks[0]
                    mm(bk[:, :], w2s[:, L, e, hh * 128:(hh + 1) * 128], tw[:, :], [w2s, tw], [bk])
                    cidx = 32 + L * 4 + e * 2 + hh
                    act(sg[:, :], bk[:, :], AF.Sigmoid, [bk, prm], [sg], bias=prm[:, cidx:cidx + 1], scale=1.0)
                    mm(bk[:, :], a2s[:, L, e, hh * 128:(hh + 1) * 128], lal[:, :], [a2s, lal], [bk])
                    act(av[:, :], bk[:, :], AF.Sigmoid, [bk, prm], [av], bias=prm[:, cidx + 8:cidx + 9], scale=1.0)
                    yield
                    if e == 0:
                        k.op("dve", lambda en: en.tensor_tensor_scan(out=Ssc[:, :], data0=rm, data1=sg[:, :], initial=0.0,
                                                                    op0=ALU.mult, op1=ALU.add), [sg, cst], [Ssc])
                    else:
                        k.op("dve", lambda en: en.tensor_tensor_scan(out=rev(Ssc[:, :]), data0=rev(rm), data1=rev(sg[:, :]),
                                                                    initial=0.0, op0=ALU.mult, op1=ALU.add), [sg, cst], [Ssc])
                    tt("pool", Sm[:, :], Ssc[:, :], sg[:, :], ALU.subtract, [Ssc, sg], [Sm])
                    act(eP[:, :], Ssc[:, :], AF.Exp, [Ssc], [eP], scale=-C0)
                    act(eN[:, :], Ssc[:, :], AF.Exp, [Ssc], [eN], scale=C0)
                    act(ePm[:, :], Sm[:, :], AF.Exp, [Sm], [ePm], scale=-C0)
                    yield
                    ts("dve", kq[:, :], k_, prm[:, 48 + L * 2 + hh:48 + L * 2 + hh + 1], None, ALU.mult, None, [sh, prm], [kq])
                    act(sq[:, :], kq[:, :], AF.Square, [kq], [sq])
                    mm(bk[:, :], BONES, sq[:, :], [cst, sq], [bk])
                    act(nrm[:, :], bk[:, :], AF.Sqrt, [bk], [nrm])
                    yield
                    ts("dve", nrm[:, :], nrm[:, :], 1e-12, None, ALU.max, None, [nrm], [nrm])
                    recip(nrm[:, :], nrm[:, :], [nrm], [nrm])
                    tt("pool", kk[:, :], kq[:, :], nrm[:, :], ALU.mult, [kq, nrm], [kk])
                    tt("pool", kka[:, :], kk[:, :], av[:, :], ALU.mult, [kk, av], [kka])
                    yield
                    ts("dve", ka[:, :], av[:, :], prm[:, 52 + L * 2 + hh:52 + L * 2 + hh + 1], lnb[:, L * 2 + hh:L * 2 + hh + 1],
                       ALU.mult, ALU.add, [av, prm, lnb], [ka])
                    tt("pool", keff[:, :], k_, ka[:, :], ALU.mult, [sh, ka], [keff])
                    v3 = lambda ap_: ap_.rearrange("p (c n) -> p c n", n=64)
                    stt(t2[:, :, 0:64], v3(kk[:, :]), -1.0, v3(ePm[:, :]), ALU.mult, ALU.mult, [kk, ePm], [t2])
                    tt("dve", t1[:, :, 128:192], v3(kka[:, :]), v3(eN[:, :]), ALU.mult, [kka, eN], [t1])
                    yield
                    tt("dve", t1[:, :, 64:128], v3(keff[:, :]), v3(eN[:, :]), ALU.mult, [keff, eN], [t1])
                    tt("dve", t2[:, :, 64:128], v3(r_), v3(eP[:, :]), ALU.mult, [sh, eP], [t2])
                    for c in range(8):
                        col = c * 64 + (63 if e == 0 else 0)
                        ts("pool", t2[:, c, 128:192], cst[:, 128:192], eP[:, col:col + 1], None, ALU.mult, None, [cst, eP], [t2])
                    yield
                    stt(rke[:, :], r_, prm[:, 56 + L * 2 + hh:56 + L * 2 + hh + 1], keff[:, :], ALU.mult, ALU.mult,
                        [sh, prm, keff], [rke])
                    mm(bk[:, :], BONES, rke[:, :], [cst, rke], [bk])
                    bo = bon[hh]
                    tt("dve", bo[:, :], bk[:, :], v_, ALU.mult, [bk, sh], [bo])
                    k.dma(bsd[e, hh * 128:(hh + 1) * 128, t0:t0 + 512], bo[:, :], reads=[bo], writes=[bs_b])
                    yield
                    vt = vT[st][hh]
                    for q in range(2):
                        b1 = banks[1]
                        for j in range(4):
                            c = q * 4 + j
                            mm(b1[0:64, j * 128:(j + 1) * 128], sh[:, 4 + hh, c * 64:(c + 1) * 64], I128b, [sh, cb], [b1])
                        cp("act", vt[:, q * 4:(q + 1) * 4, :], b1[0:64, :].rearrange("p (j n) -> p j n", n=128), [b1], [vt])
                        yield
                    cp("act", T1o[st][hh][:, :, :], t1[64:128, :, :], [t1], [T1o[st][hh]])
                    cp("dve", T2o[st][hh][:, :, :], t2[64:128, :, :], [t2], [T2o[st][hh]])
                    yield

            def quad(i, e, hh, pp, q, st):
                def gen(slot):
                    hd = hh * 2 + pp
                    if pp == 1:
                        t1, t2 = T1o[st][hh], T2o[st][hh]
                    else:
                        t1, t2 = T1[st][hh], T2[st][hh]
                    vt = vT[st][hh]
                    cs_ = [q * 4 + j for j in range(4)]
                    at = lambda c: t2[0:64, c, 0:64]
                    rtc = lambda c: t2[0:64, c, 64:128]
                    dW = lambda c: t2[0:64, c, 128:192]
                    Ic = lambda c: t1[0:64, c, 0:64]
                    ktc = lambda c: t1[0:64, c, 64:128]
                    btc = lambda c: t1[0:64, c, 128:192]
                    bk = banks[2 + slot]
                    LO = bank_lo[2 + slot]
                    HI = bank_hi[2 + slot]
                    lo4 = bk[0:64, :].rearrange("p (j n) -> p j n", n=128)
                    hi4 = bk[64:128, :].rearrange("p (j n) -> p j n", n=128)
                    for j, c in enumerate(cs_):
                        mm(bk[0:64, j * 128:(j + 1) * 128], at(c), t1[0:64, c, 0:128], [t1, t2], [LO])
                        mm(bk[64:128, j * 128:j * 128 + 64], at(c), btc(c), [t1, t2], [HI])
                        mm(bk[64:128, j * 128 + 64:(j + 1) * 128], btc(c), at(c), [t1, t2], [HI])
                    X = Xb[slot * 2]
                    X2 = Xb[slot * 2 + 1]
                    PN = PNb[slot * 2]
                    PN2 = PNb[slot * 2 + 1]
                    RB = RBs[slot]
                    RK = RKs[slot]
                    tt("dve", X[:, :, :], lo4, qm[:, e, 0, :, :], ALU.mult, [LO, qm], [X])
                    cp("act", PN2[:, :, :], hi4, [HI], [PN2])
                    tt("pool", PN[:, :, :], PN2[:, :, :], qm[:, e, 1, :, :], ALU.mult, [PN2, qm], [PN])
                    yield
                    for lv in range(6):
                        for j in range(4):
                            mm(bk[0:64, j * 128:(j + 1) * 128], PN[:, j, 64:128], X[:, j, :], [PN, X], [LO])
                            if lv < 5:
                                mm(bk[64:128, j * 128:j * 128 + 64], PN[:, j, 64:128], PN[:, j, 0:64], [PN], [HI])
                                mm(bk[64:128, j * 128 + 64:(j + 1) * 128], PN[:, j, 0:64], PN[:, j, 64:128], [PN], [HI])
                        tt("dve", X2[:, :, :], lo4, X[:, :, :], ALU.add, [LO, X], [X2])
                        if lv < 5:
                            cp("act", PN2[:, :, :], hi4, [HI], [PN2])
                        X, X2 = X2, X
                        PN, PN2 = PN2, PN
                        yield
                    for j, c in enumerate(cs_):
                        mm(bk[0:64, j * 128:(j + 1) * 128], btc(c), t2[0:64, c, 64:192], [t1, t2], [LO])
                        mm(bk[64:128, j * 128:(j + 1) * 128], ktc(c), t2[0:64, c, 64:192], [t1, t2], [HI])
                    tt("dve", RB[:, :, :], lo4, qm[:, e, 2, :, :], ALU.mult, [LO, qm], [RB])
                    cp("act", PN2[:, :, :], hi4, [HI], [PN2])
                    tt("pool", RK[:, :, :], PN2[:, :, :], qm[:, e, 2, :, :], ALU.mult, [PN2, qm], [RK])
                    yield
                    QY = QYs[slot]
                    MG = MGs[slot]
                    Itok = cb[0:64, 0:64]
                    for j, c in enumerate(cs_):
                        o = j * 128
                        mm(bk[0:64, o:o + 128], Ic(c), t2[0:64, c, 64:192], [t1, t2], [LO], start=True, stop=False)
                        mm(bk[0:64, o:o + 128], X[:, j, 0:64], RB[:, j, :], [X, RB], [LO], start=False, stop=True)
                        mm(bk[64:128, o:o + 128], X[:, j, 64:128], RB[:, j, :], [X, RB], [HI], start=True, stop=False)
                        mm(bk[64:128, o:o + 128], Itok, RK[:, j, :], [cb, RK], [HI], start=False, stop=True)
                    cp("act", QY[:, :, :], lo4, [LO], [QY])
                    cp("act", MG[:, :, :], hi4, [HI], [MG])
                    yield
                    order = range(4) if e == 0 else range(3, -1, -1)
                    Ys = Ysb[slot]
                    for j in order:
                        c = cs_[j]
                        hc = hcnt[e][hd]
                        H = Hs[e][hd][hc % 2]
                        Hn = Hs[e][hd][(hc + 1) % 2]
                        hcnt[e][hd] += 1
                        tok = i * 512 + c * 64
                        if (e == 0 and tok == HALF) or (e == 1 and tok + 64 == HALF):
                            ts("dve", H[:, :], H[:, :], linkt[0:64, 0:1], None, ALU.mult, None, [H, linkt], [H])
                        vv = vt[:, c, pp * 64:(pp + 1) * 64]
                        mm(bk[0:64, j * 64:(j + 1) * 64], vv, MG[:, j, 0:64], [vt, MG], [LO], start=True, stop=False)
                        mm(bk[0:64, j * 64:(j + 1) * 64], H[:, :], QY[:, j, 0:64], [H, QY], [LO], start=False, stop=True)
                        mm(bk[64:128, 0:64], MG[:, j, 64:128], vv, [MG, vt], [HI], start=True, stop=False)
                        mm(bk[64:128, 0:64], QY[:, j, 64:128], H[:, :], [QY, H], [HI], start=False, stop=True)
                        cp("dve", Hn[:, :], bk[64:128, 0:64], [HI], [Hn])
                        yield
                    cp("act", Ys[:, :, :], bk[0:64, 0:256].rearrange("p (j n) -> p j n", n=64), [LO], [Ys])
                    t0 = i * 512 + q * 256
                    k.dma(ysd[e, hd * 64:(hd + 1) * 64, t0:t0 + 256], Ys[:, :, :].rearrange("p j n -> p (j n)"),
                          reads=[Ys], writes=[ys_b])
                return gen

            def run_side(factories, nslots, side):
                active = []
                free = list(range(nslots))
                it = iter(factories)
                done = False
                while active or not done:
                    while free and not done:
                        try:
                            f_ = next(it)
                        except StopIteration:
                            done = True
                            break
                        sl_ = free.pop(0)
                        active.append((sl_, f_(sl_)))
                    if side is not None:
                        try:
                            next(side)
                        except StopIteration:
                            side = None
                    for (sl_, g_) in list(active):
                        try:
                            next(g_)
                        except StopIteration:
                            active.remove((sl_, g_))
                            free.append(sl_)
                if side is not None:
                    for _ in side:
                        pass

            seq = []
            for i in range(NT):
                seq.append((i, 0))
                seq.append((NT - 1 - i, 1))
            if stop == "p2c":
                seq = [(i, 0) for i in range(NT)]
            if stop in ("x3", "x4"):
                seq = []
            for _ in (rwkv_prep(seq[0][0], seq[0][1], 0) if seq else []):
                pass
            for n_, (i, e) in enumerate(seq):
                st = n_ % 2
                qorder = [0, 1] if e == 0 else [1, 0]
                facs = [quad(i, e, hh, pp, q, st) for q in qorder for hh in range(2) for pp in range(2)]
                side = None
                if n_ + 1 < len(seq):
                    side = rwkv_prep(seq[n_ + 1][0], seq[n_ + 1][1], (n_ + 1) % 2)
                run_side(facs, NSL, side)
            if stop in ("p2a", "p2b", "p2c", "p2d", "p2e"):
                break

            if stop == "p2":
                break
            k.barrier()
            k.phase_begin()
            def run_pipelined(factories, nslots):
                active = []
                free = list(range(nslots))
                it = iter(factories)
                done = False
                while active or not done:
                    while free and not done:
                        try:
                            f_ = next(it)
                        except StopIteration:
                            done = True
                            break
                        sl_ = free.pop(0)
                        active.append((sl_, f_(sl_)))
                    for (sl_, g_) in list(active):
                        try:
                            next(g_)
                        except StopIteration:
                            active.remove((sl_, g_))
                            free.append(sl_)

            qTb = k.sb("qTb", [64, 4, WIN], BF16)
            kTb = k.sb("kTb", [64, 4, WIN + 2048], BF16)
            acc = k.sb("acc", [65, 4, WIN], F32)
            Vb = [k.sb(f"Vb{i}", [128, 4, 65], BF16) for i in range(8)]
            Pt = [k.sb(f"Pt{i}", [128, 512], BF16) for i in range(6)]
            rden = [k.sb(f"rden{i}", [64, 512], F32) for i in range(2)]
            ocb = [k.sb(f"ocb{i}", [64, 512], BF16) for i in range(2)]
            for v_ in Vb:
                memset("pool", v_[:, :, :], 1.0, [v_])
            ncnt = [0]
            NW = TC // WIN
            for w in range(NW if stop != "x4" else 0):
                w0 = w * WIN
                k.dma(qTb[:, :, :], proj[16:18, :, w0:w0 + WIN].rearrange("g (h p) n -> p (g h) n", p=64), reads=[proj_b], writes=[qTb])
                lo = max(0, w0 - 1024)
                hi = min(TC, w0 + WIN + 1024)
                k.dma(kTb[:, :, lo - (w0 - 1024):hi - (w0 - 1024)], proj[18:20, :, lo:hi].rearrange("g (h p) n -> p (g h) n", p=64),
                      reads=[proj_b], writes=[kTb])
                if w0 - 1024 < 0:
                    memset("pool", kTb[:, :, 0:1024], 0.0, [kTb])
                if w0 + WIN + 1024 > TC:
                    memset("pool", kTb[:, :, WIN + 1024:WIN + 2048], 0.0, [kTb])
                combos = []
                for di, d in enumerate((1, 4, 16)):
                    nb = WIN // (128 * d)
                    for r in range(d):
                        for b in range(nb):
                            combos.append((di, d, nb, r, b))

                def v_issue(n, w0=w0, combos=combos):
                    if n >= len(combos):
                        return
                    di, d, nb, r, b = combos[n]
                    for tile in range(2):
                        kpos0 = 128 * b - 64 + 128 * tile
                        rs = PAD + w0 + r + d * kpos0
                        vb = Vb[(n % 4) * 2 + tile]
                        k.dma(vb[:, :, 0:64], vtm[rs:rs + 127 * d + 1:d, 0:256].rearrange("p (h c) -> p h c", c=64),
                              reads=[vtm_b], writes=[vb])

                def b_unit(n, w0=w0, combos=combos):
                    def gen(slot):
                        di, d, nb, r, b = combos[n]
                        qc0 = r + d * 128 * b
                        v_issue(n + 2)
                        pts = []
                        vbs = []
                        sbks = []
                        for tile in range(2):
                            kpos0 = 128 * b - 64 + 128 * tile
                            kc0 = 1024 + r + d * kpos0
                            sbk = banks[3 * slot + tile]
                            for h in range(4):
                                mm(sbk[:, h * 128:(h + 1) * 128], kTb[:, h, kc0:kc0 + 127 * d + 1:d],
                                   qTb[:, h, qc0:qc0 + 127 * d + 1:d], [kTb, qTb], [sbk])
                            sbks.append(sbk)
                            pts.append(Pt[3 * slot + tile])
                            vbs.append(Vb[(n % 4) * 2 + tile])
                        yield
                        for tile in range(2):
                            pt = pts[tile]
                            act(pt[:, :], sbks[tile][:, :], AF.Exp, [sbks[tile]], [pt], scale=0.125)
                            colm = None
                            if tile == 0 and b == 0:
                                if w0 == 0:
                                    colm = (cmask, cmask[:, 0:1])
                                elif w0 == HALF:
                                    colm = (linkt, linkt[:, 1:2])
                            if tile == 1 and b == nb - 1:
                                if w0 + WIN == TC:
                                    colm = (cmask, cmask[:, 1:2])
                                elif w0 + WIN == HALF:
                                    colm = (linkt, linkt[:, 2:3])
                            band = bm[:, tile, :, :].rearrange("p a n -> p (a n)")
                            if colm is None:
                                tt("dve", pt[:, :], pt[:, :], band, ALU.mult, [pt, bm], [pt])
                            else:
                                stt(pt[:, :], pt[:, :], colm[1], band, ALU.mult, ALU.mult, [pt, bm, colm[0]], [pt])
                        yield
                        ob = banks[6 + slot]
                        for h in range(4):
                            for tile in range(2):
                                mm(ob[0:65, h * 128:(h + 1) * 128], vbs[tile][:, h, :], pts[tile][:, h * 128:(h + 1) * 128],
                                   [vbs[tile], pts[tile]], [ob], start=(tile == 0), stop=(tile == 1))
                        yield
                        accv = acc[0:65, :, qc0:qc0 + 127 * d + 1:d]
                        obv = ob[0:65, :].rearrange("p (h n) -> p h n", n=128)
                        if di == 0:
                            cp("act", accv, obv, [ob], [acc])
                        else:
                            tt("dve", accv, accv, obv, ALU.add, [ob, acc], [acc])
                    return gen

                v_issue(0)
                v_issue(1)
                run_pipelined([b_unit(n) for n in range(len(combos))], 2)
                for h in range(4):
                    for ch in range(WIN // 512):
                        n = ncnt[0]
                        ncnt[0] += 1
                        db = banks[2 + 3 * (n % 2)]
                        mm(db[0:64, :], sel65[:, :], acc[0:65, h, ch * 512:(ch + 1) * 512], [sel65, acc], [db])
                        rd = rden[n % 2]
                        recip(rd[:, :], db[0:64, :], [db], [rd])
                        oc_ = ocb[n % 2]
                        tt("pool", oc_[:, :], acc[0:64, h, ch * 512:(ch + 1) * 512], rd[:, :], ALU.mult, [acc, rd], [oc_])
                        k.dma(att[h // 2, (h % 2) * 64:(h % 2) * 64 + 64, w0 + ch * 512:w0 + (ch + 1) * 512], oc_[:, :],
                              reads=[oc_], writes=[att_b])
            if stop == "p3b":
                break
            qTc = [k.sb(f"qTc{i}", [64, 2, 4, 512], BF16) for i in range(2)]
            kTc = [k.sb(f"kTc{i}", [64, 2, 768], BF16) for i in range(2)]
            Vc = [k.sb(f"Vc{i}", [128, 6, 2, 65], BF16) for i in range(2)]
            Osb = [k.sb(f"Osb{i}", [65, 512], F32) for i in range(2)]
            occ = [[k.sb(f"occ{i}{g_}", [64, 4, 512], BF16) for g_ in range(2)] for i in range(2)]
            for v_ in Vc:
                memset("pool", v_[:, :, :, :], 1.0, [v_])

            def c_load(sbi):
                t0 = sbi * 512
                qt = qTc[sbi % 2]
                kt_ = kTc[sbi % 2]
                vc = Vc[sbi % 2]
                for g_ in range(2):
                    k.dma(qt[:, g_, :, :], proj[20:24, g_ * 64:(g_ + 1) * 64, t0:t0 + 512].rearrange("a p n -> p a n"),
                          reads=[proj_b], writes=[qt])
                lo = max(0, t0 - 128)
                hi = min(TC, t0 + 640)
                k.dma(kt_[:, :, lo - (t0 - 128):hi - (t0 - 128)], proj[24, :, lo:hi].rearrange("(g p) n -> p g n", p=64),
                      reads=[proj_b], writes=[kt_])
                if t0 - 128 < 0:
                    memset("pool", kt_[:, :, 0:128], 0.0, [kt_])
                if t0 + 640 > TC:
                    memset("pool", kt_[:, :, 640:768], 0.0, [kt_])
                for kt6 in range(6):
                    rs = PAD + t0 - 128 + kt6 * 128
                    k.dma(vc[:, kt6, :, 0:64], vtm[rs:rs + 128, 256:384].rearrange("p (g c) -> p g c", c=64),
                          reads=[vtm_b], writes=[vc])

            sinkb = [k.sb(f"sinkb{g_}", [64, 512], F32) for g_ in range(2)]
            for g_ in range(2):
                so = (L * 2 + g_) * 512
                mm(banks[0][0:64, :], ones1[:, :], sinkr[0:1, so:so + 512], [ones1, sinkr], [banks[0]])
                cp("act", sinkb[g_][:, :], banks[0][0:64, :], [banks[0]], [sinkb[g_]])
            onesb = k.sb("onesb", [128, 64], BF16)
            memset("pool", onesb[:, :], 1.0, [onesb])
            dtmp = [k.sb(f"dtmp{i}", [64, 512], F32) for i in range(2)]

            def c_unit(sbi, jq, g_):
                def gen(slot):
                    t0 = sbi * 512
                    qt = qTc[sbi % 2]
                    kt_ = kTc[sbi % 2]
                    vc = Vc[sbi % 2]
                    if jq == 1 and g_ == 0 and sbi + 1 < NT:
                        c_load(sbi + 1)
                    qtok = t0 + 128 * jq
                    tiles = []
                    for rel in range(3):
                        ktile = jq + rel
                        ktok = t0 - 128 + 128 * ktile
                        if ktok < 0 or ktok >= TC:
                            continue
                        tiles.append((rel, ktile, ktok))
                    sb_ = [banks[4 * slot], banks[4 * slot + 1]]
                    ob = banks[4 * slot + 2]
                    db = banks[4 * slot + 3]

                    def s_mm(ti):
                        rel, ktile, ktok = tiles[ti]
                        sbk = sb_[ti % 2]
                        mm(sbk[:, :], kt_[:, g_, ktile * 128:(ktile + 1) * 128],
                           qt[:, g_, :, jq * 128:(jq + 1) * 128], [kt_, qt], [sbk])

                    def p_ev(ti):
                        rel, ktile, ktok = tiles[ti]
                        sbk = sb_[ti % 2]
                        pt = Pt[3 * slot + ti]
                        act(pt[:, :], sbk[:, :], AF.Exp, [sbk], [pt], scale=0.125)
                        cross = (ktok < HALF) != (qtok < HALF)
                        if rel != 1:
                            band = bm[:, 0 if rel == 0 else 1, :, :].rearrange("p a n -> p (a n)")
                            if cross:
                                stt(pt[:, :], pt[:, :], linkt[:, 0:1], band, ALU.mult, ALU.mult, [pt, bm, linkt], [pt])
                            else:
                                tt("dve", pt[:, :], pt[:, :], band, ALU.mult, [pt, bm], [pt])
                    for ti in range(min(2, len(tiles))):
                        s_mm(ti)
                    yield
                    for ti in range(min(2, len(tiles))):
                        p_ev(ti)
                    yield
                    if len(tiles) > 2:
                        s_mm(2)
                        yield
                        p_ev(2)
                        yield
                    for ti, (rel, ktile, ktok) in enumerate(tiles):
                        pt = Pt[3 * slot + ti]
                        mm(ob[0:64, :], vc[:, ktile, g_, 0:64], pt[:, :], [vc, pt], [ob], start=(ti == 0), stop=(ti == len(tiles) - 1))
                    for ti, (rel, ktile, ktok) in enumerate(tiles):
                        pt = Pt[3 * slot + ti]
                        mm(db[0:64, :], onesb[:, :], pt[:, :], [onesb, pt], [db], start=(ti == 0), stop=(ti == len(tiles) - 1))
                    yield
                    osb = Osb[slot]
                    cp("act", osb[0:64, :], ob[0:64, :], [ob], [osb])
                    dt_ = dtmp[slot]
                    tt("dve", dt_[:, :], db[0:64, :], sinkb[g_][:, :], ALU.add, [db, sinkb[g_]], [dt_])
                    yield
                    rd = rden[slot]
                    recip(rd[:, :], dt_[:, :], [dt_], [rd])
                    yield
                    oc_ = occ[sbi % 2][g_]
                    tt("pool", oc_[:, :, jq * 128:(jq + 1) * 128], osb[0:64, :].rearrange("p (a n) -> p a n", n=128),
                       rd[:, :].rearrange("p (a n) -> p a n", n=128), ALU.mult, [osb, rd], [oc_])
                    if jq == 3:
                        k.dma(att[2:6, g_ * 64:(g_ + 1) * 64, t0:t0 + 512].rearrange("a p n -> p a n"), oc_[:, :, :],
                              reads=[oc_], writes=[att_b])
                return gen

            if stop != "x4":
                c_load(0)
            run_pipelined([] if stop == "x4" else [c_unit(sbi, jq, g_) for sbi in range(NT) for jq in range(4) for g_ in range(2)], 2)

            k.barrier()
            k.phase_begin()
            wob = k.sb("wob", [128, 8, D], BF16)
            for c in range(8):
                k.dma(wob[:, c, :], wout_in[L, c * 128:(c + 1) * 128, :], writes=[wob], eng="pool")
            gt = [k.sb(f"gt{i}", [128, 8, 512], BF16) for i in range(2)]
            att_t = [k.sb(f"att{i}", [128, 6, 512], BF16) for i in range(2)]
            yv = [k.sb(f"yv{i}", [128, 2, 2, 512], F32) for i in range(2)]
            bv = [k.sb(f"bv{i}", [128, 2, 2, 512], F32) for i in range(2)]
            xt4 = [k.sb(f"xt4_{i}", [128, 4, D], F32) for i in range(3)]
            mix = [k.sb(f"mix{i}", [128, 8, 512], BF16) for i in range(2)]
            tYs = [[k.sb(f"tY{sl}{hh}", [128, 512], F32) for hh in range(2)] for sl in range(2)]
            tCs = [[k.sb(f"tC{sl}{hh}", [128, 512], F32) for hh in range(2)] for sl in range(2)]
            tRs = [[k.sb(f"tR{sl}{hh}", [128, 512], F32) for hh in range(2)] for sl in range(2)]
            tG = [k.sb(f"tG{i}", [128, 512], BF16) for i in range(2)]
            ss4 = k.sb("ss4", [128, 8], F32)
            xo = [k.sb(f"xo{i}", [128, D], F32) for i in range(1)]

            def p4_load_yb(i):
                t0 = i * 512
                for e in range(2):
                    k.dma(yv[i % 2][:, e, :, :], ysd[e, :, t0:t0 + 512].rearrange("(h p) n -> p h n", p=128),
                          reads=[ys_b], writes=[yv[i % 2]])
                    k.dma(bv[i % 2][:, e, :, :], bsd[e, :, t0:t0 + 512].rearrange("(h p) n -> p h n", p=128),
                          reads=[bs_b], writes=[bv[i % 2]])

            def p4_load_ga(i):
                t0 = i * 512
                k.dma(gt[i % 2][:, :, :], proj[8:16, :, t0:t0 + 512].rearrange("g p n -> p g n"), reads=[proj_b], writes=[gt[i % 2]])
                k.dma(att_t[i % 2][:, :, :], att[:, :, t0:t0 + 512].rearrange("g p n -> p g n"), reads=[att_b], writes=[att_t[i % 2]])

            def p4_load_x(i):
                t0 = i * 512
                for m in range(4):
                    k.dma(xt4[i % 3][:, m, :], x_src[t0 + m * 128:t0 + (m + 1) * 128, :], reads=[x1_b] if L else [],
                          writes=[xt4[i % 3]])

            opc = [0]

            def p4_unit(i):
                def gen(slot):
                    t0 = i * 512
                    g_ = gt[i % 2]
                    a_ = att_t[i % 2]
                    y_ = yv[i % 2]
                    b_ = bv[i % 2]
                    x_ = xt4[i % 3]
                    mx = mix[i % 2]
                    tY = tYs[slot]
                    tC = tCs[slot]
                    tR = tRs[slot]
                    bks = [banks[slot * 2], banks[slot * 2 + 1]]
                    for hh in range(2):
                        tt("pool", tY[hh][:, :], y_[:, 0, hh, :], y_[:, 1, hh, :], ALU.add, [y_], [tY[hh]])
                        mm(bks[hh][:, :], BAVG, tY[hh][:, :], [cst, tY[hh]], [bks[hh]])
                    yield
                    for hh in range(2):
                        tt("dve", tC[hh][:, :], tY[hh][:, :], bks[hh][:, :], ALU.subtract, [tY[hh], bks[hh]], [tC[hh]])
                        act(tY[hh][:, :], tC[hh][:, :], AF.Square, [tC[hh]], [tY[hh]])
                        mm(bks[hh][:, :], BAVG, tY[hh][:, :], [cst, tY[hh]], [bks[hh]])
                    yield
                    for hh in range(2):
                        ts("dve", tR[hh][:, :], bks[hh][:, :], 64e-5, None, ALU.add, None, [bks[hh]], [tR[hh]])
                        act(tR[hh][:, :], tR[hh][:, :], AF.Sqrt, [tR[hh]], [tR[hh]])
                    yield
                    for hh in range(2):
                        recip(tR[hh][:, :], tR[hh][:, :], [tR[hh]], [tR[hh]])
                        tt("pool", tC[hh][:, :], tC[hh][:, :], tR[hh][:, :], ALU.mult, [tC[hh], tR[hh]], [tC[hh]])
                        tt("pool", tR[hh][:, :], b_[:, 0, hh, :], b_[:, 1, hh, :], ALU.add, [b_], [tR[hh]])
                    if i + 2 < NT:
                        p4_load_yb(i + 2)
                    yield
                    for hh in range(2):
                        ts("dve", tC[hh][:, :], tC[hh][:, :], prm[:, 60 + L * 2 + hh:60 + L * 2 + hh + 1],
                           lnb[:, 4 + L * 2 + hh:4 + L * 2 + hh + 1], ALU.mult, ALU.add, [tC[hh], prm, lnb], [tC[hh]])
                        tt("pool", tC[hh][:, :], tC[hh][:, :], tR[hh][:, :], ALU.add, [tC[hh], tR[hh]], [tC[hh]])
                        tg = tG[hh]
                        act(tg[:, :], g_[:, hh, :], AF.Silu, [g_], [tg])
                        tt("dve", mx[:, hh, :], tC[hh][:, :], tg[:, :], ALU.mult, [tC[hh], tg], [mx])
                    yield
                    for j in range(6):
                        tg = tG[j % 2]
                        act(tg[:, :], g_[:, 2 + j, :], AF.Silu, [g_], [tg])
                        tt("pool" if j % 2 else "dve", mx[:, 2 + j, :], a_[:, j, :], tg[:, :], ALU.mult, [a_, tg], [mx])
                        if j == 5 and i + 2 < NT:
                            p4_load_ga(i + 2)
                        if j % 2:
                            yield
                    for m in range(4):
                        for hf in range(2):
                            bk = banks[4 + opc[0] % 4]
                            opc[0] += 1
                            for c in range(8):
                                mm(bk[:, :], mx[:, c, m * 128:(m + 1) * 128], wob[:, c, hf * 512:(hf + 1) * 512], [mx, wob], [bk],
                                   start=(c == 0), stop=(c == 7))
                            tt("dve", x_[:, m, hf * 512:(hf + 1) * 512], bk[:, :], x_[:, m, hf * 512:(hf + 1) * 512], ALU.add,
                               [bk, x_], [x_])
                        r0 = t0 + m * 128
                        if not last:
                            k.dma(x1d[r0:r0 + 128, :], x_[:, m, :], reads=[x_], writes=[x1_b])
                        else:
                            n = i * 4 + m
                            sc = ss4[:, (n % 2) * 4:(n % 2) * 4 + 1]
                            s2 = ss4[:, (n % 2) * 4 + 1:(n % 2) * 4 + 2]
                            s3 = ss4[:, (n % 2) * 4 + 2:(n % 2) * 4 + 3]
                            o_ = xo[0]
                            act(o_[:, :], x_[:, m, :], AF.Square, [x_], [o_, ss4], accum=sc)
                            ts("dve", s2, sc, 1.0 / D, 1e-5, ALU.mult, ALU.add, [ss4], [ss4])
                            act(s2, s2, AF.Sqrt, [ss4], [ss4])
                            recip(s3, s2, [ss4], [ss4])
                            stt(o_[:, :], x_[:, m, :], s3, fgt[:, :], ALU.mult, ALU.mult, [x_, ss4, fgt], [o_])
                            k.dma(y_out[r0:r0 + 128, :], o_[:, :], reads=[o_], writes=[yout_b])
                            outs.append(o_)
                        if m < 3:
                            yield
                    if i + 3 < NT:
                        p4_load_x(i + 3)
                return gen

            for i0 in range(min(2, NT)):
                p4_load_yb(i0)
                p4_load_ga(i0)
            for i0 in range(min(3, NT)):
                p4_load_x(i0)
            run_pipelined([p4_unit(i) for i in range(NT)], 2)
            if stop in ("p4", "x3", "x4"):
                break

        k.barrier()
        k.finalize(final_waits=list({id(o): o for o in outs}.values()))
    return nc


def _col_layout():
    rot = lambda base: [base + (j + 32) % 64 for j in range(64)]
    pl = lambda base: [base + j for j in range(64)]
    cols = list(range(1024))
    R = 1024
    cols += [R + j for j in range(256)]
    cols += [R + 1024 + j for j in range(256)]
    permc = []
    for a in range(4):
        permc += pl(a * 64) + pl((4 + a) * 64)
    cols += [R + 2048 + j for j in permc]
    for base in (R + 256, R + 512):
        for gi in range(2):
            cols += pl(base + gi * 128) + pl(base + gi * 128 + 64)
            cols += rot(base + gi * 128) + rot(base + gi * 128 + 64)
    for a in range(4):
        base = R + 1280
        cols += pl(base + a * 64) + pl(base + (4 + a) * 64)
        cols += rot(base + a * 64) + rot(base + (4 + a) * 64)
    base = R + 1792
    cols += pl(base) + pl(base + 64)
    cols += rot(base) + rot(base + 64)
    cols += [R + 768 + j for j in range(256)]
    cols += [R + 1920 + j for j in range(128)]
    assert len(cols) == NCOL
    return np.array(cols), np.array(permc)


def _constants():
    c = np.zeros((128, CW), np.float32)
    p = np.arange(128)[:, None]
    j = np.arange(128)[None, :]
    c[:, 0:128] = (p == j)
    c[:, 128:192] = (np.arange(64)[None, :] == (p % 64))
    c[:, 192:320] = (p // 64 == j // 64)
    c[:, 320:448] = (p // 64 == j // 64) / 64.0
    c[:, 448:576] = (p >= j)
    c[:, 576:704] = (p <= j)
    r = np.arange(64)[:, None]
    q = np.arange(64)[None, :]
    c[0:64, 704:768] = (q < r)
    c[0:64, 768:832] = (r < q)
    c[0:64, 832:896] = (r <= q)
    c[0:64, 896:960] = (q > r)
    c[0:64, 960:1024] = (r > q)
    c[0:64, 1024:1088] = (r >= q)
    col = np.arange(512)
    c[:, 1152:1664] = (col % 64 != 0)[None, :]
    c[:, 1664:2176] = (col % 64 != 63)[None, :]
    return c


def _prep_shared(inp):
    cols, permc = _col_layout()
    f = lambda a: np.ascontiguousarray(np.asarray(a, dtype=np.float32))
    win = f(np.asarray(inp["w_in"])[:, :, cols])
    rows = np.concatenate([np.arange(512), 512 + permc])
    wout = f(np.asarray(inp["w_out"])[:, rows, :])
    prm = np.zeros((128, 64), np.float32)
    lnb = np.zeros((128, 4), np.float32)
    for l in range(NLAYER):
        prm[:, l * 8:(l + 1) * 8] = np.asarray(inp["norm_g"])[l].reshape(8, 128).T
        prm[:, 16 + l * 8:16 + (l + 1) * 8] = np.asarray(inp["tshift_mu"])[l].reshape(8, 128).T
        for e in range(2):
            prm[:, 32 + l * 4 + e * 2:32 + l * 4 + e * 2 + 2] = np.asarray(inp["rwkv_w0"])[l, e].reshape(2, 128).T
            prm[:, 40 + l * 4 + e * 2:40 + l * 4 + e * 2 + 2] = np.asarray(inp["rwkv_a0"])[l, e].reshape(2, 128).T
        prm[:, 48 + l * 2:48 + l * 2 + 2] = np.asarray(inp["rwkv_k_k"])[l].reshape(2, 128).T
        prm[:, 52 + l * 2:52 + l * 2 + 2] = np.asarray(inp["rwkv_k_a"])[l].reshape(2, 128).T
        prm[:, 56 + l * 2:56 + l * 2 + 2] = np.asarray(inp["rwkv_r_k"])[l].reshape(2, 128).T
        prm[:, 60 + l * 2:60 + l * 2 + 2] = np.asarray(inp["ln_x_w"])[l].reshape(2, 128).T
        lnb[:, l * 2:l * 2 + 2] = np.asarray(inp["ln_x_b"])[l].reshape(2, 128).T
    w2s = f(np.asarray(inp["rwkv_w2"]).transpose(2, 0, 1, 3))
    a2s = f(np.asarray(inp["rwkv_a2"]).transpose(2, 0, 1, 3))
    sink = np.asarray(inp["attn_sink"], dtype=np.float32)
    sinkrow = f(np.repeat(sink.reshape(NLAYER, 2, 4, 1), 128, axis=3).reshape(1, -1))
    fg = f(np.tile(np.asarray(inp["final_g"], dtype=np.float32)[None, :], (128, 1)))
    return dict(win=win, wout=wout, prm=prm, lnb=lnb, w2s=w2s, a2s=a2s, sinkrow=sinkrow, fg=fg, cst=_constants())


def _rope_tables(pos):
    inv = (10000.0 ** (-np.arange(0, 64, 2, dtype=np.float32) / np.float32(64))).astype(np.float32)
    ang = (pos.astype(np.float32)[:, None] * inv[None, :]).astype(np.float32)
    cos = np.cos(ang).astype(np.float32).T
    sin = np.sin(ang).astype(np.float32).T
    p = np.arange(128)
    cs = cos[p % 32, :]
    sn = sin[p % 32, :] * np.where((p % 64) < 32, -1.0, 1.0)[:, None].astype(np.float32)
    return np.ascontiguousarray(cs, dtype=np.float32), np.ascontiguousarray(sn, dtype=np.float32)


_NC_CACHE = {}


def _run(core_x, core_link, inp, TC, debug=False, stop=None):
    sh = _prep_shared(inp)
    key = (TC, debug, stop)
    if key not in _NC_CACHE:
        _NC_CACHE[key] = build_nc(TC, debug, stop)
    nc = _NC_CACHE[key]
    HALF = TC // 2
    maps = []
    tabs = {}
    for x, lk in zip(core_x, core_link):
        if lk not in tabs:
            t = np.arange(TC)
            pos = t if lk else (t % HALF)
            tabs[lk] = _rope_tables(pos)
        cs, sn = tabs[lk]
        m = dict(sh)
        m["x"] = np.ascontiguousarray(x, dtype=np.float32)
        m["link"] = np.full((128, 1), float(lk), np.float32)
        m["cs"] = cs
        m["sn"] = sn
        maps.append(m)
    res = run_bass_kernel_spmd(nc, maps, core_ids=list(range(len(maps))))
    return res.results


def kernel(**inputs):
    xp = np.asarray(inputs["x_prompt"], dtype=np.float32)
    xs = np.asarray(inputs["x_sample"], dtype=np.float32)
    TC = 16384
    core_x = [xp[0:2].reshape(TC, D), xp[2:4].reshape(TC, D), xs[0], xs[1]]
    core_link = [0, 0, 1, 1]
    for _ in range(4):
        core_x.append(np.zeros((TC, D), np.float32))
        core_link.append(0)
    res = _run(core_x, core_link, inputs, TC)
    yp = np.stack([res[0]["y"], res[1]["y"]]).reshape(4, 8192, D).astype(np.float32)
    ys = np.stack([res[2]["y"], res[3]["y"]]).astype(np.float32)
    return (yp, ys)
```

```python
import os
import numpy as np
from contextlib import ExitStack
import concourse.bass as bass
import concourse.mybir as mybir
from concourse.bass_utils import run_bass_kernel_spmd

F32 = mybir.dt.float32
BF16 = mybir.dt.bfloat16
AF = mybir.ActivationFunctionType
ALU = mybir.AluOpType

ENGS = ("pe", "act", "dve", "pool", "sp")
EPOCH = 12000
D = 1024
NCOL = 25 * 128 + 384
C0 = float(np.exp(-0.5))
PAD = 1024
NLAYER = 2
CW = 2304


class Buf:
    __slots__ = ("name", "w", "r")

    def __init__(self, name=""):
        self.name = name
        self.w = None
        self.r = []


class DSem:
    def __init__(self, k):
        self.k = k
        self.sems = []
        self.finals = {}
        self.count = 0
        self.new_epoch()
        k.dsems.append(self)

    def new_epoch(self):
        if self.sems:
            self.finals[id(self.sems[-1])] = self.count
        self.sems.append(self.k.new_sem())
        self.count = 0

    def value_for(self, h):
        if self.sems[-1] is h:
            return self.count
        return self.finals[id(h)]


class Op:
    __slots__ = ("eng", "fn", "deps", "signal", "dma", "dsem", "dsem_h", "cnt", "ep")

    def __init__(self, eng, fn):
        self.eng = eng
        self.fn = fn
        self.deps = []
        self.signal = False
        self.dma = False


class T:
    def __init__(self, k, t, name):
        self.k = k
        self.t = t
        self.b = Buf(name)
        self._ds = None

    def __getitem__(self, key):
        return self.t[key]

    @property
    def ds(self):
        if self._ds is None:
            k = self.k
            if k.sb_mark is not None:
                self._ds = k.free_ds.pop() if k.free_ds else DSem(k)
                k.phase_ds.append(self._ds)
            else:
                self._ds = DSem(k)
        return self._ds


class K:
    def __init__(self, nc, es):
        self.nc = nc
        self.es = es
        self.ops = {e: [] for e in ENGS}
        self.nsem = 0
        self.dsems = []
        self.nt = 0
        self.sb_lo = 16512
        self.sb_hi = 229344
        self.sb_cur = self.sb_lo
        self.sb_mark = None
        self.free_ds = []
        self.phase_ds = []

    def new_sem(self):
        self.nsem += 1
        return self.es.enter_context(self.nc.semaphore(f"s{self.nsem}"))

    def sb(self, name, shape, dtype):
        nb = int(np.prod(shape[1:])) * (4 if dtype == F32 else 2)
        nb = (nb + 63) // 64 * 64
        off = self.sb_cur
        self.sb_cur += nb
        assert self.sb_cur <= self.sb_hi, f"SBUF overflow at {name}: {self.sb_cur}"
        self.nt += 1
        t = self.nc.alloc_sbuf_tensor_at(f"{name}_{self.nt}", list(shape), dtype, offset=off)
        return T(self, t, name)

    def phase_begin(self):
        if self.sb_mark is None:
            self.sb_mark = self.sb_cur
        self.sb_cur = self.sb_mark
        self.free_ds.extend(self.phase_ds)
        self.phase_ds = []

    def ps(self, name, shape, dtype=F32):
        t = self.es.enter_context(self.nc.psum_tensor(name, list(shape), dtype))
        return T(self, t, name)

    @staticmethod
    def _b(x):
        return x.b if isinstance(x, T) else x

    def _dep(self, op, d):
        if d.dma:
            op.deps.append(("dma", d.dsem_h, d.dsem.value_for(d.dsem_h)))
        else:
            op.deps.append(d)

    def _track(self, op, reads, writes):
        eng = op.eng
        for x in reads:
            b = self._b(x)
            if b.w is not None and not (b.w.eng == eng and eng == "pe" and not b.w.dma):
                self._dep(op, b.w)
        for x in writes:
            b = self._b(x)
            for r in b.r:
                if r.dma or r.eng != eng:
                    self._dep(op, r)
            if b.w is not None and (b.w.dma or b.w.eng != eng):
                self._dep(op, b.w)
        for x in reads:
            self._b(x).r.append(op)
        for x in writes:
            b = self._b(x)
            b.w = op
            b.r = []

    def op(self, eng, fn, reads=(), writes=()):
        o = Op(eng, fn)
        self._track(o, reads, writes)
        self.ops[eng].append(o)
        return o

    def dma(self, out, in_, reads=(), writes=(), sem=None, eng="sp", **kw):
        if sem is None:
            for x in list(writes) + list(reads):
                if isinstance(x, T):
                    sem = x
                    break
        ds = sem.ds
        if ds.count + 16 > EPOCH:
            ds.new_epoch()
        o = Op(eng, lambda e: e.dma_start(out=out, in_=in_, **kw))
        o.dma = True
        self._track(o, reads, writes)
        ds.count += 16
        o.dsem = ds
        o.dsem_h = ds.sems[-1]
        self.ops[eng].append(o)
        return o

    def barrier(self):
        lasts = {}
        for e in ENGS:
            for o in reversed(self.ops[e]):
                if not o.dma:
                    lasts[e] = o
                    break
        dd = []
        for ds in self.dsems:
            if ds.count:
                dd.append(("dma", ds.sems[-1], ds.count))
        for e in ENGS:
            o = Op(e, lambda eng: eng.nop())
            for e2, lo in lasts.items():
                if e2 != e:
                    o.deps.append(lo)
            o.deps.extend(dd)
            self.ops[e].append(o)

    def finalize(self, final_waits=()):
        for e in ENGS:
            for o in self.ops[e]:
                for d in o.deps:
                    if not isinstance(d, tuple):
                        d.signal = True
        for e in ENGS:
            cnt = 0
            sems = [self.new_sem()]
            for o in self.ops[e]:
                if o.dma:
                    continue
                if o.signal:
                    if cnt + 1 > EPOCH:
                        sems.append(self.new_sem())
                        cnt = 0
                    cnt += 1
                    o.cnt = cnt
                    o.ep = sems[-1]
        fw = []
        for t in final_waits:
            ds = t.ds
            for h in ds.sems:
                v = ds.value_for(h)
                if v:
                    fw.append((h, v))
        k = self
        with self.nc.Block() as block:
            def emit(e_name):
                def body(e):
                    seen = {}
                    for o in k.ops[e_name]:
                        need = {}
                        for d in o.deps:
                            if isinstance(d, tuple):
                                key, h, v = id(d[1]), d[1], d[2]
                            else:
                                key, h, v = id(d.ep), d.ep, d.cnt
                            if seen.get(key, 0) >= v:
                                continue
                            if key not in need or need[key][1] < v:
                                need[key] = (h, v)
                        for key, (h, v) in need.items():
                            e.wait_ge(h, v)
                            seen[key] = v
                        ins = o.fn(e)
                        if o.dma:
                            ins.then_inc(o.dsem_h, 16)
                        elif o.signal:
                            ins.then_inc(o.ep, 1)
                    if e_name == "sp":
                        for (h, v) in fw:
                            e.wait_ge(h, v)
                return body
            block.sync(emit("sp"))
            block.scalar(emit("act"))
            block.vector(emit("dve"))
            block.gpsimd(emit("pool"))
            block.tensor(emit("pe"))


def rev(ap):
    pat = [list(p) for p in ap.ap]
    off = ap.offset + (pat[-1][1] - 1) * pat[-1][0]
    pat[-1][0] = -pat[-1][0]
    return bass.AP(ap.tensor, off, pat)


def build_nc(TC, debug=False, stop=None):
    HALF = TC // 2
    NT = TC // 512
    WIN = min(2048, HALF)
    assert HALF % WIN == 0 and WIN % 2048 == 0 or WIN == HALF
    nc = bass.Bass("TRN2", target_bir_lowering=False)
    dk = "ExternalOutput" if debug else "Internal"

    def din(name, shape, dt=F32):
        return nc.dram_tensor(name, list(shape), dt, kind="ExternalInput").ap()

    x_in = din("x", [TC, D])
    link_in = din("link", [128, 1])
    cs_in = din("cs", [128, TC])
    sn_in = din("sn", [128, TC])
    win_in = din("win", [NLAYER, D, NCOL])
    wout_in = din("wout", [NLAYER, D, D])
    prm_in = din("prm", [128, 64])
    w2_in = din("w2s", [64, NLAYER, 2, 256])
    a2_in = din("a2s", [64, NLAYER, 2, 256])
    sink_in = din("sinkrow", [1, NLAYER * 2 * 512])
    fg_in = din("fg", [128, D])
    cst_in = din("cst", [128, CW])
    y_out = nc.dram_tensor("y", [TC, D], F32, kind="ExternalOutput").ap()

    proj = nc.dram_tensor("proj", [25, 128, TC], BF16, kind=dk).ap()
    vtm = nc.dram_tensor("vtm", [TC + 2 * PAD, 384], BF16, kind=dk).ap()
    ysd = nc.dram_tensor("ysd", [2, 256, TC], F32, kind=dk).ap()
    bsd = nc.dram_tensor("bsd", [2, 256, TC], F32, kind=dk).ap()
    att = nc.dram_tensor("att", [6, 128, TC], BF16, kind=dk).ap()
    x1d = nc.dram_tensor("x1d", [TC, D], F32, kind=dk).ap()

    with ExitStack() as es:
        k = K(nc, es)
        banks = [k.ps(f"bk{i}", [128, 512]) for i in range(8)]

        def act(out, in_, func, r, w, bias=None, scale=None, accum=None, eng="act"):
            kw = {}
            if bias is not None:
                kw["bias"] = bias
            if scale is not None:
                kw["scale"] = scale
            if accum is not None:
                kw["accum_out"] = accum
            return k.op("act", lambda e: e.activation(out=out, in_=in_, func=func, **kw), r, w)

        def tt(eng, out, in0, in1, op, r, w):
            return k.op(eng, lambda e: e.tensor_tensor(out=out, in0=in0, in1=in1, op=op), r, w)

        def ts(eng, out, in0, s1, s2, op0, op1, r, w):
            if op1 is None:
                return k.op(eng, lambda e: e.tensor_scalar(out=out, in0=in0, scalar1=s1, scalar2=None, op0=op0), r, w)
            return k.op(eng, lambda e: e.tensor_scalar(out=out, in0=in0, scalar1=s1, scalar2=s2, op0=op0, op1=op1), r, w)

        def stt(out, in0, scalar, in1, op0, op1, r, w):
            return k.op("dve", lambda e: e.scalar_tensor_tensor(out=out, in0=in0, scalar=scalar, in1=in1, op0=op0, op1=op1), r, w)

        def cp(eng, out, in_, r, w):
            if eng == "act":
                return k.op("act", lambda e: e.activation(out=out, in_=in_, func=AF.Copy), r, w)
            return k.op(eng, lambda e: e.tensor_copy(out=out, in_=in_), r, w)

        def mm(out, lhsT, rhs, r, w, start=True, stop=True):
            return k.op("pe", lambda e: e.matmul(out, lhsT=lhsT, rhs=rhs, start=start, stop=stop), r, w)

        def recip(out, in_, r, w):
            return k.op("dve", lambda e: e.reciprocal(out=out, in_=in_), r, w)

        def memset(eng, ap, val, w):
            return k.op(eng, lambda e: e.memset(ap, val), (), w)

        prm = k.sb("prm", [128, 64], F32)
        prm2 = k.sb("prm2", [128, 32], F32)
        cst = k.sb("cst", [128, CW], F32)
        cb = k.sb("cstb", [128, 1088], BF16)
        w2s = k.sb("w2s", [64, NLAYER, 2, 256], BF16)
        a2s = k.sb("a2s", [64, NLAYER, 2, 256], BF16)
        linkt = k.sb("link", [128, 4], F32)
        sinkr = k.sb("sinkr", [1, NLAYER * 2 * 512], F32)
        fgt = k.sb("fg", [128, D], F32)
        zt = k.sb("zeros", [128, 384], BF16)

        k.dma(prm[:, :], prm_in, writes=[prm])
        k.dma(cst[:, :], cst_in, writes=[cst])
        k.dma(cb[:, 0:544], cst_in[:, 0:544], writes=[cb], eng="pool")
        k.dma(cb[:, 544:1088], cst_in[:, 544:1088], writes=[cb], eng="pool")
        rpb = k.sb("rpb", [128, 128], BF16)
        k.dma(rpb[:, :], cst_in[:, 2176:2304], writes=[rpb], eng="pool")
        k.dma(w2s[:, :, :, :], w2_in, writes=[w2s], eng="pool")
        k.dma(a2s[:, :, :, :], a2_in, writes=[a2s], eng="pool")
        k.dma(linkt[:, 0:1], link_in, writes=[linkt])
        k.dma(sinkr[:, :], sink_in, writes=[sinkr])
        k.dma(fgt[:, :], fg_in, writes=[fgt])
        memset("pool", zt[:, :], 0.0, [zt])
        act(sinkr[:, :], sinkr[:, :], AF.Exp, [sinkr], [sinkr])
        vtm_b = Buf("vtm")
        proj_b = Buf("proj")
        ys_b = Buf("ys")
        bs_b = Buf("bs")
        att_b = Buf("att")
        x1_b = Buf("x1")
        yout_b = Buf("yout")
        outs = []
        for side in range(2):
            base = 0 if side == 0 else PAD + TC
            for j in range(PAD // 128):
                k.dma(vtm[base + j * 128: base + (j + 1) * 128, :], zt[:, :], reads=[zt], writes=[vtm_b])
        IDN = lambda t, p0, p1, c0, c1: t[p0:p1, c0:c1]
        ts("dve", prm2[:, 0:16], prm[:, 16:32], -1.0, 1.0, ALU.mult, ALU.add, [prm], [prm2])
        ts("dve", prm2[:, 16:32], prm[:, 16:32], 0.5, None, ALU.mult, None, [prm], [prm2])
        lnb = k.sb("lnb", [128, 8], F32)
        ts("dve", lnb[:, 0:4], prm[:, 52:56], -1.0, 1.0, ALU.mult, ALU.add, [prm], [lnb])
        cp("dve", linkt[:, 1:2], linkt[:, 0:1], [linkt], [linkt])
        cp("dve", linkt[:, 2:3], linkt[:, 0:1], [linkt], [linkt])
        memset("dve", linkt[64:128, 1:2], 1.0, [linkt])
        memset("dve", linkt[0:64, 2:3], 1.0, [linkt])
        cmask = k.sb("cmask", [128, 2], F32)
        memset("pool", cmask[:, :], 1.0, [cmask])
        memset("pool", cmask[0:64, 0:1], 0.0, [cmask])
        memset("pool", cmask[64:128, 1:2], 0.0, [cmask])
        lnb_in = din("lnb", [128, 4])
        k.dma(lnb[:, 4:8], lnb_in, writes=[lnb])

        I128b = cb[:, 0:128]
        BONES = cst[:, 192:320]
        BAVG = cst[:, 320:448]

        qm = k.sb("qmask", [64, 2, 3, 4, 128], BF16)
        for e in range(2):
            b0 = 704 + e * 192
            for j in range(4):
                memset("pool", qm[:, e, 0, j, 0:64], 1.0, [qm])
                cp("pool", qm[:, e, 0, j, 64:128], cst[0:64, b0:b0 + 64], [cst], [qm])
                cp("pool", qm[:, e, 1, j, 0:64], cst[0:64, b0:b0 + 64], [cst], [qm])
                cp("pool", qm[:, e, 1, j, 64:128], cst[0:64, b0 + 64:b0 + 128], [cst], [qm])
                cp("pool", qm[:, e, 2, j, 0:64], cst[0:64, b0 + 128:b0 + 192], [cst], [qm])
                memset("pool", qm[:, e, 2, j, 64:128], 1.0, [qm])
        bm = k.sb("bmask", [128, 2, 4, 128], BF16)
        for j in range(4):
            cp("pool", bm[:, 0, j, :], cst[:, 448:576], [cst], [bm])
            cp("pool", bm[:, 1, j, :], cst[:, 576:704], [cst], [bm])
        sel65 = k.sb("sel65", [65, 64], F32)
        memset("pool", sel65[:, :], 0.0, [sel65])
        memset("pool", sel65[64:65, :], 1.0, [sel65])
        ones1 = k.sb("ones1", [1, 64], F32)
        memset("pool", ones1[:, :], 1.0, [ones1])

        k.phase_begin()

        for L in range(NLAYER):
            x_src = x_in if L == 0 else x1d
            last = (L == NLAYER - 1)
            k.barrier()
            k.phase_begin()
            WCH = [(0, 1280), (1280, 2560), (2560, NCOL)]
            wbufs = [k.sb(f"wbuf{j}", [128, 8, b_ - a_], BF16) for j, (a_, b_) in enumerate(WCH)]
            for j, (a_, b_) in enumerate(WCH):
                for c in range(8):
                    k.dma(wbufs[j][:, c, :], win_in[L, c * 128:(c + 1) * 128, a_:b_], writes=[wbufs[j]], eng="pool")

            def wcols(c, a_, b_):
                for j, (lo_, hi_) in enumerate(WCH):
                    if lo_ <= a_ and b_ <= hi_:
                        return wbufs[j], wbufs[j][:, c, a_ - lo_:b_ - lo_]
                raise AssertionError((a_, b_))

            xt = [k.sb(f"xt{i}", [128, D], F32) for i in range(4)]
            xs = [k.sb(f"xs{i}", [128, D], F32) for i in range(4)]
            junk = k.sb("junk", [128, D], BF16)
            ss = k.sb("ss", [128, 16], F32)
            hT = [k.sb(f"hT{i}", [128, 8, 512], BF16) for i in range(2)]
            gb = [k.sb(f"gb{i}", [128, 512], BF16) for i in range(6)]
            rt1 = [k.sb(f"rt1_{i}", [128, 512], F32) for i in range(2)]
            rt2 = [k.sb(f"rt2_{i}", [128, 512], F32) for i in range(2)]
            cst_t = [k.sb(f"cos{i}", [128, 512], F32) for i in range(2)]
            snt_t = [k.sb(f"sin{i}", [128, 512], F32) for i in range(2)]
            vtb = [k.sb(f"vtb{i}", [128, 384], BF16) for i in range(2)]
            xcnt = [0]

            def p1_norm(i):
                h = hT[i % 2]
                xas = []
                for m in range(4):
                    xa = xt[m]
                    xb = xs[m]
                    r0 = i * 512 + m * 128
                    k.dma(xa[:, :], x_src[r0:r0 + 128, :], reads=[x1_b] if L else [], writes=[xa])
                    sc = ss[:, m * 4:m * 4 + 1]
                    act(junk[:, :], xa[:, :], AF.Square, [xa], [junk, ss], accum=sc)
                    s2 = ss[:, m * 4 + 1:m * 4 + 2]
                    ts("dve", s2, sc, 1.0 / D, 1e-5, ALU.mult, ALU.add, [ss], [ss])
                    act(s2, s2, AF.Sqrt, [ss], [ss])
                    s3 = ss[:, m * 4 + 2:m * 4 + 3]
                    recip(s3, s2, [ss], [ss])
                    ts("dve", xb[:, :], xa[:, :], s3, None, ALU.mult, None, [xa, ss], [xb])
                    yield
                for c in range(8):
                    bk = banks[c % 2]
                    for m in range(4):
                        xb = xs[m]
                        k.op("pe", lambda e, bk=bk, c=c, m=m, xb=xb: e.transpose(
                            bk[:, m * 128:(m + 1) * 128], xb[:, c * 128:(c + 1) * 128], cst[:, 0:128]),
                            [xb, cst], [bk])
                    ts("dve", h[:, c, :], bk[:, :], prm[:, L * 8 + c:L * 8 + c + 1], None, ALU.mult, None, [bk, prm], [h])
                    yield

            gcnt = [0]
            bcnt = [0]
            groups = [([g], g, g >= 16) for g in range(25)]
            qbs = [k.sb(f"qb{i}", [128, 512], BF16) for i in range(2)]

            def p1_proj(i):
                h = hT[i % 2]
                t0 = i * 512
                ct = cst_t[i % 2]
                st_ = snt_t[i % 2]
                k.dma(ct[:, :], cs_in[:, t0:t0 + 512], writes=[ct])
                k.dma(st_[:, :], sn_in[:, t0:t0 + 512], writes=[st_])
                pending = [None]
                for (wg, slot, rope) in groups:
                    bks = []
                    for wgi in wg:
                        bk = banks[2 + bcnt[0] % 4]
                        bcnt[0] += 1
                        for c in range(8):
                            wt_, wap_ = wcols(c, wgi * 128, (wgi + 1) * 128)
                            mm(bk[:, :], wap_, h[:, c, :], [wt_, h], [bk],
                               start=(c == 0), stop=(c == 7))
                        bks.append(bk)
                    if pending[0] is not None:
                        pending[0]()
                        pending[0] = None
                    o = gb[gcnt[0] % 6]
                    gcnt[0] += 1
                    if not rope:
                        cp("act", o[:, :], bks[0][:, :], [bks[0]], [o])
                        k.dma(proj[slot, :, t0:t0 + 512], o[:, :], reads=[o], writes=[proj_b])
                    else:
                        qb = qbs[gcnt[0] % 2]
                        cp("act", qb[:, :], bks[0][:, :], [bks[0]], [qb])

                        def tail(qb=qb, o=o, slot=slot, gi=gcnt[0]):
                            bk2 = banks[2 + bcnt[0] % 4]
                            bcnt[0] += 1
                            mm(bk2[:, :], rpb[:, :], qb[:, :], [rpb, qb], [bk2])
                            a = rt1[gi % 2]
                            b = rt2[gi % 2]
                            tt("pool", a[:, :], qb[:, :], ct[:, :], ALU.mult, [qb, ct], [a])
                            tt("dve", b[:, :], bk2[:, :], st_[:, :], ALU.mult, [bk2, st_], [b])
                            tt("pool", o[:, :], a[:, :], b[:, :], ALU.add, [a, b], [o])
                            k.dma(proj[slot, :, t0:t0 + 512], o[:, :], reads=[o], writes=[proj_b])
                        pending[0] = tail
                    yield
                if pending[0] is not None:
                    pending[0]()
                    pending[0] = None
                for m in range(4):
                    bk = banks[6 + m % 2]
                    for c in range(8):
                        wt_, wap_ = wcols(c, 25 * 128, NCOL)
                        mm(bk[:, 0:384], h[:, c, m * 128:(m + 1) * 128], wap_, [wt_, h], [bk],
                           start=(c == 0), stop=(c == 7))
                    vb = vtb[m % 2]
                    cp("act", vb[:, :], bk[:, 0:384], [bk], [vb])
                    r0 = PAD + t0 + m * 128
                    k.dma(vtm[r0:r0 + 128, :], vb[:, :], reads=[vb], writes=[vtm_b])
                    yield

            for _ in (p1_norm(0) if stop not in ("x3", "x4") else []):
                pass
            for i in range(NT if stop not in ("x3", "x4") else 0):
                side = p1_norm(i + 1) if i + 1 < NT else None
                stepn = 0
                for _ in p1_proj(i):
                    stepn += 1
                    if side is not None and stepn % 2 == 0:
                        try:
                            next(side)
                        except StopIteration:
                            side = None
                if side is not None:
                    for _ in side:
                        pass

            if stop == "p1":
                break
            k.barrier()
            k.phase_begin()
            raw = k.sb("raw", [128, 8, 514], BF16)
            sh = k.sb("shf", [128, 8, 512], BF16)
            tA = [k.sb(f"tA{i}", [128, 512], F32) for i in range(2)]
            tB = [k.sb(f"tB{i}", [128, 512], F32) for i in range(2)]
            tw = k.sb("tw", [64, 512], BF16)
            lal = k.sb("lal", [64, 512], BF16)
            sg = k.sb("sg", [128, 512], F32)
            Ssc = k.sb("Ssc", [128, 512], F32)
            Sm = k.sb("Sm", [128, 512], F32)
            eP = k.sb("eP", [128, 512], F32)
            eN = k.sb("eN", [128, 512], F32)
            ePm = k.sb("ePm", [128, 512], F32)
            av = k.sb("av", [128, 512], F32)
            kq = k.sb("kq", [128, 512], F32)
            sq = sg
            nrm = Ssc
            kk = k.sb("kk", [128, 512], F32)
            kka = k.sb("kka", [128, 512], F32)
            ka = Sm
            keff = k.sb("keff", [128, 512], F32)
            rke = kq
            bon = [k.sb(f"bon{i}", [128, 512], F32) for i in range(2)]
            T1 = [[k.sb(f"T1_{s_}{i}", [128, 8, 192], BF16) for i in range(2)] for s_ in range(2)]
            T2 = [[k.sb(f"T2_{s_}{i}", [128, 8, 192], BF16) for i in range(2)] for s_ in range(2)]
            T1o = [[k.sb(f"T1o_{s_}{i}", [64, 8, 192], BF16) for i in range(2)] for s_ in range(2)]
            T2o = [[k.sb(f"T2o_{s_}{i}", [64, 8, 192], BF16) for i in range(2)] for s_ in range(2)]
            vT = [[k.sb(f"vT{s_}{i}", [64, 8, 128], BF16) for i in range(2)] for s_ in range(2)]
            NSL = 4
            bank_lo = [Buf(f"lo{i}") for i in range(8)]
            bank_hi = [Buf(f"hi{i}") for i in range(8)]
            Xb = [k.sb(f"X{i}", [64, 4, 128], BF16) for i in range(2 * NSL)]
            PNb = [k.sb(f"PN{i}", [64, 4, 128], BF16) for i in range(2 * NSL)]
            RBs = [k.sb(f"RB{i}", [64, 4, 128], BF16) for i in range(NSL)]
            RKs = [k.sb(f"RK{i}", [64, 4, 128], BF16) for i in range(NSL)]
            QYs = [k.sb(f"QY{i}", [64, 4, 128], BF16) for i in range(NSL)]
            MGs = [k.sb(f"MG{i}", [64, 4, 128], BF16) for i in range(NSL)]
            Ysb = [k.sb(f"Ys{i}", [64, 4, 64], F32) for i in range(NSL)]
            Hs = [[[k.sb(f"H{e}{h}{i}", [64, 64], BF16) for i in range(2)] for h in range(4)] for e in range(2)]
            hcnt = [[0] * 4 for _ in range(2)]
            for e in range(2):
                for h in range(4):
                    memset("pool", Hs[e][h][0][:, :], 0.0, [Hs[e][h][0]])
            for s_ in range(2):
                for t_ in T1[s_]:
                    for c in range(8):
                        cp("pool", t_[:, c, 0:64], cb[:, 128:192], [cb], [t_])

            def rwkv_prep(i, e, st):
                t0 = i * 512
                rw = raw
                lo = 1 if i == 0 else 0
                hi = 513 if i == NT - 1 else 514
                k.dma(rw[:, :, lo:hi], proj[0:8, :, t0 - 1 + lo:t0 - 1 + hi].rearrange("g p n -> p g n"),
                      reads=[proj_b], writes=[rw])
                if i == 0:
                    memset("pool", rw[:, :, 0:1], 0.0, [rw])
                if i == NT - 1:
                    memset("pool", rw[:, :, 513:514], 0.0, [rw])
                if t0 == HALF:
                    ts("dve", rw[:, :, 0:1], rw[:, :, 0:1], linkt[:, 0:1], None, ALU.mult, None, [rw, linkt], [rw])
                if t0 + 512 == HALF:
                    ts("dve", rw[:, :, 513:514], rw[:, :, 513:514], linkt[:, 0:1], None, ALU.mult, None, [rw, linkt], [rw])
                yield
                for s in range(8):
                    a_ = tA[s % 2]
                    b_ = tB[s % 2]
                    tt("pool", a_[:, :], rw[:, s, 0:512], rw[:, s, 2:514], ALU.add, [rw], [a_])
                    act(b_[:, :], rw[:, s, 1:513], AF.Copy, [rw, prm2], [b_], scale=prm2[:, L * 8 + s:L * 8 + s + 1])
                    stt(sh[:, s, :], a_[:, :], prm2[:, 16 + L * 8 + s:16 + L * 8 + s + 1], b_[:, :], ALU.mult, ALU.add,
                        [a_, b_, prm2], [sh])
                    if s % 2:
                        yield
                act(tw[:, :], sh[e * 64:(e + 1) * 64, 6, :], AF.Tanh, [sh], [tw])
                cp("act", lal[:, :], sh[e * 64:(e + 1) * 64, 7, :], [sh], [lal])
                rm = cst[:, 1152:1664] if e == 0 else cst[:, 1664:2176]
                for hh in range(2):
                    r_ = sh[:, hh, :]
                    k_ = sh[:, 2 + hh, :]
                    v_ = sh[:, 4 + hh, :]
                    t1 = T1[st][hh]
                    t2 = T2[st][hh]
                    bk = banks[0]
                    mm(bk[:, :], w2s[:, L, e, hh * 128:(hh + 1) * 128], tw[:, :], [w2s, tw], [bk])
                    cidx = 32 + L * 4 + e * 2 + hh
                    act(sg[:, :], bk[:, :], AF.Sigmoid, [bk, prm], [sg], bias=prm[:, cidx:cidx + 1], scale=1.0)
                    mm(bk[:, :], a2s[:, L, e, hh * 128:(hh + 1) * 128], lal[:, :], [a2s, lal], [bk])
                    act(av[:, :], bk[:, :], AF.Sigmoid, [bk, prm], [av], bias=prm[:, cidx + 8:cidx + 9], scale=1.0)
                    yield
                    if e == 0:
                        k.op("dve", lambda en: en.tensor_tensor_scan(out=Ssc[:, :], data0=rm, data1=sg[:, :], initial=0.0,
                                                                    op0=ALU.mult, op1=ALU.add), [sg, cst], [Ssc])
                    else:
                        k.op("dve", lambda en: en.tensor_tensor_scan(out=rev(Ssc[:, :]), data0=rev(rm), data1=rev(sg[:, :]),
                                                                    initial=0.0, op0=ALU.mult, op1=ALU.add), [sg, cst], [Ssc])
                    tt("pool", Sm[:, :], Ssc[:, :], sg[:, :], ALU.subtract, [Ssc, sg], [Sm])
                    act(eP[:, :], Ssc[:, :], AF.Exp, [Ssc], [eP], scale=-C0)
                    act(eN[:, :], Ssc[:, :], AF.Exp, [Ssc], [eN], scale=C0)
                    act(ePm[:, :], Sm[:, :], AF.Exp, [Sm], [ePm], scale=-C0)
                    yield
                    ts("dve", kq[:, :], k_, prm[:, 48 + L * 2 + hh:48 + L * 2 + hh + 1], None, ALU.mult, None, [sh, prm], [kq])
                    act(sq[:, :], kq[:, :], AF.Square, [kq], [sq])
                    mm(bk[:, :], BONES, sq[:, :], [cst, sq], [bk])
                    act(nrm[:, :], bk[:, :], AF.Sqrt, [bk], [nrm])
                    yield
                    ts("dve", nrm[:, :], nrm[:, :], 1e-12, None, ALU.max, None, [nrm], [nrm])
                    recip(nrm[:, :], nrm[:, :], [nrm], [nrm])
                    tt("pool", kk[:, :], kq[:, :], nrm[:, :], ALU.mult, [kq, nrm], [kk])
                    tt("pool", kka[:, :], kk[:, :], av[:, :], ALU.mult, [kk, av], [kka])
                    yield
                    ts("dve", ka[:, :], av[:, :], prm[:, 52 + L * 2 + hh:52 + L * 2 + hh + 1], lnb[:, L * 2 + hh:L * 2 + hh + 1],
                       ALU.mult, ALU.add, [av, prm, lnb], [ka])
                    tt("pool", keff[:, :], k_, ka[:, :], ALU.mult, [sh, ka], [keff])
                    v3 = lambda ap_: ap_.rearrange("p (c n) -> p c n", n=64)
                    stt(t2[:, :, 0:64], v3(kk[:, :]), -1.0, v3(ePm[:, :]), ALU.mult, ALU.mult, [kk, ePm], [t2])
                    tt("dve", t1[:, :, 128:192], v3(kka[:, :]), v3(eN[:, :]), ALU.mult, [kka, eN], [t1])
                    yield
                    tt("dve", t1[:, :, 64:128], v3(keff[:, :]), v3(eN[:, :]), ALU.mult, [keff, eN], [t1])
                    tt("dve", t2[:, :, 64:128], v3(r_), v3(eP[:, :]), ALU.mult, [sh, eP], [t2])
                    for c in range(8):
                        col = c * 64 + (63 if e == 0 else 0)
                        ts("pool", t2[:, c, 128:192], cst[:, 128:192], eP[:, col:col + 1], None, ALU.mult, None, [cst, eP], [t2])
                    yield
                    stt(rke[:, :], r_, prm[:, 56 + L * 2 + hh:56 + L * 2 + hh + 1], keff[:, :], ALU.mult, ALU.mult,
                        [sh, prm, keff], [rke])
                    mm(bk[:, :], BONES, rke[:, :], [cst, rke], [bk])
                    bo = bon[hh]
                    tt("dve", bo[:, :], bk[:, :], v_, ALU.mult, [bk, sh], [bo])
                    k.dma(bsd[e, hh * 128:(hh + 1) * 128, t0:t0 + 512], bo[:, :], reads=[bo], writes=[bs_b])
                    yield
                    vt = vT[st][hh]
                    for q in range(2):
                        b1 = banks[1]
                        for j in range(4):
                            c = q * 4 + j
                            mm(b1[0:64, j * 128:(j + 1) * 128], sh[:, 4 + hh, c * 64:(c + 1) * 64], I128b, [sh, cb], [b1])
                        cp("act", vt[:, q * 4:(q + 1) * 4, :], b1[0:64, :].rearrange("p (j n) -> p j n", n=128), [b1], [vt])
                        yield
                    cp("act", T1o[st][hh][:, :, :], t1[64:128, :, :], [t1], [T1o[st][hh]])
                    cp("dve", T2o[st][hh][:, :, :], t2[64:128, :, :], [t2], [T2o[st][hh]])
                    yield

            def quad(i, e, hh, pp, q, st):
                def gen(slot):
                    hd = hh * 2 + pp
                    if pp == 1:
                        t1, t2 = T1o[st][hh], T2o[st][hh]
                    else:
                        t1, t2 = T1[st][hh], T2[st][hh]
                    vt = vT[st][hh]
                    cs_ = [q * 4 + j for j in range(4)]
                    at = lambda c: t2[0:64, c, 0:64]
                    rtc = lambda c: t2[0:64, c, 64:128]
                    dW = lambda c: t2[0:64, c, 128:192]
                    Ic = lambda c: t1[0:64, c, 0:64]
                    ktc = lambda c: t1[0:64, c, 64:128]
                    btc = lambda c: t1[0:64, c, 128:192]
                    bk = banks[2 + slot]
                    LO = bank_lo[2 + slot]
                    HI = bank_hi[2 + slot]
                    lo4 = bk[0:64, :].rearrange("p (j n) -> p j n", n=128)
                    hi4 = bk[64:128, :].rearrange("p (j n) -> p j n", n=128)
                    for j, c in enumerate(cs_):
                        mm(bk[0:64, j * 128:(j + 1) * 128], at(c), t1[0:64, c, 0:128], [t1, t2], [LO])
                        mm(bk[64:128, j * 128:j * 128 + 64], at(c), btc(c), [t1, t2], [HI])
                        mm(bk[64:128, j * 128 + 64:(j + 1) * 128], btc(c), at(c), [t1, t2], [HI])
                    X = Xb[slot * 2]
                    X2 = Xb[slot * 2 + 1]
                    PN = PNb[slot * 2]
                    PN2 = PNb[slot * 2 + 1]
                    RB = RBs[slot]
                    RK = RKs[slot]
                    tt("dve", X[:, :, :], lo4, qm[:, e, 0, :, :], ALU.mult, [LO, qm], [X])
                    cp("act", PN2[:, :, :], hi4, [HI], [PN2])
                    tt("pool", PN[:, :, :], PN2[:, :, :], qm[:, e, 1, :, :], ALU.mult, [PN2, qm], [PN])
                    yield
                    for lv in range(6):
                        for j in range(4):
                            mm(bk[0:64, j * 128:(j + 1) * 128], PN[:, j, 64:128], X[:, j, :], [PN, X], [LO])
                            if lv < 5:
                                mm(bk[64:128, j * 128:j * 128 + 64], PN[:, j, 64:128], PN[:, j, 0:64], [PN], [HI])
                                mm(bk[64:128, j * 128 + 64:(j + 1) * 128], PN[:, j, 0:64], PN[:, j, 64:128], [PN], [HI])
                        tt("dve", X2[:, :, :], lo4, X[:, :, :], ALU.add, [LO, X], [X2])
                        if lv < 5:
                            cp("act", PN2[:, :, :], hi4, [HI], [PN2])
                        X, X2 = X2, X
                        PN, PN2 = PN2, PN
                        yield
                    for j, c in enumerate(cs_):
                        mm(bk[0:64, j * 128:(j + 1) * 128], btc(c), t2[0:64, c, 64:192], [t1, t2], [LO])
                        mm(bk[64:128, j * 128:(j + 1) * 128], ktc(c), t2[0:64, c, 64:192], [t1, t2], [HI])
                    tt("dve", RB[:, :, :], lo4, qm[:, e, 2, :, :], ALU.mult, [LO, qm], [RB])
                    cp("act", PN2[:, :, :], hi4, [HI], [PN2])
                    tt("pool", RK[:, :, :], PN2[:, :, :], qm[:, e, 2, :, :], ALU.mult, [PN2, qm], [RK])
                    yield
                    QY = QYs[slot]
                    MG = MGs[slot]
                    Itok = cb[0:64, 0:64]
                    for j, c in enumerate(cs_):
                        o = j * 128
                        mm(bk[0:64, o:o + 128], Ic(c), t2[0:64, c, 64:192], [t1, t2], [LO], start=True, stop=False)
                        mm(bk[0:64, o:o + 128], X[:, j, 0:64], RB[:, j, :], [X, RB], [LO], start=False, stop=True)
                        mm(bk[64:128, o:o + 128], X[:, j, 64:128], RB[:, j, :], [X, RB], [HI], start=True, stop=False)
                        mm(bk[64:128, o:o + 128], Itok, RK[:, j, :], [cb, RK], [HI], start=False, stop=True)
                    cp("act", QY[:, :, :], lo4, [LO], [QY])
                    cp("act", MG[:, :, :], hi4, [HI], [MG])
                    yield
                    order = range(4) if e == 0 else range(3, -1, -1)
                    Ys = Ysb[slot]
                    for j in order:
                        c = cs_[j]
                        hc = hcnt[e][hd]
                        H = Hs[e][hd][hc % 2]
                        Hn = Hs[e][hd][(hc + 1) % 2]
                        hcnt[e][hd] += 1
                        tok = i * 512 + c * 64
                        if (e == 0 and tok == HALF) or (e == 1 and tok + 64 == HALF):
                            ts("dve", H[:, :], H[:, :], linkt[0:64, 0:1], None, ALU.mult, None, [H, linkt], [H])
                        vv = vt[:, c, pp * 64:(pp + 1) * 64]
                        mm(bk[0:64, j * 64:(j + 1) * 64], vv, MG[:, j, 0:64], [vt, MG], [LO], start=True, stop=False)
                        mm(bk[0:64, j * 64:(j + 1) * 64], H[:, :], QY[:, j, 0:64], [H, QY], [LO], start=False, stop=True)
                        mm(bk[64:128, 0:64], MG[:, j, 64:128], vv, [MG, vt], [HI], start=True, stop=False)
                        mm(bk[64:128, 0:64], QY[:, j, 64:128], H[:, :], [QY, H], [HI], start=False, stop=True)
                        cp("dve", Hn[:, :], bk[64:128, 0:64], [HI], [Hn])
                        yield
                    cp("act", Ys[:, :, :], bk[0:64, 0:256].rearrange("p (j n) -> p j n", n=64), [LO], [Ys])
                    t0 = i * 512 + q * 256
                    k.dma(ysd[e, hd * 64:(hd + 1) * 64, t0:t0 + 256], Ys[:, :, :].rearrange("p j n -> p (j n)"),
                          reads=[Ys], writes=[ys_b])
                return gen

            def run_side(factories, nslots, side):
                active = []
                free = list(range(nslots))
                it = iter(factories)
                done = False
                while active or not done:
                    while free and not done:
                        try:
                            f_ = next(it)
                        except StopIteration:
                            done = True
                            break
                        sl_ = free.pop(0)
                        active.append((sl_, f_(sl_)))
                    if side is not None:
                        try:
                            next(side)
                        except StopIteration:
                            side = None
                    for (sl_, g_) in list(active):
                        try:
                            next(g_)
                        except StopIteration:
                            active.remove((sl_, g_))
                            free.append(sl_)
                if side is not None:
                    for _ in side:
                        pass

            seq = []
            for i in range(NT):
                seq.append((i, 0))
                seq.append((NT - 1 - i, 1))
            if stop == "p2c":
                seq = [(i, 0) for i in range(NT)]
            if stop in ("x3", "x4"):
                seq = []
            for _ in (rwkv_prep(seq[0][0], seq[0][1], 0) if seq else []):
                pass
            for n_, (i, e) in enumerate(seq):
                st = n_ % 2
                qorder = [0, 1] if e == 0 else [1, 0]
                facs = [quad(i, e, hh, pp, q, st) for q in qorder for hh in range(2) for pp in range(2)]
                side = None
                if n_ + 1 < len(seq):
                    side = rwkv_prep(seq[n_ + 1][0], seq[n_ + 1][1], (n_ + 1) % 2)
                run_side(facs, NSL, side)
            if stop in ("p2a", "p2b", "p2c", "p2d", "p2e"):
                break

            if stop == "p2":
                break
            k.barrier()
            k.phase_begin()
            def run_pipelined(factories, nslots):
                active = []
                free = list(range(nslots))
                it = iter(factories)
                done = False
                while active or not done:
                    while free and not done:
                        try:
                            f_ = next(it)
                        except StopIteration:
                            done = True
                            break
                        sl_ = free.pop(0)
                        active.append((sl_, f_(sl_)))
                    for (sl_, g_) in list(active):
                        try:
                            next(g_)
                        except StopIteration:
                            active.remove((sl_, g_))
                            free.append(sl_)

            qTb = k.sb("qTb", [64, 4, WIN], BF16)
            kTb = k.sb("kTb", [64, 4, WIN + 2048], BF16)
            acc = k.sb("acc", [65, 4, WIN], F32)
            Vb = [k.sb(f"Vb{i}", [128, 4, 65], BF16) for i in range(8)]
            Pt = [k.sb(f"Pt{i}", [128, 512], BF16) for i in range(6)]
            rden = [k.sb(f"rden{i}", [64, 512], F32) for i in range(2)]
            ocb = [k.sb(f"ocb{i}", [64, 512], BF16) for i in range(2)]
            for v_ in Vb:
                memset("pool", v_[:, :, :], 1.0, [v_])
            ncnt = [0]
            NW = TC // WIN
            for w in range(NW if stop != "x4" else 0):
                w0 = w * WIN
                k.dma(qTb[:, :, :], proj[16:18, :, w0:w0 + WIN].rearrange("g (h p) n -> p (g h) n", p=64), reads=[proj_b], writes=[qTb])
                lo = max(0, w0 - 1024)
                hi = min(TC, w0 + WIN + 1024)
                k.dma(kTb[:, :, lo - (w0 - 1024):hi - (w0 - 1024)], proj[18:20, :, lo:hi].rearrange("g (h p) n -> p (g h) n", p=64),
                      reads=[proj_b], writes=[kTb])
                if w0 - 1024 < 0:
                    memset("pool", kTb[:, :, 0:1024], 0.0, [kTb])
                if w0 + WIN + 1024 > TC:
                    memset("pool", kTb[:, :, WIN + 1024:WIN + 2048], 0.0, [kTb])
                combos = []
                for di, d in enumerate((1, 4, 16)):
                    nb = WIN // (128 * d)
                    for r in range(d):
                        for b in range(nb):
                            combos.append((di, d, nb, r, b))

                def v_issue(n, w0=w0, combos=combos):
                    if n >= len(combos):
                        return
                    di, d, nb, r, b = combos[n]
                    for tile in range(2):
                        kpos0 = 128 * b - 64 + 128 * tile
                        rs = PAD + w0 + r + d * kpos0
                        vb = Vb[(n % 4) * 2 + tile]
                        k.dma(vb[:, :, 0:64], vtm[rs:rs + 127 * d + 1:d, 0:256].rearrange("p (h c) -> p h c", c=64),
                              reads=[vtm_b], writes=[vb])

                def b_unit(n, w0=w0, combos=combos):
                    def gen(slot):
                        di, d, nb, r, b = combos[n]
                        qc0 = r + d * 128 * b
                        v_issue(n + 2)
                        pts = []
                        vbs = []
                        sbks = []
                        for tile in range(2):
                            kpos0 = 128 * b - 64 + 128 * tile
                            kc0 = 1024 + r + d * kpos0
                            sbk = banks[3 * slot + tile]
                            for h in range(4):
                                mm(sbk[:, h * 128:(h + 1) * 128], kTb[:, h, kc0:kc0 + 127 * d + 1:d],
                                   qTb[:, h, qc0:qc0 + 127 * d + 1:d], [kTb, qTb], [sbk])
                            sbks.append(sbk)
                            pts.append(Pt[3 * slot + tile])
                            vbs.append(Vb[(n % 4) * 2 + tile])
                        yield
                        for tile in range(2):
                            pt = pts[tile]
                            act(pt[:, :], sbks[tile][:, :], AF.Exp, [sbks[tile]], [pt], scale=0.125)
                            colm = None
                            if tile == 0 and b == 0:
                                if w0 == 0:
                                    colm = (cmask, cmask[:, 0:1])
                                elif w0 == HALF:
                                    colm = (linkt, linkt[:, 1:2])
                            if tile == 1 and b == nb - 1:
                                if w0 + WIN == TC:
                                    colm = (cmask, cmask[:, 1:2])
                                elif w0 + WIN == HALF:
                                    colm = (linkt, linkt[:, 2:3])
                            band = bm[:, tile, :, :].rearrange("p a n -> p (a n)")
                            if colm is None:
                                tt("dve", pt[:, :], pt[:, :], band, ALU.mult, [pt, bm], [pt])
                            else:
                                stt(pt[:, :], pt[:, :], colm[1], band, ALU.mult, ALU.mult, [pt, bm, colm[0]], [pt])
                        yield
                        ob = banks[6 + slot]
                        for h in range(4):
                            for tile in range(2):
                                mm(ob[0:65, h * 128:(h + 1) * 128], vbs[tile][:, h, :], pts[tile][:, h * 128:(h + 1) * 128],
                                   [vbs[tile], pts[tile]], [ob], start=(tile == 0), stop=(tile == 1))
                        yield
                        accv = acc[0:65, :, qc0:qc0 + 127 * d + 1:d]
                        obv = ob[0:65, :].rearrange("p (h n) -> p h n", n=128)
                        if di == 0:
                            cp("act", accv, obv, [ob], [acc])
                        else:
                            tt("dve", accv, accv, obv, ALU.add, [ob, acc], [acc])
                    return gen

                v_issue(0)
                v_issue(1)
                run_pipelined([b_unit(n) for n in range(len(combos))], 2)
                for h in range(4):
                    for ch in range(WIN // 512):
                        n = ncnt[0]
                        ncnt[0] += 1
                        db = banks[2 + 3 * (n % 2)]
                        mm(db[0:64, :], sel65[:, :], acc[0:65, h, ch * 512:(ch + 1) * 512], [sel65, acc], [db])
                        rd = rden[n % 2]
                        recip(rd[:, :], db[0:64, :], [db], [rd])
                        oc_ = ocb[n % 2]
                        tt("pool", oc_[:, :], acc[0:64, h, ch * 512:(ch + 1) * 512], rd[:, :], ALU.mult, [acc, rd], [oc_])
                        k.dma(att[h // 2, (h % 2) * 64:(h % 2) * 64 + 64, w0 + ch * 512:w0 + (ch + 1) * 512], oc_[:, :],
                              reads=[oc_], writes=[att_b])
            if stop == "p3b":
                break
            qTc = [k.sb(f"qTc{i}", [64, 2, 4, 512], BF16) for i in range(2)]
            kTc = [k.sb(f"kTc{i}", [64, 2, 768], BF16) for i in range(2)]
            Vc = [k.sb(f"Vc{i}", [128, 6, 2, 65], BF16) for i in range(2)]
            Osb = [k.sb(f"Osb{i}", [65, 512], F32) for i in range(2)]
            occ = [[k.sb(f"occ{i}{g_}", [64, 4, 512], BF16) for g_ in range(2)] for i in range(2)]
            for v_ in Vc:
                memset("pool", v_[:, :, :, :], 1.0, [v_])

            def c_load(sbi):
                t0 = sbi * 512
                qt = qTc[sbi % 2]
                kt_ = kTc[sbi % 2]
                vc = Vc[sbi % 2]
                for g_ in range(2):
                    k.dma(qt[:, g_, :, :], proj[20:24, g_ * 64:(g_ + 1) * 64, t0:t0 + 512].rearrange("a p n -> p a n"),
                          reads=[proj_b], writes=[qt])
                lo = max(0, t0 - 128)
                hi = min(TC, t0 + 640)
                k.dma(kt_[:, :, lo - (t0 - 128):hi - (t0 - 128)], proj[24, :, lo:hi].rearrange("(g p) n -> p g n", p=64),
                      reads=[proj_b], writes=[kt_])
                if t0 - 128 < 0:
                    memset("pool", kt_[:, :, 0:128], 0.0, [kt_])
                if t0 + 640 > TC:
                    memset("pool", kt_[:, :, 640:768], 0.0, [kt_])
                for kt6 in range(6):
                    rs = PAD + t0 - 128 + kt6 * 128
                    k.dma(vc[:, kt6, :, 0:64], vtm[rs:rs + 128, 256:384].rearrange("p (g c) -> p g c", c=64),
                          reads=[vtm_b], writes=[vc])

            sinkb = [k.sb(f"sinkb{g_}", [64, 512], F32) for g_ in range(2)]
            for g_ in range(2):
                so = (L * 2 + g_) * 512
                mm(banks[0][0:64, :], ones1[:, :], sinkr[0:1, so:so + 512], [ones1, sinkr], [banks[0]])
                cp("act", sinkb[g_][:, :], banks[0][0:64, :], [banks[0]], [sinkb[g_]])
            onesb = k.sb("onesb", [128, 64], BF16)
            memset("pool", onesb[:, :], 1.0, [onesb])
            dtmp = [k.sb(f"dtmp{i}", [64, 512], F32) for i in range(2)]

            def c_unit(sbi, jq, g_):
                def gen(slot):
                    t0 = sbi * 512
                    qt = qTc[sbi % 2]
                    kt_ = kTc[sbi % 2]
                    vc = Vc[sbi % 2]
                    if jq == 1 and g_ == 0 and sbi + 1 < NT:
                        c_load(sbi + 1)
                    qtok = t0 + 128 * jq
                    tiles = []
                    for rel in range(3):
                        ktile = jq + rel
                        ktok = t0 - 128 + 128 * ktile
                        if ktok < 0 or ktok >= TC:
                            continue
                        tiles.append((rel, ktile, ktok))
                    sb_ = [banks[4 * slot], banks[4 * slot + 1]]
                    ob = banks[4 * slot + 2]
                    db = banks[4 * slot + 3]

                    def s_mm(ti):
                        rel, ktile, ktok = tiles[ti]
                        sbk = sb_[ti % 2]
                        mm(sbk[:, :], kt_[:, g_, ktile * 128:(ktile + 1) * 128],
                           qt[:, g_, :, jq * 128:(jq + 1) * 128], [kt_, qt], [sbk])

                    def p_ev(ti):
                        rel, ktile, ktok = tiles[ti]
                        sbk = sb_[ti % 2]
                        pt = Pt[3 * slot + ti]
                        act(pt[:, :], sbk[:, :], AF.Exp, [sbk], [pt], scale=0.125)
                        cross = (ktok < HALF) != (qtok < HALF)
                        if rel != 1:
                            band = bm[:, 0 if rel == 0 else 1, :, :].rearrange("p a n -> p (a n)")
                            if cross:
                                stt(pt[:, :], pt[:, :], linkt[:, 0:1], band, ALU.mult, ALU.mult, [pt, bm, linkt], [pt])
                            else:
                                tt("dve", pt[:, :], pt[:, :], band, ALU.mult, [pt, bm], [pt])
                    for ti in range(min(2, len(tiles))):
                        s_mm(ti)
                    yield
                    for ti in range(min(2, len(tiles))):
                        p_ev(ti)
                    yield
                    if len(tiles) > 2:
                        s_mm(2)
                        yield
                        p_ev(2)
                        yield
                    for ti, (rel, ktile, ktok) in enumerate(tiles):
                        pt = Pt[3 * slot + ti]
                        mm(ob[0:64, :], vc[:, ktile, g_, 0:64], pt[:, :], [vc, pt], [ob], start=(ti == 0), stop=(ti == len(tiles) - 1))
                    for ti, (rel, ktile, ktok) in enumerate(tiles):
                        pt = Pt[3 * slot + ti]
                        mm(db[0:64, :], onesb[:, :], pt[:, :], [onesb, pt], [db], start=(ti == 0), stop=(ti == len(tiles) - 1))
                    yield
                    osb = Osb[slot]
                    cp("act", osb[0:64, :], ob[0:64, :], [ob], [osb])
                    dt_ = dtmp[slot]
                    tt("dve", dt_[:, :], db[0:64, :], sinkb[g_][:, :], ALU.add, [db, sinkb[g_]], [dt_])
                    yield
                    rd = rden[slot]
                    recip(rd[:, :], dt_[:, :], [dt_], [rd])
                    yield
                    oc_ = occ[sbi % 2][g_]
                    tt("pool", oc_[:, :, jq * 128:(jq + 1) * 128], osb[0:64, :].rearrange("p (a n) -> p a n", n=128),
                       rd[:, :].rearrange("p (a n) -> p a n", n=128), ALU.mult, [osb, rd], [oc_])
                    if jq == 3:
                        k.dma(att[2:6, g_ * 64:(g_ + 1) * 64, t0:t0 + 512].rearrange("a p n -> p a n"), oc_[:, :, :],
                              reads=[oc_], writes=[att_b])
                return gen

            if stop != "x4":
                c_load(0)
            run_pipelined([] if stop == "x4" else [c_unit(sbi, jq, g_) for sbi in range(NT) for jq in range(4) for g_ in range(2)], 2)

            k.barrier()
            k.phase_begin()
            wob = k.sb("wob", [128, 8, D], BF16)
            for c in range(8):
                k.dma(wob[:, c, :], wout_in[L, c * 128:(c + 1) * 128, :], writes=[wob], eng="pool")
            gt = [k.sb(f"gt{i}", [128, 8, 512], BF16) for i in range(2)]
            att_t = [k.sb(f"att{i}", [128, 6, 512], BF16) for i in range(2)]
            yv = [k.sb(f"yv{i}", [128, 2, 2, 512], F32) for i in range(2)]
            bv = [k.sb(f"bv{i}", [128, 2, 2, 512], F32) for i in range(2)]
            xt4 = [k.sb(f"xt4_{i}", [128, 4, D], F32) for i in range(3)]
            mix = [k.sb(f"mix{i}", [128, 8, 512], BF16) for i in range(2)]
            tYs = [[k.sb(f"tY{sl}{hh}", [128, 512], F32) for hh in range(2)] for sl in range(2)]
            tCs = [[k.sb(f"tC{sl}{hh}", [128, 512], F32) for hh in range(2)] for sl in range(2)]
            tRs = [[k.sb(f"tR{sl}{hh}", [128, 512], F32) for hh in range(2)] for sl in range(2)]
            tG = [k.sb(f"tG{i}", [128, 512], BF16) for i in range(2)]
            ss4 = k.sb("ss4", [128, 8], F32)
            xo = [k.sb(f"xo{i}", [128, D], F32) for i in range(1)]

            def p4_load_yb(i):
                t0 = i * 512
                for e in range(2):
                    k.dma(yv[i % 2][:, e, :, :], ysd[e, :, t0:t0 + 512].rearrange("(h p) n -> p h n", p=128),
                          reads=[ys_b], writes=[yv[i % 2]])
                    k.dma(bv[i % 2][:, e, :, :], bsd[e, :, t0:t0 + 512].rearrange("(h p) n -> p h n", p=128),
                          reads=[bs_b], writes=[bv[i % 2]])

            def p4_load_ga(i):
                t0 = i * 512
                k.dma(gt[i % 2][:, :, :], proj[8:16, :, t0:t0 + 512].rearrange("g p n -> p g n"), reads=[proj_b], writes=[gt[i % 2]])
                k.dma(att_t[i % 2][:, :, :], att[:, :, t0:t0 + 512].rearrange("g p n -> p g n"), reads=[att_b], writes=[att_t[i % 2]])

            def p4_load_x(i):
                t0 = i * 512
                for m in range(4):
                    k.dma(xt4[i % 3][:, m, :], x_src[t0 + m * 128:t0 + (m + 1) * 128, :], reads=[x1_b] if L else [],
                          writes=[xt4[i % 3]])

            opc = [0]

            def p4_unit(i):
                def gen(slot):
                    t0 = i * 512
                    g_ = gt[i % 2]
                    a_ = att_t[i % 2]
                    y_ = yv[i % 2]
                    b_ = bv[i % 2]
                    x_ = xt4[i % 3]
                    mx = mix[i % 2]
                    tY = tYs[slot]
                    tC = tCs[slot]
                    tR = tRs[slot]
                    bks = [banks[slot * 2], banks[slot * 2 + 1]]
                    for hh in range(2):
                        tt("pool", tY[hh][:, :], y_[:, 0, hh, :], y_[:, 1, hh, :], ALU.add, [y_], [tY[hh]])
                        mm(bks[hh][:, :], BAVG, tY[hh][:, :], [cst, tY[hh]], [bks[hh]])
                    yield
                    for hh in range(2):
                        tt("dve", tC[hh][:, :], tY[hh][:, :], bks[hh][:, :], ALU.subtract, [tY[hh], bks[hh]], [tC[hh]])
                        act(tY[hh][:, :], tC[hh][:, :], AF.Square, [tC[hh]], [tY[hh]])
                        mm(bks[hh][:, :], BAVG, tY[hh][:, :], [cst, tY[hh]], [bks[hh]])
                    yield
                    for hh in range(2):
                        ts("dve", tR[hh][:, :], bks[hh][:, :], 64e-5, None, ALU.add, None, [bks[hh]], [tR[hh]])
                        act(tR[hh][:, :], tR[hh][:, :], AF.Sqrt, [tR[hh]], [tR[hh]])
                    yield
                    for hh in range(2):
                        recip(tR[hh][:, :], tR[hh][:, :], [tR[hh]], [tR[hh]])
                        tt("pool", tC[hh][:, :], tC[hh][:, :], tR[hh][:, :], ALU.mult, [tC[hh], tR[hh]], [tC[hh]])
                        tt("pool", tR[hh][:, :], b_[:, 0, hh, :], b_[:, 1, hh, :], ALU.add, [b_], [tR[hh]])
                    if i + 2 < NT:
                        p4_load_yb(i + 2)
                    yield
                    for hh in range(2):
                        ts("dve", tC[hh][:, :], tC[hh][:, :], prm[:, 60 + L * 2 + hh:60 + L * 2 + hh + 1],
                           lnb[:, 4 + L * 2 + hh:4 + L * 2 + hh + 1], ALU.mult, ALU.add, [tC[hh], prm, lnb], [tC[hh]])
                        tt("pool", tC[hh][:, :], tC[hh][:, :], tR[hh][:, :], ALU.add, [tC[hh], tR[hh]], [tC[hh]])
                        tg = tG[hh]
                        act(tg[:, :], g_[:, hh, :], AF.Silu, [g_], [tg])
                        tt("dve", mx[:, hh, :], tC[hh][:, :], tg[:, :], ALU.mult, [tC[hh], tg], [mx])
                    yield
                    for j in range(6):
                        tg = tG[j % 2]
                        act(tg[:, :], g_[:, 2 + j, :], AF.Silu, [g_], [tg])
                        tt("pool" if j % 2 else "dve", mx[:, 2 + j, :], a_[:, j, :], tg[:, :], ALU.mult, [a_, tg], [mx])
                        if j == 5 and i + 2 < NT:
                            p4_load_ga(i + 2)
                        if j % 2:
                            yield
                    for m in range(4):
                        for hf in range(2):
                            bk = banks[4 + opc[0] % 4]
                            opc[0] += 1
                            for c in range(8):
                                mm(bk[:, :], mx[:, c, m * 128:(m + 1) * 128], wob[:, c, hf * 512:(hf + 1) * 512], [mx, wob], [bk],
                                   start=(c == 0), stop=(c == 7))
                            tt("dve", x_[:, m, hf * 512:(hf + 1) * 512], bk[:, :], x_[:, m, hf * 512:(hf + 1) * 512], ALU.add,
                               [bk, x_], [x_])
                        r0 = t0 + m * 128
                        if not last:
                            k.dma(x1d[r0:r0 + 128, :], x_[:, m, :], reads=[x_], writes=[x1_b])
                        else:
                            n = i * 4 + m
                            sc = ss4[:, (n % 2) * 4:(n % 2) * 4 + 1]
                            s2 = ss4[:, (n % 2) * 4 + 1:(n % 2) * 4 + 2]
                            s3 = ss4[:, (n % 2) * 4 + 2:(n % 2) * 4 + 3]
                            o_ = xo[0]
                            act(o_[:, :], x_[:, m, :], AF.Square, [x_], [o_, ss4], accum=sc)
                            ts("dve", s2, sc, 1.0 / D, 1e-5, ALU.mult, ALU.add, [ss4], [ss4])
                            act(s2, s2, AF.Sqrt, [ss4], [ss4])
                            recip(s3, s2, [ss4], [ss4])
                            stt(o_[:, :], x_[:, m, :], s3, fgt[:, :], ALU.mult, ALU.mult, [x_, ss4, fgt], [o_])
                            k.dma(y_out[r0:r0 + 128, :], o_[:, :], reads=[o_], writes=[yout_b])
                            outs.append(o_)
                        if m < 3:
                            yield
                    if i + 3 < NT:
                        p4_load_x(i + 3)
                return gen

            for i0 in range(min(2, NT)):
                p4_load_yb(i0)
                p4_load_ga(i0)
            for i0 in range(min(3, NT)):
                p4_load_x(i0)
            run_pipelined([p4_unit(i) for i in range(NT)], 2)
            if stop in ("p4", "x3", "x4"):
                break

        k.barrier()
        k.finalize(final_waits=list({id(o): o for o in outs}.values()))
    return nc


def _col_layout():
    rot = lambda base: [base + (j + 32) % 64 for j in range(64)]
    pl = lambda base: [base + j for j in range(64)]
    cols = list(range(1024))
    R = 1024
    cols += [R + j for j in range(256)]
    cols += [R + 1024 + j for j in range(256)]
    permc = []
    for a in range(4):
        permc += pl(a * 64) + pl((4 + a) * 64)
    cols += [R + 2048 + j for j in permc]
    for base in (R + 256, R + 512):
        for gi in range(2):
            cols += pl(base + gi * 128) + pl(base + gi * 128 + 64)
    for a in range(4):
        base = R + 1280
        cols += pl(base + a * 64) + pl(base + (4 + a) * 64)
    base = R + 1792
    cols += pl(base) + pl(base + 64)
    cols += [R + 768 + j for j in range(256)]
    cols += [R + 1920 + j for j in range(128)]
    assert len(cols) == NCOL
    return np.array(cols), np.array(permc)


def _constants():
    c = np.zeros((128, CW), np.float32)
    p = np.arange(128)[:, None]
    j = np.arange(128)[None, :]
    c[:, 0:128] = (p == j)
    c[:, 128:192] = (np.arange(64)[None, :] == (p % 64))
    c[:, 192:320] = (p // 64 == j // 64)
    c[:, 320:448] = (p // 64 == j // 64) / 64.0
    c[:, 448:576] = (p >= j)
    c[:, 576:704] = (p <= j)
    r = np.arange(64)[:, None]
    q = np.arange(64)[None, :]
    c[0:64, 704:768] = (q < r)
    c[0:64, 768:832] = (r < q)
    c[0:64, 832:896] = (r <= q)
    c[0:64, 896:960] = (q > r)
    c[0:64, 960:1024] = (r > q)
    c[0:64, 1024:1088] = (r >= q)
    col = np.arange(512)
    c[:, 1152:1664] = (col % 64 != 0)[None, :]
    c[:, 1664:2176] = (col % 64 != 63)[None, :]
    pm = np.where((np.arange(128) % 64) < 32, np.arange(128) + 32, np.arange(128) - 32)
    c[:, 2176:2304] = (np.arange(128)[:, None] == pm[None, :])
    return c


def _prep_shared(inp):
    cols, permc = _col_layout()
    f = lambda a: np.ascontiguousarray(np.asarray(a, dtype=np.float32))
    win = f(np.asarray(inp["w_in"])[:, :, cols])
    rows = np.concatenate([np.arange(512), 512 + permc])
    wout = f(np.asarray(inp["w_out"])[:, rows, :])
    prm = np.zeros((128, 64), np.float32)
    lnb = np.zeros((128, 4), np.float32)
    for l in range(NLAYER):
        prm[:, l * 8:(l + 1) * 8] = np.asarray(inp["norm_g"])[l].reshape(8, 128).T
        prm[:, 16 + l * 8:16 + (l + 1) * 8] = np.asarray(inp["tshift_mu"])[l].reshape(8, 128).T
        for e in range(2):
            prm[:, 32 + l * 4 + e * 2:32 + l * 4 + e * 2 + 2] = np.asarray(inp["rwkv_w0"])[l, e].reshape(2, 128).T
            prm[:, 40 + l * 4 + e * 2:40 + l * 4 + e * 2 + 2] = np.asarray(inp["rwkv_a0"])[l, e].reshape(2, 128).T
        prm[:, 48 + l * 2:48 + l * 2 + 2] = np.asarray(inp["rwkv_k_k"])[l].reshape(2, 128).T
        prm[:, 52 + l * 2:52 + l * 2 + 2] = np.asarray(inp["rwkv_k_a"])[l].reshape(2, 128).T
        prm[:, 56 + l * 2:56 + l * 2 + 2] = np.asarray(inp["rwkv_r_k"])[l].reshape(2, 128).T
        prm[:, 60 + l * 2:60 + l * 2 + 2] = np.asarray(inp["ln_x_w"])[l].reshape(2, 128).T
        lnb[:, l * 2:l * 2 + 2] = np.asarray(inp["ln_x_b"])[l].reshape(2, 128).T
    w2s = f(np.asarray(inp["rwkv_w2"]).transpose(2, 0, 1, 3))
    a2s = f(np.asarray(inp["rwkv_a2"]).transpose(2, 0, 1, 3))
    sink = np.asarray(inp["attn_sink"], dtype=np.float32)
    sinkrow = f(np.repeat(sink.reshape(NLAYER, 2, 4, 1), 128, axis=3).reshape(1, -1))
    fg = f(np.tile(np.asarray(inp["final_g"], dtype=np.float32)[None, :], (128, 1)))
    return dict(win=win, wout=wout, prm=prm, lnb=lnb, w2s=w2s, a2s=a2s, sinkrow=sinkrow, fg=fg, cst=_constants())


def _rope_tables(pos):
    inv = (10000.0 ** (-np.arange(0, 64, 2, dtype=np.float32) / np.float32(64))).astype(np.float32)
    ang = (pos.astype(np.float32)[:, None] * inv[None, :]).astype(np.float32)
    cos = np.cos(ang).astype(np.float32).T
    sin = np.sin(ang).astype(np.float32).T
    p = np.arange(128)
    cs = cos[p % 32, :]
    sn = sin[p % 32, :] * np.where((p % 64) < 32, -1.0, 1.0)[:, None].astype(np.float32)
    return np.ascontiguousarray(cs, dtype=np.float32), np.ascontiguousarray(sn, dtype=np.float32)


_NC_CACHE = {}


def _run(core_x, core_link, inp, TC, debug=False, stop=None):
    sh = _prep_shared(inp)
    key = (TC, debug, stop)
    if key not in _NC_CACHE:
        _NC_CACHE[key] = build_nc(TC, debug, stop)
    nc = _NC_CACHE[key]
    HALF = TC // 2
    maps = []
    tabs = {}
    for x, lk in zip(core_x, core_link):
        if lk not in tabs:
            t = np.arange(TC)
            pos = t if lk else (t % HALF)
            tabs[lk] = _rope_tables(pos)
        cs, sn = tabs[lk]
        m = dict(sh)
        m["x"] = np.ascontiguousarray(x, dtype=np.float32)
        m["link"] = np.full((128, 1), float(lk), np.float32)
        m["cs"] = cs
        m["sn"] = sn
        maps.append(m)
    res = run_bass_kernel_spmd(nc, maps, core_ids=list(range(len(maps))))
    return res.results


def kernel(**inputs):
    xp = np.asarray(inputs["x_prompt"], dtype=np.float32)
    xs = np.asarray(inputs["x_sample"], dtype=np.float32)
    TC = 16384
    core_x = [xp[0:2].reshape(TC, D), xp[2:4].reshape(TC, D), xs[0], xs[1]]
    core_link = [0, 0, 1, 1]
    for _ in range(4):
        core_x.append(np.zeros((TC, D), np.float32))
        core_link.append(0)
    res = _run(core_x, core_link, inputs, TC)
    yp = np.stack([res[0]["y"], res[1]["y"]]).reshape(4, 8192, D).astype(np.float32)
    ys = np.stack([res[2]["y"], res[3]["y"]]).astype(np.float32)
    return (yp, ys)
```
